# Optimizing a Trainium2 kernel written in Bass

```python
import math
import jax, jax.numpy as jnp
from jax import lax
import numpy as np

D_MODEL = 1024
BATCH = 32
SEQ = 2048
DEPTH = 2
DEC_BATCH = 8
DEC_SEQ = 8192
PAST_LEN = 128

EPS = 1e-6
CONV_W = 5
CHUNK = 64
Q_BLOCK = 128

GDN_HEADS = 4
GDN_DK = 128
GDN_DV = 128
GDN_QK_W = GDN_HEADS * GDN_DK
GDN_V_W = GDN_HEADS * GDN_DV
GDN_QKV_W = 2 * GDN_QK_W + GDN_V_W
GDN_COLS = GDN_QKV_W + GDN_V_W + 4 * GDN_HEADS

SSD_HEADS = 8
SSD_HEAD_DIM = 64
SSD_INNER = SSD_HEADS * SSD_HEAD_DIM
SSD_GROUPS = 2
SSD_STATE = 128
SSD_XBC = SSD_INNER + 2 * SSD_GROUPS * SSD_STATE
SSD_COLS = SSD_INNER + SSD_XBC + 2 * SSD_HEADS

MLA_HEADS = 4
MLA_Q_LORA = 256
MLA_KV_LORA = 256
MLA_NOPE = 128
MLA_ROPE = 64
MLA_V = 128
MLA_COLS = MLA_Q_LORA + MLA_KV_LORA + MLA_ROPE
MLA_SCALE = (MLA_NOPE + MLA_ROPE) ** -0.5
ROPE_BASE = 10000.0

IN_COLS = GDN_COLS + SSD_COLS + MLA_COLS
N_BRANCH = 3
FFN_HIDDEN = 2816
DT_MIN = 0.001
DT_MAX = 0.1

kernel_name = "hybrid_gdn_ssd_mla_macaron_encoder"

F32 = jnp.float32


def rmsnorm(x, g):
    xf = x.astype(F32)
    y = xf * lax.rsqrt(jnp.mean(xf * xf, axis=-1, keepdims=True) + EPS)
    return (y * g.astype(F32)).astype(x.dtype)


def l2norm(x):
    xf = x.astype(F32)
    return xf * lax.rsqrt(jnp.sum(xf * xf, axis=-1, keepdims=True) + EPS)


def swiglu(x, w_in, w_out):
    g, u = jnp.split(x @ w_in, 2, axis=-1)
    return (jax.nn.silu(g) * u) @ w_out


def depthwise_conv(x, w):
    return lax.conv_general_dilated(
        x, w.astype(x.dtype)[:, None, :], window_strides=(1,),
        padding=((CONV_W // 2, CONV_W // 2),),
        dimension_numbers=("NWC", "WIO", "NWC"), feature_group_count=x.shape[-1])


def rope_tables(L, dtype):
    inv_freq = jnp.power(ROPE_BASE, -jnp.arange(0, MLA_ROPE, 2, dtype=F32) / MLA_ROPE)
    ang = jnp.arange(L, dtype=F32)[:, None] * inv_freq[None, :]
    ang = jnp.concatenate([ang, ang], axis=-1)
    return jnp.cos(ang).astype(dtype), jnp.sin(ang).astype(dtype)


def rope(x, cos, sin):
    x1, x2 = jnp.split(x, 2, axis=-1)
    return x * cos + jnp.concatenate([-x2, x1], axis=-1) * sin


def gated_delta_chunked(q, k, v, g, beta):
    Bsz, L, H, DK = q.shape
    DV = v.shape[-1]
    N = L // CHUNK

    def to_chunks(t):
        t = t.reshape(Bsz, N, CHUNK, H, *t.shape[3:])
        return jnp.moveaxis(t, 3, 1)

    q = to_chunks(q * DK ** -0.5)
    k = to_chunks(k)
    v = to_chunks(v)
    beta = to_chunks(beta)
    gc = jnp.cumsum(to_chunks(g), axis=-1)
    idx = jnp.arange(CHUNK)
    lower = idx[:, None] >= idx[None, :]
    strict = idx[:, None] > idx[None, :]
    decay = jnp.exp(jnp.where(lower, gc[..., :, None] - gc[..., None, :], -jnp.inf))
    kb = k * beta[..., None]
    a_mat = jnp.where(strict, jnp.einsum("bhnid,bhnjd->bhnij", kb, k) * decay, 0.0)
    eye = jnp.eye(CHUNK, dtype=F32)
    rhs = jnp.concatenate([v * beta[..., None], kb * jnp.exp(gc)[..., None]], axis=-1)
    sol = lax.linalg.triangular_solve(eye + a_mat, rhs, left_side=True, lower=True,
                                      unit_diagonal=True)
    u, w = sol[..., :DV], sol[..., DV:]
    attn = jnp.einsum("bhnid,bhnjd->bhnij", q, k) * decay
    q_dec = q * jnp.exp(gc)[..., None]
    g_last = gc[..., -1]
    k_dec = k * jnp.exp(g_last[..., None] - gc)[..., None]

    def step(S, xs):
        u_c, w_c, q_c, k_c, a_c, gl = xs
        v_new = u_c - w_c @ S
        o = q_c @ S + a_c @ v_new
        S = S * jnp.exp(gl)[..., None, None] + jnp.swapaxes(k_c, -1, -2) @ v_new
        return S, o

    xs = tuple(jnp.moveaxis(t, 2, 0) for t in (u, w, q_dec, k_dec, attn, g_last))
    S0 = jnp.zeros((Bsz, H, DK, DV), F32)
    _, o = lax.scan(step, S0, xs)
    return o.transpose(1, 0, 3, 2, 4).reshape(Bsz, L, H, DV)


def gdn_branch(cols, conv_w, A_log, dt_bias, norm_g):
    Bsz, L, _ = cols.shape
    qkv = jax.nn.silu(depthwise_conv(cols[..., :GDN_QKV_W], conv_w))
    z = cols[..., GDN_QKV_W:GDN_QKV_W + GDN_V_W].reshape(Bsz, L, GDN_HEADS, GDN_DV)
    ab = cols[..., GDN_QKV_W + GDN_V_W:].astype(F32).reshape(Bsz, L, 4, GDN_HEADS)
    q = l2norm(qkv[..., :GDN_QK_W].reshape(Bsz, L, GDN_HEADS, GDN_DK))
    k = l2norm(qkv[..., GDN_QK_W:2 * GDN_QK_W].reshape(Bsz, L, GDN_HEADS, GDN_DK))
    v = qkv[..., 2 * GDN_QK_W:].astype(F32).reshape(Bsz, L, GDN_HEADS, GDN_DV)
    g = -jnp.exp(A_log.astype(F32)) * jax.nn.softplus(ab[:, :, 0:2] + dt_bias.astype(F32))
    beta = jax.nn.sigmoid(ab[:, :, 2:4])
    flip = lambda t: jnp.flip(t, axis=1)
    o_f = gated_delta_chunked(q, k, v, g[:, :, 0], beta[:, :, 0])
    o_b = flip(gated_delta_chunked(flip(q), flip(k), flip(v), flip(g[:, :, 1]), flip(beta[:, :, 1])))
    o = rmsnorm(o_f + o_b, norm_g) * jax.nn.silu(z.astype(F32))
    return o.reshape(Bsz, L, GDN_V_W).astype(cols.dtype)


def ssd_chunked(x, dt, A, Bm, Cm):
    Bsz, L, H, P = x.shape
    G, S = Bm.shape[2], Bm.shape[3]
    R = H // G
    N = L // CHUNK
    xr = (x * dt[..., None]).reshape(Bsz, N, CHUNK, G, R, P)
    ac = jnp.cumsum((dt * A).reshape(Bsz, N, CHUNK, G, R), axis=2)
    Bc = Bm.reshape(Bsz, N, CHUNK, G, S)
    Cc = Cm.reshape(Bsz, N, CHUNK, G, S)
    idx = jnp.arange(CHUNK)
    lower = (idx[:, None] >= idx[None, :])[:, :, None, None]
    seg = jnp.exp(jnp.where(lower, ac[:, :, :, None] - ac[:, :, None, :], -jnp.inf))
    cb = jnp.einsum("bnigs,bnjgs->bnijg", Cc, Bc)
    y_diag = jnp.einsum("bnijg,bnijgr,bnjgrp->bnigrp", cb, seg, xr)
    decay_states = jnp.exp(ac[:, :, -1:] - ac)
    states = jnp.einsum("bncgs,bncgr,bncgrp->bngrps", Bc, decay_states, xr)
    chunk_decay = jnp.exp(ac[:, :, -1])

    def step(h, inp):
        st, dec = inp
        return h * dec[..., None, None] + st, h

    h0 = jnp.zeros((Bsz, G, R, P, S), F32)
    _, h_in = lax.scan(step, h0, (jnp.moveaxis(states, 1, 0), jnp.moveaxis(chunk_decay, 1, 0)))
    h_in = jnp.moveaxis(h_in, 0, 1)
    y_off = jnp.einsum("bncgs,bngrps,bncgr->bncgrp", Cc, h_in, jnp.exp(ac))
    return (y_diag + y_off).reshape(Bsz, L, H, P)


def ssd_branch(cols, conv_w, conv_b, A_log, dt_bias, D_skip, norm_g):
    Bsz, L, _ = cols.shape
    z = cols[..., :SSD_INNER]
    xbc = jax.nn.silu(depthwise_conv(cols[..., SSD_INNER:SSD_INNER + SSD_XBC], conv_w) + conv_b.astype(cols.dtype))
    xbc = xbc.astype(F32)
    xh = xbc[..., :SSD_INNER].reshape(Bsz, L, SSD_HEADS, SSD_HEAD_DIM)
    Bm = xbc[..., SSD_INNER:SSD_INNER + SSD_GROUPS * SSD_STATE].reshape(Bsz, L, SSD_GROUPS, SSD_STATE)
    Cm = xbc[..., SSD_INNER + SSD_GROUPS * SSD_STATE:].reshape(Bsz, L, SSD_GROUPS, SSD_STATE)
    dt = cols[..., SSD_INNER + SSD_XBC:].astype(F32).reshape(Bsz, L, 2, SSD_HEADS)
    dt = jax.nn.softplus(dt + dt_bias.astype(F32))
    A = -jnp.exp(A_log.astype(F32))
    flip = lambda t: jnp.flip(t, axis=1)
    y_f = ssd_chunked(xh, dt[:, :, 0], A[0], Bm, Cm)
    y_b = flip(ssd_chunked(flip(xh), flip(dt[:, :, 1]), A[1], flip(Bm), flip(Cm)))
    y = y_f + y_b + xh * D_skip.astype(F32)[:, None]
    y = rmsnorm(y.reshape(Bsz, L, SSD_INNER) * jax.nn.silu(z.astype(F32)), norm_g)
    return y.astype(cols.dtype)


def mla_branch(cols, cos, sin, q_norm, w_uq, kv_norm, w_ukv):
    Bsz, L, _ = cols.shape
    cq = cols[..., :MLA_Q_LORA]
    ckv = cols[..., MLA_Q_LORA:MLA_Q_LORA + MLA_KV_LORA]
    k_rope = rope(cols[..., MLA_Q_LORA + MLA_KV_LORA:], cos, sin)
    q = (rmsnorm(cq, q_norm) @ w_uq).reshape(Bsz, L, MLA_HEADS, MLA_NOPE + MLA_ROPE)
    kv = (rmsnorm(ckv, kv_norm) @ w_ukv).reshape(Bsz, L, MLA_HEADS, MLA_NOPE + MLA_V)
    q_nope = q[..., :MLA_NOPE]
    q_rope = rope(q[..., MLA_NOPE:], cos[:, None], sin[:, None])
    k_nope = kv[..., :MLA_NOPE]
    v = kv[..., MLA_NOPE:]
    nb = L // Q_BLOCK

    def blocks(t):
        return jnp.moveaxis(t.reshape(Bsz, nb, Q_BLOCK, *t.shape[2:]), 1, 0)

    def attend(qb):
        qn, qr = qb
        s = jnp.einsum("bqhd,bkhd->bhqk", qn, k_nope) + jnp.einsum("bqhr,bkr->bhqk", qr, k_rope)
        p = jax.nn.softmax(s.astype(F32) * MLA_SCALE, axis=-1).astype(v.dtype)
        return jnp.einsum("bhqk,bkhd->bqhd", p, v)

    o = lax.map(attend, (blocks(q_nope), blocks(q_rope)))
    return jnp.moveaxis(o, 0, 1).reshape(Bsz, L, MLA_HEADS * MLA_V)


def encoder_layer(x, cos, sin, ffn1_norm, w_ffn1_in, w_ffn1_out, mix_norm, w_in,
                  gdn_conv, gdn_A_log, gdn_dt_bias, gdn_norm,
                  ssd_conv, ssd_conv_b, ssd_A_log, ssd_dt_bias, ssd_D, ssd_norm,
                  mla_q_norm, mla_w_uq, mla_kv_norm, mla_w_ukv,
                  w_branch_a, w_branch_b, w_branch_c, w_gate, b_gate, w_out,
                  ffn2_norm, w_ffn2_in, w_ffn2_out):
    h = x + 0.5 * swiglu(rmsnorm(x, ffn1_norm), w_ffn1_in, w_ffn1_out)
    u = rmsnorm(h, mix_norm)
    cols = u @ w_in
    c_a = cols[..., :GDN_COLS]
    c_b = cols[..., GDN_COLS:GDN_COLS + SSD_COLS]
    c_c = cols[..., GDN_COLS + SSD_COLS:]
    y_a = gdn_branch(c_a, gdn_conv, gdn_A_log, gdn_dt_bias, gdn_norm) @ w_branch_a
    y_b = ssd_branch(c_b, ssd_conv, ssd_conv_b, ssd_A_log, ssd_dt_bias, ssd_D, ssd_norm) @ w_branch_b
    y_c = mla_branch(c_c, cos, sin, mla_q_norm, mla_w_uq, mla_kv_norm, mla_w_ukv) @ w_branch_c
    gates = jax.nn.sigmoid((u @ w_gate + b_gate).astype(F32)).astype(u.dtype)
    g_a, g_b, g_c = jnp.split(gates, N_BRANCH, axis=-1)
    h = h + (g_a * y_a + g_b * y_b + g_c * y_c) @ w_out
    return h + 0.5 * swiglu(rmsnorm(h, ffn2_norm), w_ffn2_in, w_ffn2_out)


def _a_log(k, shape):
    return jnp.log(jax.random.uniform(k, shape, F32, minval=1.0, maxval=16.0))


def _dt_bias(k, shape):
    r = jax.random.uniform(k, shape, F32)
    dt = jnp.exp(r * (math.log(DT_MAX) - math.log(DT_MIN)) + math.log(DT_MIN))
    return dt + jnp.log(-jnp.expm1(-dt))


def setup_inputs(seed: int = 0) -> dict:
    key = jax.random.key(seed)
    ks = iter(jax.random.split(key, 40))
    nrm = lambda shape, scale: jax.random.normal(next(ks), shape, F32) * scale
    gain = lambda shape: 1.0 + 0.02 * jax.random.normal(next(ks), shape, F32)
    D, F = D_MODEL, FFN_HIDDEN
    return {
        "x_prompt": nrm((BATCH, SEQ, D), 1.0),
        "x_sample": nrm((DEC_BATCH, DEC_SEQ, D), 1.0),
        "ffn1_norm": gain((DEPTH, D)),
        "w_ffn1_in": nrm((DEPTH, D, 2 * F), D ** -0.5),
        "w_ffn1_out": nrm((DEPTH, F, D), F ** -0.5),
        "mix_norm": gain((DEPTH, D)),
        "w_in": nrm((DEPTH, D, IN_COLS), D ** -0.5),
        "gdn_conv": nrm((DEPTH, CONV_W, GDN_QKV_W), CONV_W ** -0.5),
        "gdn_A_log": _a_log(next(ks), (DEPTH, 2, GDN_HEADS)),
        "gdn_dt_bias": _dt_bias(next(ks), (DEPTH, 2, GDN_HEADS)),
        "gdn_norm": gain((DEPTH, GDN_DV)),
        "ssd_conv": nrm((DEPTH, CONV_W, SSD_XBC), CONV_W ** -0.5),
        "ssd_conv_b": nrm((DEPTH, SSD_XBC), 0.02),
        "ssd_A_log": _a_log(next(ks), (DEPTH, 2, SSD_HEADS)),
        "ssd_dt_bias": _dt_bias(next(ks), (DEPTH, 2, SSD_HEADS)),
        "ssd_D": gain((DEPTH, SSD_HEADS)),
        "ssd_norm": gain((DEPTH, SSD_INNER)),
        "mla_q_norm": gain((DEPTH, MLA_Q_LORA)),
        "mla_w_uq": nrm((DEPTH, MLA_Q_LORA, MLA_HEADS * (MLA_NOPE + MLA_ROPE)), MLA_Q_LORA ** -0.5),
        "mla_kv_norm": gain((DEPTH, MLA_KV_LORA)),
        "mla_w_ukv": nrm((DEPTH, MLA_KV_LORA, MLA_HEADS * (MLA_NOPE + MLA_V)), MLA_KV_LORA ** -0.5),
        "w_branch_a": nrm((DEPTH, GDN_V_W, D), GDN_V_W ** -0.5),
        "w_branch_b": nrm((DEPTH, SSD_INNER, D), SSD_INNER ** -0.5),
        "w_branch_c": nrm((DEPTH, MLA_HEADS * MLA_V, D), (MLA_HEADS * MLA_V) ** -0.5),
        "w_gate": nrm((DEPTH, D, N_BRANCH * D), D ** -0.5),
        "b_gate": nrm((DEPTH, N_BRANCH * D), 0.02),
        "w_out": nrm((DEPTH, D, D), D ** -0.5),
        "ffn2_norm": gain((DEPTH, D)),
        "w_ffn2_in": nrm((DEPTH, D, 2 * F), D ** -0.5),
        "w_ffn2_out": nrm((DEPTH, F, D), F ** -0.5),
        "final_norm": gain((D,)),
    }


def reference(x_prompt, x_sample, ffn1_norm, w_ffn1_in, w_ffn1_out, mix_norm, w_in,
              gdn_conv, gdn_A_log, gdn_dt_bias, gdn_norm,
              ssd_conv, ssd_conv_b, ssd_A_log, ssd_dt_bias, ssd_D, ssd_norm,
              mla_q_norm, mla_w_uq, mla_kv_norm, mla_w_ukv,
              w_branch_a, w_branch_b, w_branch_c, w_gate, b_gate, w_out,
              ffn2_norm, w_ffn2_in, w_ffn2_out, final_norm):
    def trunk(x):
        cos, sin = rope_tables(x.shape[1], x.dtype)
        for i in range(DEPTH):
            x = encoder_layer(
                x, cos, sin, ffn1_norm[i], w_ffn1_in[i], w_ffn1_out[i], mix_norm[i], w_in[i],
                gdn_conv[i], gdn_A_log[i], gdn_dt_bias[i], gdn_norm[i],
                ssd_conv[i], ssd_conv_b[i], ssd_A_log[i], ssd_dt_bias[i], ssd_D[i], ssd_norm[i],
                mla_q_norm[i], mla_w_uq[i], mla_kv_norm[i], mla_w_ukv[i],
                w_branch_a[i], w_branch_b[i], w_branch_c[i], w_gate[i], b_gate[i], w_out[i],
                ffn2_norm[i], w_ffn2_in[i], w_ffn2_out[i])
        return rmsnorm(x, final_norm)

    y_prompt = trunk(x_prompt)
    y_sample = trunk(x_sample)
    return (y_prompt, y_sample)
```

```python
import math
import numpy as np
from contextlib import ExitStack, contextmanager
import concourse.bass as bass
import concourse.mybir as mybir
from concourse.bass_utils import run_bass_kernel_spmd

F32 = mybir.dt.float32
BF16 = mybir.dt.bfloat16
AF = mybir.ActivationFunctionType
ALU = mybir.AluOpType
AX = mybir.AxisListType

D = 1024
FH = 2816
NJ = FH // 128
DEPTH = 2
INC = 4192
EPS = 1e-6
MLA_SCALE = 192 ** -0.5
STRICT = True
INV_DT = F32
N_CORES = 8
TT = 512
SCRATCH_EXTERNAL = False
STORE_Q = "sp"


class Res:
    def __init__(self, acc=False):
        self.w = None
        self.r = {}
        self.acc = acc


class Buf(Res):
    def __init__(self, t):
        super().__init__()
        self.t = t

    def __getitem__(self, key):
        return self.t[key]


class K:
    def __init__(self, nc):
        self.nc = nc
        self.es = ExitStack()
        self.eng = {"pe": nc.tensor, "act": nc.scalar, "dve": nc.vector, "pool": nc.gpsimd, "sp": nc.sync}
        self.semh = {}
        self.cnt = {}
        for e in self.eng:
            self.semh[e] = self.es.enter_context(nc.semaphore("se_" + e))
            self.cnt[e] = 0
        self.seen = {e: {} for e in self.eng}
        self.dtot = {}
        self.nops = 0
        self.uid = 0

    @contextmanager
    def phase(self):
        st = ExitStack()
        self._ph = st
        try:
            yield st
        finally:
            self.barrier()
            st.close()

    def sb(self, shape, dt, name=None, glob=False):
        self.uid += 1
        t = (self.es if glob else self._ph).enter_context(
            self.nc.sbuf_tensor(f"{name or 'sb'}_{self.uid}", list(shape), dt))
        return Buf(t)

    def ring(self, n, shape, dt, name=None):
        return [self.sb(shape, dt, name) for _ in range(n)]

    def _dsem(self, key):
        if key not in self.semh:
            self.semh[key] = self.es.enter_context(self.nc.semaphore("sd_" + key))
            self.dtot[key] = 0
        return self.semh[key]

    def _deps(self, E, reads, writes, skip=None):
        evs = {}

        def add(d):
            for k, v in d.items():
                if evs.get(k, 0) < v:
                    evs[k] = v
        for r in reads:
            if r.w:
                add(r.w)
        for r in writes:
            if r.w:
                add(r.w)
            add(r.r)
        if E == "pe" or not STRICT:
            evs.pop(E, None)
        if skip:
            evs.pop(skip, None)
        return evs

    def _wait(self, E, evs):
        for key, val in evs.items():
            if key in self.dtot:
                val = self.dtot[key]
            if self.seen[E].get(key, 0) < val:
                self.eng[E].wait_ge(self.semh[key], val)
                self.seen[E][key] = val

    def op(self, E, fn, reads=(), writes=()):
        self._wait(E, self._deps(E, reads, writes))
        ins = fn(self.eng[E])
        self.cnt[E] += 1
        c = self.cnt[E]
        ins.then_inc(self.semh[E], 1)
        self.nops += 1
        for r in reads:
            if r.r.get(E, 0) < c:
                r.r[E] = c
        for r in writes:
            r.w = {E: c}
            r.r = {}

    def dma(self, Q, out, in_, reads, writes, key, **kw):
        if Q == "pool" and out.dtype == in_.dtype:
            Q = STORE_Q
        self._dsem(key)
        self._wait(Q, self._deps(Q, reads, writes, skip=key))
        ins = self.eng[Q].dma_start(out=out, in_=in_, **kw)
        self.dtot[key] += 16
        v = self.dtot[key]
        ins.then_inc(self.semh[key], 16)
        self.nops += 1
        for r in reads:
            r.r[key] = v
        for r in writes:
            if getattr(r, "acc", False):
                r.w = dict(r.w or {})
                r.w[key] = v
            else:
                r.w = {key: v}
                r.r = {}

    def barrier(self):
        tot = {e: self.cnt[e] for e in self.eng}
        tot.update(self.dtot)
        for E in self.eng:
            ev = dict(tot)
            ev.pop(E, None)
            self._wait(E, ev)


def kernel_build(seqs, want_debug=False):
    NT = sum(seqs)
    LMAX = max(seqs)
    nc = bass.Bass("TRN2", target_bir_lowering=False)
    k = K(nc)
    P = 128

    def din(name, shape, dt=F32):
        return nc.dram_tensor(name, list(shape), dt, kind="ExternalInput").ap()

    def dscr(name, shape, dt):
        ext = want_debug or SCRATCH_EXTERNAL
        return nc.dram_tensor(name, list(shape), dt, kind="ExternalOutput" if ext else "Internal").ap()

    xin = din("xin", [NT, D])
    yout = nc.dram_tensor("yout", [NT, D], F32, kind="ExternalOutput").ap()
    W = {}
    wshapes = {
        "ffn1_norm": [DEPTH, D], "w_ffn1_in": [DEPTH, D, 2 * FH], "w_ffn1_out": [DEPTH, FH, D],
        "mix_norm": [DEPTH, D], "w_in": [DEPTH, D, INC], "gdn_conv": [DEPTH, 5, 1536],
        "gdn_A_log": [DEPTH, 8], "gdn_dt_bias": [DEPTH, 8], "gdn_norm": [DEPTH, 128],
        "ssd_conv": [DEPTH, 5, 1024], "ssd_conv_b": [DEPTH, 1024], "ssd_A_log": [DEPTH, 16],
        "ssd_dt_bias": [DEPTH, 16], "ssd_D": [DEPTH, 8], "ssd_norm": [DEPTH, 512],
        "mla_q_norm": [DEPTH, 256], "mla_w_uq": [DEPTH, 256, 768], "mla_kv_norm": [DEPTH, 256],
        "mla_w_ukv": [DEPTH, 256, 1024], "w_branch_a": [DEPTH, 512, D], "w_branch_b": [DEPTH, 512, D],
        "w_branch_c": [DEPTH, 512, D], "w_gate": [DEPTH, D, 3 * D], "b_gate": [DEPTH, 3 * D],
        "w_out": [DEPTH, D, D], "ffn2_norm": [DEPTH, D], "w_ffn2_in": [DEPTH, D, 2 * FH],
        "w_ffn2_out": [DEPTH, FH, D], "final_norm": [1, D],
    }
    for n, s in wshapes.items():
        W[n] = din(n, s)
    cmask = din("cmask", [128, 7, 128])
    ropecs = din("ropecs", [2, 64, LMAX])

    hA = dscr("hA", [NT, D], F32)
    hB = dscr("hB", [NT, D], F32)
    featT = dscr("featT", [3712, NT], BF16)
    zs_tm = dscr("zs_tm", [NT, 512], F32)
    sm_tm = dscr("sm_tm", [NT, 32], F32)
    of_tm = dscr("of_tm", [NT, 512], F32)
    yf_tm = dscr("yf_tm", [NT, 512], F32)
    brT = [dscr(f"brT{i}", [512, NT], BF16) for i in range(3)]
    qT_d = dscr("qT_d", [4, 128, NT], BF16)
    qrT_d = dscr("qrT_d", [4, 64, NT], BF16)
    kT_d = dscr("kT_d", [4, 128, NT], BF16)
    krT_d = dscr("krT_d", [64, NT], BF16)
    v_d = dscr("v_d", [NT, 512], BF16)
    wffn_bf = dscr("wffn_bf", [DEPTH * 2, NJ, 128, 8 * 256], BF16)

    cm = k.sb([128, 7, 128], F32, "cm", glob=True)
    identb = k.sb([128, 128], BF16, "identb", glob=True)
    onesf = k.sb([128, 128], F32, "onesf", glob=True)
    nonesf = k.sb([128, 128], F32, "nonesf", glob=True)
    onesb = k.sb([128, 128], BF16, "onesb", glob=True)
    cst = k.sb([128, 4], F32, "cst", glob=True)
    banks = []
    for i in range(8):
        banks.append(Buf(k.es.enter_context(nc.psum_tensor(f"psb{i}", [128, 512], F32))))
    pstate = {"i": 0, "set": list(range(8))}

    def pget():
        s = pstate["set"]
        b = banks[s[pstate["i"] % len(s)]]
        pstate["i"] += 1
        return b

    def pset(lst):
        pstate["set"] = list(lst)
        pstate["i"] = 0

    k.dma("sp", cm[:], cmask, [], [cm], "cm")
    k.op("dve", lambda e: e.memset(onesf[:], 1.0), [], [onesf])
    k.op("dve", lambda e: e.memset(nonesf[:], -1.0), [], [nonesf])
    k.op("dve", lambda e: e.memset(onesb[:], 1.0), [], [onesb])
    k.op("dve", lambda e: e.memset(cst[:, 0:1], EPS), [], [cst])
    k.op("dve", lambda e: e.memset(cst[:, 1:2], 1.0), [], [cst])
    k.op("dve", lambda e: e.memset(cst[:, 2:3], 0.0), [], [cst])
    k.op("dve", lambda e: e.tensor_copy(out=identb[:], in_=cm[:, 0, :]), [cm], [identb])
    ident_f = cm[:, 0, :]

    def bc3(ap2, n):
        return ap2.unsqueeze(2).to_broadcast([ap2.shape[0], ap2.shape[1], n])

    def bcm(ap2, n):
        return ap2.unsqueeze(1).to_broadcast([ap2.shape[0], n, ap2.shape[1]])

    def rsqrt(out_buf, out_ap, in_buf, in_ap, scale=1.0):
        k.op("dve", lambda e: e.tensor_scalar(out=out_ap, in0=in_ap, scalar1=scale, scalar2=EPS, op0=ALU.mult, op1=ALU.add),
             [in_buf], [out_buf])
        k.op("act", lambda e: e.activation(out=out_ap, in_=out_ap, func=AF.Ln), [out_buf], [out_buf])
        k.op("act", lambda e: e.activation(out=out_ap, in_=out_ap, func=AF.Exp, scale=-0.5), [out_buf], [out_buf])

    wst = [k.sb([128, 1024], F32, "wst", glob=True) for _ in range(3)]
    wctr = [0]

    def load_w(dst_buf, dst_ap, src_ap, scale=None, view=None):
        i = wctr[0]
        wctr[0] += 1
        stg_ = wst[i % 3]
        if view is None:
            np_, n = src_ap.shape
            sv = stg_[0:np_, 0:n]
        else:
            sv = view(stg_)
        k.dma("sp", sv, src_ap, [], [stg_], f"wst{i % 3}")
        if scale is not None:
            k.op("dve", lambda e: e.tensor_scalar(out=dst_ap, in0=sv, scalar1=scale, scalar2=None, op0=ALU.mult), [stg_], [dst_buf])
        elif i % 3 == 0:
            k.op("act", lambda e: e.activation(out=dst_ap, in_=sv, func=AF.Copy), [stg_], [dst_buf])
        elif i % 3 == 1:
            k.op("dve", lambda e: e.tensor_copy(out=dst_ap, in_=sv), [stg_], [dst_buf])
        else:
            k.op("pool", lambda e: e.tensor_copy(out=dst_ap, in_=sv), [stg_], [dst_buf])

    def load_w_wide(dst_buf, dst2d, src2d):
        n = src2d.shape[1]
        for c0 in range(0, n, 1024):
            c1 = min(c0 + 1024, n)
            load_w(dst_buf, dst2d[:, c0:c1], src2d[:, c0:c1])

    with k.phase():
        stg = k.ring(3, [128, 8, 256], BF16, "wstg")
        n = 0
        for l in range(DEPTH):
            for f, wn in enumerate(("w_ffn1_in", "w_ffn2_in")):
                for j in range(NJ):
                    b = stg[n % 3]
                    n += 1
                    for half in range(2):
                        src = W[wn][l, :, half * FH + j * 128: half * FH + (j + 1) * 128].rearrange("(kk p) n -> p kk n", p=128)
                        load_w(b, b[:, :, half * 128:(half + 1) * 128], src, view=lambda t: t[:, 0:1024].rearrange("p (kk n) -> p kk n", kk=8))
                    k.dma("sp", wffn_bf[l * 2 + f, j].rearrange("p (kk n) -> p kk n", kk=8), b[:], [b], [], f"wstg_o{(n - 1) % 3}")

    def load_h(src, t0, hb, key):
        k.dma("sp", hb[:], src[t0:t0 + TT, :].rearrange("(s p) d -> p s d", p=128), [], [hb], key)

    def norm_T(hb, gain, uT, xn_ring, ss, rs, ctr):
        for s in range(4):
            xn = xn_ring[ctr[0] % len(xn_ring)]
            ctr[0] += 1
            k.op("dve", lambda e: e.scalar_tensor_tensor(out=xn[:], in0=hb[:, s, :], scalar=1.0, in1=hb[:, s, :],
                                                         op0=ALU.mult, op1=ALU.mult, accum_out=ss[:, s:s + 1]),
                 [hb], [xn, ss])
        rsqrt(rs, rs[:, 0:4], ss, ss[:, 0:4], scale=1.0 / D)
        xns = []
        for s in range(4):
            xn = xn_ring[ctr[0] % len(xn_ring)]
            ctr[0] += 1
            k.op("dve", lambda e: e.tensor_scalar(out=xn[:], in0=hb[:, s, :], scalar1=rs[:, s:s + 1], scalar2=None, op0=ALU.mult),
                 [hb, rs], [xn])
            xns.append(xn)
        for kk in range(8):
            pb = pget()
            pv = pb[:].bitcast(BF16)
            for s in range(4):
                k.op("pe", lambda e: e.transpose(out=pv[:, s * 128:(s + 1) * 128], in_=xns[s][:, kk * 128:(kk + 1) * 128], identity=identb[:]),
                     [xns[s], identb], [pb])
            k.op("dve" if kk % 2 else "act",
                 (lambda e: e.tensor_scalar(out=uT[:, kk, :], in0=pv[:, 0:512], scalar1=gain[:, kk:kk + 1], scalar2=None, op0=ALU.mult)) if kk % 2 else
                 (lambda e: e.activation(out=uT[:, kk, :], in_=pv[:, 0:512], func=AF.Copy, scale=gain[:, kk:kk + 1])),
                 [pb, gain], [uT])

    def load_gain(dst, src_row, nk):
        k.dma("sp", dst[:, 0:nk], src_row.rearrange("(kk p) -> p kk", p=128), [], [dst], "gain", allow_slow_non_contiguous=True)

    def ffn_phase(l, f, src, dst, final):
        wn_out = "w_ffn1_out" if f == 0 else "w_ffn2_out"
        gn = "ffn1_norm" if f == 0 else "ffn2_norm"
        with k.phase():
            pset(range(8))
            wout = k.sb([128, NJ, D], BF16, "wout")
            for j in range(NJ):
                load_w(wout, wout[:, j, :], W[wn_out][l, j * 128:(j + 1) * 128, :])
            gain = k.sb([128, 8], F32, "gain")
            load_gain(gain, W[gn][l], 8)
            if final:
                fg = k.sb([128, D], F32, "fg")
                k.dma("sp", fg[:], W["final_norm"][0].partition_broadcast(128), [], [fg], "fg")
            hbs = k.ring(2, [128, 4, D], F32, "hb")
            xnr = k.ring(4, [128, D], BF16, "xn")
            uT = k.sb([128, 8, TT], BF16, "uT")
            hid = k.sb([128, NJ, TT], BF16, "hid")
            sgs = k.ring(2, [128, TT], F32, "sg")
            slabs = k.ring(3, [128, 8, 256], BF16, "slab")
            ss = k.sb([128, 8], F32, "ss")
            rs = k.sb([128, 8], F32, "rs")
            ctr = [0]
            nsl = 0
            for ti in range(NT // TT):
                t0 = ti * TT
                hb = hbs[ti % 2]
                load_h(src, t0, hb, f"hld{ti % 2}")
                norm_T(hb, gain, uT, xnr, ss, rs, ctr)
                for j in range(NJ):
                    sl = slabs[nsl % 3]
                    k.dma("sp", sl[:], wffn_bf[l * 2 + f, j].rearrange("p (kk n) -> p kk n", kk=8), [], [sl], f"slab{nsl % 3}")
                    nsl += 1
                    pg = pget()
                    pu = pget()
                    for kk in range(8):
                        k.op("pe", lambda e: e.matmul(pg[:], lhsT=sl[:, kk, 0:128], rhs=uT[:, kk, :], start=(kk == 0), stop=(kk == 7)),
                             [sl, uT], [pg])
                    for kk in range(8):
                        k.op("pe", lambda e: e.matmul(pu[:], lhsT=sl[:, kk, 128:256], rhs=uT[:, kk, :], start=(kk == 0), stop=(kk == 7)),
                             [sl, uT], [pu])
                    sg = sgs[j % 2]
                    k.op("act", lambda e: e.activation(out=sg[:], in_=pg[:], func=AF.Silu), [pg], [sg])
                    k.op("dve", lambda e: e.tensor_tensor(out=hid[:, j, :], in0=sg[:], in1=pu[:], op=ALU.mult), [sg, pu], [hid])
                for s in range(4):
                    for fh in range(2):
                        po = pget()
                        for j in range(NJ):
                            k.op("pe", lambda e: e.matmul(po[:], lhsT=hid[:, j, s * 128:(s + 1) * 128], rhs=wout[:, j, fh * 512:(fh + 1) * 512],
                                                          start=(j == 0), stop=(j == NJ - 1)), [hid, wout], [po])
                        k.op("dve", lambda e: e.scalar_tensor_tensor(out=hb[:, s, fh * 512:(fh + 1) * 512], in0=po[:], scalar=0.5,
                                                                     in1=hb[:, s, fh * 512:(fh + 1) * 512], op0=ALU.mult, op1=ALU.add),
                             [po, hb], [hb])
                if final:
                    for s in range(4):
                        xn = xnr[ctr[0] % 4]
                        ctr[0] += 1
                        k.op("dve", lambda e: e.scalar_tensor_tensor(out=sgs[0][:], in0=hb[:, s, 0:512], scalar=1.0, in1=hb[:, s, 0:512],
                                                                     op0=ALU.mult, op1=ALU.mult, accum_out=ss[:, s:s + 1]), [hb], [sgs[0], ss])
                        k.op("dve", lambda e: e.scalar_tensor_tensor(out=sgs[0][:], in0=hb[:, s, 512:1024], scalar=1.0, in1=hb[:, s, 512:1024],
                                                                     op0=ALU.mult, op1=ALU.mult, accum_out=ss[:, 4 + s:5 + s]), [hb], [sgs[0], ss])
                    k.op("dve", lambda e: e.tensor_tensor(out=ss[:, 0:4], in0=ss[:, 0:4], in1=ss[:, 4:8], op=ALU.add), [ss], [ss])
                    rsqrt(rs, rs[:, 0:4], ss, ss[:, 0:4], scale=1.0 / D)
                    for s in range(4):
                        k.op("dve", lambda e: e.scalar_tensor_tensor(out=hb[:, s, :], in0=hb[:, s, :], scalar=rs[:, s:s + 1], in1=fg[:],
                                                                     op0=ALU.mult, op1=ALU.mult), [hb, rs, fg], [hb])
                k.dma("pool", dst[t0:t0 + TT, :].rearrange("(s p) d -> p s d", p=128), hb[:], [hb], [], f"hst{ti % 2}")

    def m1_phase(l, src):
        with k.phase():
            pset(range(8))
            win = k.sb([128, 8, INC + 64], BF16, "win")
            for kk in range(8):
                load_w_wide(win, win[:, kk, 0:INC], W["w_in"][l, kk * 128:(kk + 1) * 128, :])
                load_w(win, win[:, kk, INC:INC + 32], W["w_in"][l, kk * 128:(kk + 1) * 128, 4160:4192], scale=-1.0)
                load_w(win, win[:, kk, INC + 32:INC + 64], W["w_in"][l, kk * 128:(kk + 1) * 128, 4128:4160])
            wsm = k.sb([128, 8, 32], BF16, "wsm")
            for kk in range(8):
                load_w(wsm, wsm[:, kk, 0:16], W["w_in"][l, kk * 128:(kk + 1) * 128, 2048:2064])
                load_w(wsm, wsm[:, kk, 16:32], W["w_in"][l, kk * 128:(kk + 1) * 128, 3600:3616])
            gain = k.sb([128, 8], F32, "gain")
            load_gain(gain, W["mix_norm"][l], 8)
            hbs = k.ring(2, [128, 4, D], F32, "hb")
            xnr = k.ring(4, [128, D], BF16, "xn")
            uT = k.sb([128, 8, TT], BF16, "uT")
            ss = k.sb([128, 8], F32, "ss")
            rs = k.sb([128, 8], F32, "rs")
            stg = k.ring(4, [128, TT], BF16, "stg")
            zst = k.ring(2, [128, 4, 512], F32, "zst")
            smt = k.ring(2, [128, 4, 32], F32, "smt")
            ctr = [0]
            groups = [(g, g * 128) for g in range(16)] + [(16 + g, 2576 + g * 128) for g in range(8)] + \
                     [(24 + g, 3616 + g * 128) for g in range(4)] + [(28, 4128)]
            ns = 0
            for ti in range(NT // TT):
                t0 = ti * TT
                hb = hbs[ti % 2]
                load_h(src, t0, hb, f"hld{ti % 2}")
                norm_T(hb, gain, uT, xnr, ss, rs, ctr)
                for (rg, co) in groups:
                    pb = pget()
                    for kk in range(8):
                        k.op("pe", lambda e: e.matmul(pb[:], lhsT=win[:, kk, co:co + 128], rhs=uT[:, kk, :], start=(kk == 0), stop=(kk == 7)),
                             [win, uT], [pb])
                    st = stg[ns % 4]
                    if ns % 2:
                        k.op("dve", lambda e: e.tensor_copy(out=st[:], in_=pb[:]), [pb], [st])
                    else:
                        k.op("act", lambda e: e.activation(out=st[:], in_=pb[:], func=AF.Copy), [pb], [st])
                    k.dma("pool", featT[rg * 128:(rg + 1) * 128, t0:t0 + TT], st[:], [st], [], f"stg{ns % 4}")
                    ns += 1
                z = zst[ti % 2]
                sm = smt[ti % 2]
                for s in range(4):
                    pb = pget()
                    for kk in range(8):
                        k.op("pe", lambda e: e.matmul(pb[:], lhsT=uT[:, kk, s * 128:(s + 1) * 128], rhs=win[:, kk, 2064:2576], start=(kk == 0), stop=(kk == 7)),
                             [win, uT], [pb])
                    k.op("act", lambda e: e.activation(out=z[:, s, :], in_=pb[:], func=AF.Copy), [pb], [z])
                    pb = pget()
                    for kk in range(8):
                        k.op("pe", lambda e: e.matmul(pb[:, 0:32], lhsT=uT[:, kk, s * 128:(s + 1) * 128], rhs=wsm[:, kk, :], start=(kk == 0), stop=(kk == 7)),
                             [wsm, uT], [pb])
                    k.op("dve", lambda e: e.tensor_copy(out=sm[:, s, :], in_=pb[:, 0:32]), [pb], [sm])
                k.dma("pool", zs_tm[t0:t0 + TT, :].rearrange("(s p) d -> p s d", p=128), z[:], [z], [], f"zst{ti % 2}")
                k.dma("pool", sm_tm[t0:t0 + TT, :].rearrange("(s p) d -> p s d", p=128), sm[:], [sm], [], f"smt{ti % 2}")

    def scan_phase(l):
        with k.phase():
            cw_g = k.sb([128, 12, 5], F32, "cw_g")
            for j in range(5):
                k.dma("sp", cw_g[:, :, j], W["gdn_conv"][l, j].rearrange("(g p) -> p g", p=128), [], [cw_g], "cst1", allow_slow_non_contiguous=True)
            cw_s = k.sb([128, 8, 5], F32, "cw_s")
            for j in range(5):
                k.dma("sp", cw_s[:, :, j], W["ssd_conv"][l, j].rearrange("(g p) -> p g", p=128), [], [cw_s], "cst1", allow_slow_non_contiguous=True)
            dg_g = k.sb([128, 12, 5, 128], BF16, "dg_g")
            dg_s = k.sb([128, 8, 5, 128], BF16, "dg_s")
            for g in range(12):
                for j in range(5):
                    k.op("dve", lambda e: e.tensor_scalar(out=dg_g[:, g, j, :], in0=ident_f, scalar1=cw_g[:, g, j:j + 1], scalar2=None, op0=ALU.mult),
                         [cm, cw_g], [dg_g])
            for g in range(8):
                for j in range(5):
                    k.op("dve", lambda e: e.tensor_scalar(out=dg_s[:, g, j, :], in0=ident_f, scalar1=cw_s[:, g, j:j + 1], scalar2=None, op0=ALU.mult),
                         [cm, cw_s], [dg_s])
            cb_row = k.sb([1, 1024], BF16, "cb_row")
            load_w(cb_row, cb_row[:], W["ssd_conv_b"][l:l + 1, :])
            cb_pp = k.sb([128, 8], F32, "cb_pp")
            load_gain(cb_pp, W["ssd_conv_b"][l], 8)
            gA = k.sb([128, 8], F32, "gA")
            gdtb = k.sb([128, 8], F32, "gdtb")
            sA = k.sb([128, 16], F32, "sA")
            sdtb = k.sb([128, 16], F32, "sdtb")
            sD = k.sb([128, 8], F32, "sD")
            gng = k.sb([128, 1], F32, "gng")
            sng = k.sb([128, 512], F32, "sng")
            k.dma("sp", gA[:], W["gdn_A_log"][l].partition_broadcast(128), [], [gA], "cst1")
            k.dma("sp", gdtb[:], W["gdn_dt_bias"][l].partition_broadcast(128), [], [gdtb], "cst1")
            k.dma("sp", sA[:], W["ssd_A_log"][l].partition_broadcast(128), [], [sA], "cst1")
            k.dma("sp", sdtb[:], W["ssd_dt_bias"][l].partition_broadcast(128), [], [sdtb], "cst1")
            k.dma("sp", sD[:], W["ssd_D"][l].partition_broadcast(128), [], [sD], "cst1")
            k.dma("sp", sng[:], W["ssd_norm"][l].partition_broadcast(128), [], [sng], "cst1")
            k.dma("sp", gng[:], W["gdn_norm"][l].rearrange("(p o) -> p o", o=1), [], [gng], "cst1")
            for t in (gA, sA):
                k.op("act", lambda e, t=t: e.activation(out=t[:], in_=t[:], func=AF.Exp), [t], [t])
                k.op("dve", lambda e, t=t: e.tensor_scalar(out=t[:], in0=t[:], scalar1=-1.0, scalar2=None, op0=ALU.mult), [t], [t])
            ones_row = k.sb([1, 128], BF16, "ones_row")
            k.op("dve", lambda e: e.memset(ones_row[:], 1.0), [], [ones_row])

            R = 2
            pre_g = k.ring(R, [128, 12, 132], BF16, "pre_g")
            pre_s = k.ring(R, [128, 8, 132], BF16, "pre_s")
            abt = k.ring(R, [128, 32], F32, "abt")
            qkv = k.ring(R, [128, 12, 128], F32, "qkv")
            vbf = k.ring(R, [128, 4, 128], BF16, "vbf")
            sq = k.sb([128, 8, 128], F32, "sq")
            ssn = k.sb([128, 8], F32, "ssn")
            rinv = k.sb([128, 8], F32, "rinv")
            qkn = k.ring(R, [128, 8, 128], BF16, "qkn")
            gt = k.sb([128, 16], F32, "gt")
            gg = k.ring(R, [128, 8], F32, "gg")
            beta = k.ring(R, [128, 8], F32, "beta")
            nbeta = k.ring(R, [128, 8], F32, "nbeta")
            gcs = k.ring(R, [128, 4], F32, "gcs")
            egc = k.ring(R, [128, 4], F32, "egc")
            edl = k.ring(R, [128, 4], F32, "edl")
            egl = k.ring(R, [128, 4], F32, "egl")
            GL = k.ring(R, [128, 8, 128], F32, "GL")
            Dm = k.ring(R, [128, 8, 128], F32, "Dm")
            DmI = k.sb([128, 4, 128], F32, "DmI")
            kqT = k.ring(R, [128, 4, 2, 128], BF16, "kqT")
            qg = k.sb([128, 4, 128], BF16, "qg")
            qgT = k.ring(R, [128, 4, 128], BF16, "qgT")
            kg = k.ring(R, [128, 4, 128], BF16, "kg")
            kd = k.ring(R, [128, 4, 128], BF16, "kd")
            At = k.ring(R, [128, 4, 128], BF16, "At")
            t1 = k.sb([128, 4, 128], F32, "t1")
            Qs = k.ring(2, [128, 4, 128], INV_DT, "Qs")
            QTs = k.ring(2, [128, 4, 128], INV_DT, "QTs")
            IQT = k.ring(2, [128, 4, 128], INV_DT, "IQT")
            Ts = k.ring(2, [128, 4, 128], INV_DT, "Ts")
            Xbs = k.ring(R, [128, 4, 128], BF16, "Xb")
            nWT = k.ring(R, [128, 4, 128], BF16, "nWT")
            vnew = k.ring(R, [128, 4, 128], BF16, "vnew")
            S32 = [k.sb([128, 4, 128], F32, "S32") for _ in range(2)]
            Sbf = [k.sb([128, 4, 128], BF16, "Sbf") for _ in range(2)]
            osb = k.ring(R, [128, 4, 128], F32, "osb")
            ofl = k.ring(R, [128, 4, 128], F32, "ofl")
            on = k.sb([128, 4, 128], BF16, "on")
            zT = k.ring(R, [128, 4, 128], BF16, "zT")
            sz = k.sb([128, 4, 128], F32, "sz")
            goT = k.ring(R, [128, 4, 128], BF16, "goT")
            xB = k.ring(R, [128, 6, 128], F32, "xB")
            Bbf = k.ring(R, [128, 2, 128], BF16, "Bbf")
            BCT = k.ring(R, [128, 4, 128], BF16, "BCT")
            dts = k.ring(R, [128, 16], F32, "dts")
            dA = k.ring(R, [128, 16], F32, "dA")
            acs = k.ring(R, [128, 8], F32, "acs")
            eac = k.ring(R, [128, 8], F32, "eac")
            dst_ = k.ring(R, [128, 8], F32, "dst")
            ecd = k.ring(R, [128, 8], F32, "ecd")
            Mt = k.ring(R, [128, 8, 128], BF16, "Mt")
            xdt = k.ring(R, [128, 8, 64], BF16, "xdt")
            xdd = k.ring(R, [128, 8, 64], BF16, "xdd")
            dd = k.sb([128, 8], F32, "dd")
            H32 = [k.sb([128, 512], F32, "H32") for _ in range(2)]
            Hbf = [k.sb([128, 512], BF16, "Hbf") for _ in range(2)]
            ysb = k.ring(R, [128, 512], F32, "ysb")
            yfl = k.ring(R, [128, 512], F32, "yfl")
            ytmp = k.sb([128, 512], F32, "ytmp")
            zsl = k.ring(R, [128, 512], F32, "zsl")
            ynb = k.sb([128, 512], BF16, "ynb")
            soT = k.ring(R, [128, 4, 128], BF16, "soT")
            ss1 = k.sb([128, 4], F32, "ss1")
            rs1 = k.sb([128, 4], F32, "rs1")
            cnt = {"g": 0, "s": 0}
            r_of = Res(acc=True)
            r_yf = Res(acc=True)

            def load_pre(buf, row0, ngrp, seq0, L, c, key):
                t0 = c * 128
                lo = max(t0 - 2, 0)
                hi = min(t0 + 130, L)
                if lo != t0 - 2 or hi != t0 + 130:
                    k.op("pool", lambda e: e.memset(buf[:], 0.0), [], [buf])
                k.dma("sp", buf[:, :, lo - (t0 - 2): hi - (t0 - 2)],
                      featT[row0:row0 + ngrp * 128, seq0 + lo: seq0 + hi].rearrange("(g p) t -> p g t", p=128), [], [buf], key)

            def gdn_chunk(seq0, L, c, d):
                i = cnt["g"]
                cnt["g"] += 1
                r = i % R
                tg = seq0 + c * 128
                pg_, ab, qv, vb = pre_g[r], abt[r], qkv[r], vbf[r]
                load_pre(pg_, 0, 12, seq0, L, c, f"preg{r}")
                k.dma("sp", ab[:], sm_tm[tg:tg + 128, :], [], [ab], f"ab{r}")
                for part in range(3):
                    pb = pget()
                    for h in range(4):
                        g = part * 4 + h
                        for j in range(5):
                            k.op("pe", lambda e: e.matmul(pb[:, h * 128:(h + 1) * 128], lhsT=pg_[:, g, j:j + 128], rhs=dg_g[:, g, j, :],
                                                          start=(j == 0), stop=(j == 4)), [pg_, dg_g], [pb])
                    k.op("act", lambda e: e.activation(out=qv[:, part * 4:(part + 1) * 4, :], in_=pb[:].rearrange("p (h d) -> p h d", h=4), func=AF.Silu),
                         [pb], [qv])
                k.op("pool", lambda e: e.tensor_copy(out=vb[:], in_=qv[:, 8:12, :]), [qv], [vb])
                k.op("dve", lambda e: e.tensor_tensor(out=sq[:], in0=qv[:, 0:8, :], in1=qv[:, 0:8, :], op=ALU.mult), [qv], [sq])
                k.op("dve", lambda e: e.tensor_reduce(out=ssn[:], in_=sq[:], axis=AX.X, op=ALU.add), [sq], [ssn])
                rsqrt(rinv, rinv[:], ssn, ssn[:])
                k.op("dve", lambda e: e.tensor_scalar(out=rinv[:, 0:4], in0=rinv[:, 0:4], scalar1=128 ** -0.5, scalar2=None, op0=ALU.mult), [rinv], [rinv])
                qn = qkn[r]
                k.op("dve", lambda e: e.tensor_tensor(out=qn[:], in0=qv[:, 0:8, :], in1=bc3(rinv[:], 128), op=ALU.mult), [qv, rinv], [qn])
                g_, be, nbe = gg[r], beta[r], nbeta[r]
                k.op("dve", lambda e: e.tensor_tensor(out=gt[:, 0:8], in0=ab[:, 0:8], in1=gdtb[:], op=ALU.add), [ab, gdtb], [gt])
                k.op("act", lambda e: e.activation(out=gt[:, 0:8], in_=gt[:, 0:8], func=AF.Exp), [gt], [gt])
                k.op("act", lambda e: e.activation(out=gt[:, 0:8], in_=gt[:, 0:8], func=AF.Ln, bias=cst[:, 1:2]), [gt, cst], [gt])
                k.op("dve", lambda e: e.tensor_tensor(out=g_[:], in0=gt[:, 0:8], in1=gA[:], op=ALU.mult), [gt, gA], [g_])
                k.op("act", lambda e: e.activation(out=gt[:, 8:16], in_=ab[:, 8:16], func=AF.Exp, scale=-1.0), [ab], [gt])
                k.op("dve", lambda e: e.tensor_scalar(out=gt[:, 8:16], in0=gt[:, 8:16], scalar1=1.0, scalar2=None, op0=ALU.add), [gt], [gt])
                k.op("dve", lambda e: e.reciprocal(out=be[:], in_=gt[:, 8:16]), [gt], [be])
                k.op("dve", lambda e: e.tensor_scalar(out=nbe[:], in0=be[:], scalar1=-1.0, scalar2=None, op0=ALU.mult), [be], [nbe])
                gd = g_[:, d * 4:(d + 1) * 4]
                tri = cm[:, 1 + d, :]
                pb = pget()
                k.op("pe", lambda e: e.matmul(pb[:, 0:4], lhsT=tri, rhs=gd, start=True, stop=True), [cm, g_], [pb])
                k.op("pe", lambda e: e.matmul(pb[:, 4:8], lhsT=onesf[:], rhs=gd, start=True, stop=True), [onesf, g_], [pb])
                gc_, eg, ed, el = gcs[r], egc[r], edl[r], egl[r]
                k.op("act", lambda e: e.activation(out=gc_[:], in_=pb[:, 0:4], func=AF.Copy), [pb], [gc_])
                k.op("act", lambda e: e.activation(out=eg[:], in_=pb[:, 0:4], func=AF.Exp), [pb], [eg])
                k.op("act", lambda e: e.activation(out=el[:], in_=pb[:, 4:8], func=AF.Exp), [pb], [el])
                k.op("dve", lambda e: e.tensor_tensor(out=ed[:], in0=pb[:, 4:8], in1=gc_[:], op=ALU.subtract), [pb, gc_], [ed])
                k.op("act", lambda e: e.activation(out=ed[:], in_=ed[:], func=AF.Exp), [ed], [ed])
                gl, dm = GL[r], Dm[r]
                k.op("dve", lambda e: e.tensor_tensor(out=gl[:, 0:4, :], in0=bcm(tri, 4), in1=bc3(gd, 128), op=ALU.mult), [cm, g_], [gl])
                pb = pget()
                k.op("pe", lambda e: e.matmul(pb[:], lhsT=onesf[:], rhs=gl[:, 0:4, :].rearrange("p h i -> p (h i)"), start=True, stop=False), [onesf, gl], [pb])
                for h in range(4):
                    k.op("pe", lambda e: e.matmul(pb[:, h * 128:(h + 1) * 128], lhsT=gl[:, h, :], rhs=nonesf[:], start=False, stop=True), [gl, nonesf], [pb])
                k.op("dve", lambda e: e.tensor_tensor(out=dm[:, 0:4, :], in0=pb[:].rearrange("p (h i) -> p h i", h=4), in1=bcm(cm[:, 3 + d, :], 4), op=ALU.add),
                     [pb, cm], [dm])
                k.op("act", lambda e: e.activation(out=dm[:, 0:4, :], in_=dm[:, 0:4, :], func=AF.Exp), [dm], [dm])
                k.op("pool", lambda e: e.tensor_tensor(out=DmI[:], in0=dm[:, 0:4, :], in1=bcm(ident_f, 4), op=ALU.add), [dm, cm], [DmI])
                kg_, kd_ = kg[r], kd[r]
                k.op("pool", lambda e: e.tensor_tensor(out=qg[:], in0=qn[:, 0:4, :], in1=bc3(eg[:], 128), op=ALU.mult), [qn, eg], [qg])
                k.op("pool", lambda e: e.tensor_tensor(out=kg_[:], in0=qn[:, 4:8, :], in1=bc3(eg[:], 128), op=ALU.mult), [qn, eg], [kg_])
                k.op("pool", lambda e: e.tensor_tensor(out=kd_[:], in0=qn[:, 4:8, :], in1=bc3(ed[:], 128), op=ALU.mult), [qn, ed], [kd_])
                kq, qgt = kqT[r], qgT[r]
                pb = pget()
                pv = pb[:].bitcast(BF16)
                for h in range(4):
                    k.op("pe", lambda e: e.transpose(out=pv[:, (2 * h) * 128:(2 * h + 1) * 128], in_=qn[:, 4 + h, :], identity=identb[:]), [qn, identb], [pb])
                    k.op("pe", lambda e: e.transpose(out=pv[:, (2 * h + 1) * 128:(2 * h + 2) * 128], in_=qn[:, h, :], identity=identb[:]), [qn, identb], [pb])
                k.op("act", lambda e: e.activation(out=kq[:].rearrange("p h t i -> p (h t i)"), in_=pv[:, 0:1024], func=AF.Copy), [pb], [kq])
                pb = pget()
                pv = pb[:].bitcast(BF16)
                for h in range(4):
                    k.op("pe", lambda e: e.transpose(out=pv[:, h * 128:(h + 1) * 128], in_=qg[:, h, :], identity=identb[:]), [qg, identb], [pb])
                k.op("dve", lambda e: e.tensor_copy(out=qgt[:].rearrange("p h i -> p (h i)"), in_=pv[:, 0:512]), [pb], [qgt])
                pk = [pget(), pget()]
                for h in range(4):
                    b = pk[h // 2]
                    k.op("pe", lambda e: e.matmul(b[:, (h % 2) * 256:(h % 2 + 1) * 256], lhsT=kq[:, h, 0, :], rhs=kq[:, h, :, :].rearrange("p t i -> p (t i)"),
                                                  start=True, stop=True), [kq], [b])
                at = At[r]
                q0, q0t = Qs[0], QTs[0]
                for hh in range(2):
                    b = pk[hh]
                    b4 = b[:].rearrange("p (h t i) -> p h t i", h=2, t=2)
                    k.op("dve", lambda e: e.tensor_tensor(out=at[:, 2 * hh:2 * hh + 2, :], in0=b4[:, :, 1, :], in1=DmI[:, 2 * hh:2 * hh + 2, :], op=ALU.mult),
                         [b, DmI], [at])
                    k.op("dve", lambda e: e.tensor_tensor(out=t1[:, 2 * hh:2 * hh + 2, :], in0=b4[:, :, 0, :], in1=dm[:, 2 * hh:2 * hh + 2, :], op=ALU.mult),
                         [b, dm], [t1])
                nb_d = nbe[:, d * 4:(d + 1) * 4]
                k.op("dve", lambda e: e.tensor_tensor(out=q0[:], in0=t1[:], in1=bc3(nb_d, 128), op=ALU.mult), [t1, nbe], [q0])
                ident_inv = identb[:] if INV_DT == BF16 else ident_f
                pb = pget()
                if INV_DT == BF16:
                    pv = pb[:].bitcast(BF16)
                else:
                    pv = pb[:]
                for h in range(4):
                    k.op("pe", lambda e: e.transpose(out=pv[:, h * 128:(h + 1) * 128], in_=q0[:, h, :], identity=ident_inv), [q0, identb, cm], [pb])
                k.op("act", lambda e: e.activation(out=q0t[:].rearrange("p h i -> p (h i)"), in_=pv[:, 0:512], func=AF.Copy), [pb], [q0t])
                T = Ts[0]
                k.op("pool", lambda e: e.tensor_tensor(out=T[:], in0=q0[:], in1=bcm(ident_f, 4), op=ALU.add), [q0, cm], [T])
                Qp, QTp = q0, q0t
                for lev in range(1, 7):
                    Qn, QTn, iq, Tn = Qs[lev % 2], QTs[lev % 2], IQT[lev % 2], Ts[lev % 2]
                    pbt = pget()
                    for h in range(4):
                        k.op("pe", lambda e: e.matmul(pbt[:, h * 128:(h + 1) * 128], lhsT=Qp[:, h, :], rhs=QTp[:, h, :], start=True, stop=True), [Qp, QTp], [pbt])
                    if lev < 6:
                        pbq = pget()
                        for h in range(4):
                            k.op("pe", lambda e: e.matmul(pbq[:, h * 128:(h + 1) * 128], lhsT=QTp[:, h, :], rhs=Qp[:, h, :], start=True, stop=True), [Qp, QTp], [pbq])
                        k.op("act", lambda e: e.activation(out=Qn[:].rearrange("p h i -> p (h i)"), in_=pbq[:], func=AF.Copy), [pbq], [Qn])
                        k.op("act", lambda e: e.activation(out=QTn[:].rearrange("p h i -> p (h i)"), in_=pbt[:], func=AF.Copy), [pbt], [QTn])
                    k.op("dve", lambda e: e.tensor_tensor(out=iq[:], in0=pbt[:].rearrange("p (h i) -> p h i", h=4), in1=bcm(ident_f, 4), op=ALU.add),
                         [pbt, cm], [iq])
                    pbT = pget()
                    for h in range(4):
                        k.op("pe", lambda e: e.matmul(pbT[:, h * 128:(h + 1) * 128], lhsT=iq[:, h, :], rhs=T[:, h, :], start=True, stop=True), [iq, T], [pbT])
                    k.op("dve", lambda e: e.tensor_copy(out=Tn[:].rearrange("p h i -> p (h i)"), in_=pbT[:]), [pbT], [Tn])
                    T, Qp, QTp = Tn, Qn, QTn
                X = Xbs[r]
                k.op("pool", lambda e: e.tensor_copy(out=X[:], in_=T[:]), [T], [X])
                nw = nWT[r]
                pb = pget()
                for h in range(4):
                    k.op("pe", lambda e: e.matmul(pb[:, h * 128:(h + 1) * 128], lhsT=kg_[:, h, :], rhs=X[:, h, :], start=True, stop=True), [kg_, X], [pb])
                k.op("act", lambda e: e.activation(out=nw[:].rearrange("p h i -> p (h i)"), in_=pb[:], func=AF.Copy, scale=-1.0), [pb], [nw])
                S3, Sb = S32[d], Sbf[d]
                first = (c == 0) if d == 0 else (c == L // 128 - 1)
                if first:
                    k.op("pool", lambda e: e.memset(S3[:], 0.0), [], [S3])
                    k.op("pool", lambda e: e.memset(Sb[:], 0.0), [], [Sb])
                pb = pget()
                for h in range(4):
                    k.op("pe", lambda e: e.matmul(pb[:, h * 128:(h + 1) * 128], lhsT=X[:, h, :], rhs=vb[:, h, :], start=True, stop=False), [X, vb], [pb])
                    k.op("pe", lambda e: e.matmul(pb[:, h * 128:(h + 1) * 128], lhsT=nw[:, h, :], rhs=Sb[:, h, :], start=False, stop=True), [nw, Sb], [pb])
                vn = vnew[r]
                k.op("dve", lambda e: e.tensor_tensor(out=vn[:], in0=pb[:].rearrange("p (h i) -> p h i", h=4), in1=bc3(be[:, d * 4:(d + 1) * 4], 128), op=ALU.mult),
                     [pb, be], [vn])
                po = pget()
                for h in range(4):
                    k.op("pe", lambda e: e.matmul(po[:, h * 128:(h + 1) * 128], lhsT=qgt[:, h, :], rhs=Sb[:, h, :], start=True, stop=False), [qgt, Sb], [po])
                    k.op("pe", lambda e: e.matmul(po[:, h * 128:(h + 1) * 128], lhsT=at[:, h, :], rhs=vn[:, h, :], start=False, stop=True), [at, vn], [po])
                pS = pget()
                for h in range(4):
                    k.op("pe", lambda e: e.matmul(pS[:, h * 128:(h + 1) * 128], lhsT=kd_[:, h, :], rhs=vn[:, h, :], start=True, stop=True), [kd_, vn], [pS])
                k.op("dve", lambda e: e.tensor_tensor(out=S3[:], in0=S3[:], in1=bc3(el[:], 128), op=ALU.mult), [S3, el], [S3])
                k.op("dve", lambda e: e.tensor_tensor(out=S3[:], in0=S3[:], in1=pS[:].rearrange("p (h i) -> p h i", h=4), op=ALU.add), [S3, pS], [S3])
                k.op("act", lambda e: e.activation(out=Sb[:], in_=S3[:], func=AF.Copy), [S3], [Sb])
                o = osb[r]
                if d == 0:
                    k.op("act", lambda e: e.activation(out=o[:].rearrange("p h i -> p (h i)"), in_=po[:], func=AF.Copy), [po], [o])
                    k.dma("pool", of_tm[tg:tg + 128, :], o[:].rearrange("p h i -> p (h i)"), [o], [r_of], f"ofst{r}")
                    return
                ofb = ofl[r]
                k.dma("sp", ofb[:].rearrange("p h i -> p (h i)"), of_tm[tg:tg + 128, :], [r_of], [ofb], f"ofld{r}")
                k.op("dve", lambda e: e.tensor_tensor(out=o[:], in0=ofb[:], in1=po[:].rearrange("p (h i) -> p h i", h=4), op=ALU.add), [ofb, po], [o])
                k.op("dve", lambda e: e.tensor_tensor(out=sq[:, 0:4, :], in0=o[:], in1=o[:], op=ALU.mult), [o], [sq])
                k.op("dve", lambda e: e.tensor_reduce(out=ss1[:], in_=sq[:, 0:4, :], axis=AX.X, op=ALU.add), [sq], [ss1])
                rsqrt(rs1, rs1[:], ss1, ss1[:], scale=1.0 / 128)
                k.op("dve", lambda e: e.tensor_tensor(out=on[:], in0=o[:], in1=bc3(rs1[:], 128), op=ALU.mult), [o, rs1], [on])
                z_ = zT[r]
                k.dma("sp", z_[:], featT[1536:2048, tg:tg + 128].rearrange("(h p) t -> p h t", p=128), [], [z_], f"zT{r}")
                k.op("act", lambda e: e.activation(out=sz[:], in_=z_[:], func=AF.Silu), [z_], [sz])
                pb = pget()
                pv = pb[:].bitcast(BF16)
                for h in range(4):
                    k.op("pe", lambda e: e.transpose(out=pv[:, h * 128:(h + 1) * 128], in_=on[:, h, :], identity=identb[:]), [on, identb], [pb])
                go = goT[r]
                k.op("dve", lambda e: e.scalar_tensor_tensor(out=go[:].rearrange("p h i -> p (h i)"), in0=pv[:, 0:512], scalar=gng[:, 0:1],
                                                             in1=sz[:].rearrange("p h i -> p (h i)"), op0=ALU.mult, op1=ALU.mult), [pb, gng, sz], [go])
                k.dma("pool", brT[0][:, tg:tg + 128].rearrange("(h p) t -> p h t", p=128), go[:], [go], [], f"gost{r}")

            def ssd_chunk(seq0, L, c, d):
                i = cnt["s"]
                cnt["s"] += 1
                r = i % R
                tg = seq0 + c * 128
                ps_, ab = pre_s[r], abt[r]
                load_pre(ps_, 2048, 8, seq0, L, c, f"pres{r}")
                xb, bb, bct = xB[r], Bbf[r], BCT[r]
                for part, (g0, ng) in enumerate(((0, 4), (4, 2))):
                    pb = pget()
                    for gi in range(ng):
                        g = g0 + gi
                        for j in range(5):
                            k.op("pe", lambda e: e.matmul(pb[:, gi * 128:(gi + 1) * 128], lhsT=ps_[:, g, j:j + 128], rhs=dg_s[:, g, j, :],
                                                          start=(j == 0), stop=False), [ps_, dg_s], [pb])
                        k.op("pe", lambda e: e.matmul(pb[:, gi * 128:(gi + 1) * 128], lhsT=ones_row[0:1, :], rhs=cb_row[0:1, g * 128:(g + 1) * 128],
                                                      start=False, stop=True), [ones_row, cb_row], [pb])
                    k.op("act", lambda e: e.activation(out=xb[:, g0:g0 + ng, :], in_=pb[:, 0:ng * 128].rearrange("p (h d) -> p h d", h=ng), func=AF.Silu),
                         [pb], [xb])
                k.op("pool", lambda e: e.tensor_copy(out=bb[:], in_=xb[:, 4:6, :]), [xb], [bb])
                pb = pget()
                for gi in range(4):
                    g = 4 + gi
                    for j in range(5):
                        k.op("pe", lambda e: e.matmul(pb[:, gi * 128:(gi + 1) * 128], lhsT=dg_s[:, g, j, :], rhs=ps_[:, g, j:j + 128],
                                                      start=(j == 0), stop=(j == 4)), [ps_, dg_s], [pb])
                for gi in range(4):
                    k.op("act", lambda e: e.activation(out=bct[:, gi, :], in_=pb[:, gi * 128:(gi + 1) * 128], func=AF.Silu, bias=cb_pp[:, 4 + gi:5 + gi]),
                         [pb, cb_pp], [bct])
                dt_, da = dts[r], dA[r]
                k.op("dve", lambda e: e.tensor_tensor(out=dt_[:], in0=ab[:, 16:32], in1=sdtb[:], op=ALU.add), [ab, sdtb], [dt_])
                k.op("act", lambda e: e.activation(out=dt_[:], in_=dt_[:], func=AF.Exp), [dt_], [dt_])
                k.op("act", lambda e: e.activation(out=dt_[:], in_=dt_[:], func=AF.Ln, bias=cst[:, 1:2]), [dt_, cst], [dt_])
                k.op("dve", lambda e: e.tensor_tensor(out=da[:], in0=dt_[:], in1=sA[:], op=ALU.mult), [dt_, sA], [da])
                dad = da[:, d * 8:(d + 1) * 8]
                dtd = dt_[:, d * 8:(d + 1) * 8]
                tri = cm[:, 1 + d, :]
                pb = pget()
                k.op("pe", lambda e: e.matmul(pb[:, 0:8], lhsT=tri, rhs=dad, start=True, stop=True), [cm, da], [pb])
                k.op("pe", lambda e: e.matmul(pb[:, 8:16], lhsT=onesf[:], rhs=dad, start=True, stop=True), [onesf, da], [pb])
                ac, ea, ds, ec = acs[r], eac[r], dst_[r], ecd[r]
                k.op("act", lambda e: e.activation(out=ac[:], in_=pb[:, 0:8], func=AF.Copy), [pb], [ac])
                k.op("act", lambda e: e.activation(out=ea[:], in_=pb[:, 0:8], func=AF.Exp), [pb], [ea])
                k.op("act", lambda e: e.activation(out=ec[:], in_=pb[:, 8:16], func=AF.Exp), [pb], [ec])
                k.op("dve", lambda e: e.tensor_tensor(out=ds[:], in0=pb[:, 8:16], in1=ac[:], op=ALU.subtract), [pb, ac], [ds])
                k.op("act", lambda e: e.activation(out=ds[:], in_=ds[:], func=AF.Exp), [ds], [ds])
                gl, dm = GL[r], Dm[r]
                k.op("dve", lambda e: e.tensor_tensor(out=gl[:], in0=bcm(tri, 8), in1=bc3(dad, 128), op=ALU.mult), [cm, da], [gl])
                pbs = [pget(), pget()]
                for hh in range(2):
                    b = pbs[hh]
                    k.op("pe", lambda e: e.matmul(b[:], lhsT=onesf[:], rhs=gl[:, hh * 4:(hh + 1) * 4, :].rearrange("p h i -> p (h i)"), start=True, stop=False),
                         [onesf, gl], [b])
                    for h in range(4):
                        k.op("pe", lambda e: e.matmul(b[:, h * 128:(h + 1) * 128], lhsT=gl[:, hh * 4 + h, :], rhs=nonesf[:], start=False, stop=True),
                             [gl, nonesf], [b])
                    k.op("dve", lambda e: e.tensor_tensor(out=dm[:, hh * 4:(hh + 1) * 4, :], in0=b[:].rearrange("p (h i) -> p h i", h=4),
                                                          in1=bcm(cm[:, 5 + d, :], 4), op=ALU.add), [b, cm], [dm])
                k.op("act", lambda e: e.activation(out=dm[:], in_=dm[:], func=AF.Exp), [dm], [dm])
                pb = pget()
                for g in range(2):
                    k.op("pe", lambda e: e.matmul(pb[:, g * 128:(g + 1) * 128], lhsT=bct[:, g, :], rhs=bct[:, 2 + g, :], start=True, stop=True), [bct], [pb])
                mt = Mt[r]
                for g in range(2):
                    k.op("dve", lambda e: e.tensor_tensor(out=mt[:, g * 4:(g + 1) * 4, :], in0=dm[:, g * 4:(g + 1) * 4, :],
                                                          in1=bcm(pb[:, g * 128:(g + 1) * 128], 4), op=ALU.mult), [dm, pb], [mt])
                xd, xe = xdt[r], xdd[r]
                x3 = xb[:, 0:4, :].rearrange("p g (h2 q) -> p (g h2) q", h2=2)
                k.op("pool", lambda e: e.tensor_tensor(out=xd[:], in0=x3, in1=bc3(dtd, 64), op=ALU.mult), [xb, dt_], [xd])
                k.op("pool", lambda e: e.tensor_tensor(out=dd[:], in0=dtd, in1=ds[:], op=ALU.mult), [dt_, ds], [dd])
                k.op("pool", lambda e: e.tensor_tensor(out=xe[:], in0=x3, in1=bc3(dd[:], 64), op=ALU.mult), [xb, dd], [xe])
                H3, Hb = H32[d], Hbf[d]
                first = (c == 0) if d == 0 else (c == L // 128 - 1)
                if first:
                    k.op("pool", lambda e: e.memset(H3[:], 0.0), [], [H3])
                    k.op("pool", lambda e: e.memset(Hb[:], 0.0), [], [Hb])
                py = pget()
                for h in range(8):
                    k.op("pe", lambda e: e.matmul(py[:, h * 64:(h + 1) * 64], lhsT=mt[:, h, :], rhs=xd[:, h, :], start=True, stop=True), [mt, xd], [py])
                pyo = pget()
                for g in range(2):
                    k.op("pe", lambda e: e.matmul(pyo[:, g * 256:(g + 1) * 256], lhsT=bct[:, 2 + g, :], rhs=Hb[:, g * 256:(g + 1) * 256], start=True, stop=True),
                         [bct, Hb], [pyo])
                pst = pget()
                for g in range(2):
                    k.op("pe", lambda e: e.matmul(pst[:, g * 256:(g + 1) * 256], lhsT=bb[:, g, :], rhs=xe[:, g * 4:(g + 1) * 4, :].rearrange("p h q -> p (h q)"),
                                                  start=True, stop=True), [bb, xe], [pst])
                y = ysb[r]
                k.op("dve", lambda e: e.tensor_tensor(out=ytmp[:].rearrange("p (h q) -> p h q", h=8), in0=pyo[:].rearrange("p (h q) -> p h q", h=8),
                                                      in1=bc3(ea[:], 64), op=ALU.mult), [pyo, ea], [ytmp])
                k.op("dve", lambda e: e.tensor_tensor(out=y[:], in0=ytmp[:], in1=py[:], op=ALU.add), [ytmp, py], [y])
                k.op("dve", lambda e: e.tensor_tensor(out=H3[:].rearrange("p (h q) -> p h q", h=8), in0=H3[:].rearrange("p (h q) -> p h q", h=8),
                                                      in1=bc3(ec[:], 64), op=ALU.mult), [H3, ec], [H3])
                k.op("dve", lambda e: e.tensor_tensor(out=H3[:], in0=H3[:], in1=pst[:], op=ALU.add), [H3, pst], [H3])
                k.op("act", lambda e: e.activation(out=Hb[:], in_=H3[:], func=AF.Copy), [H3], [Hb])
                if d == 0:
                    k.dma("pool", yf_tm[tg:tg + 128, :], y[:], [y], [r_yf], f"yfst{r}")
                    return
                yf = yfl[r]
                zl = zsl[r]
                k.dma("sp", yf[:], yf_tm[tg:tg + 128, :], [r_yf], [yf], f"yfld{r}")
                k.dma("sp", zl[:], zs_tm[tg:tg + 128, :], [], [zl], f"zsl{r}")
                k.op("dve", lambda e: e.tensor_tensor(out=y[:], in0=y[:], in1=yf[:], op=ALU.add), [y, yf], [y])
                k.op("pool", lambda e: e.tensor_tensor(out=ytmp[:].rearrange("p (h q) -> p h q", h=8), in0=x3, in1=bc3(sD[:], 64), op=ALU.mult), [xb, sD], [ytmp])
                k.op("dve", lambda e: e.tensor_tensor(out=y[:], in0=y[:], in1=ytmp[:], op=ALU.add), [y, ytmp], [y])
                k.op("act", lambda e: e.activation(out=zl[:], in_=zl[:], func=AF.Silu), [zl], [zl])
                k.op("dve", lambda e: e.tensor_tensor(out=y[:], in0=y[:], in1=zl[:], op=ALU.mult), [y, zl], [y])
                k.op("dve", lambda e: e.scalar_tensor_tensor(out=ytmp[:], in0=y[:], scalar=1.0, in1=y[:], op0=ALU.mult, op1=ALU.mult, accum_out=ss1[:, 0:1]),
                     [y], [ytmp, ss1])
                rsqrt(rs1, rs1[:, 0:1], ss1, ss1[:, 0:1], scale=1.0 / 512)
                k.op("dve", lambda e: e.scalar_tensor_tensor(out=ynb[:], in0=y[:], scalar=rs1[:, 0:1], in1=sng[:], op0=ALU.mult, op1=ALU.mult),
                     [y, rs1, sng], [ynb])
                pb = pget()
                pv = pb[:].bitcast(BF16)
                for cc in range(4):
                    k.op("pe", lambda e: e.transpose(out=pv[:, cc * 128:(cc + 1) * 128], in_=ynb[:, cc * 128:(cc + 1) * 128], identity=identb[:]), [ynb, identb], [pb])
                so = soT[r]
                k.op("act", lambda e: e.activation(out=so[:].rearrange("p h i -> p (h i)"), in_=pv[:, 0:512], func=AF.Copy), [pb], [so])
                k.dma("pool", brT[1][:, tg:tg + 128].rearrange("(h p) t -> p h t", p=128), so[:], [so], [], f"sost{r}")

            seq0 = 0
            for L in seqs:
                nch = L // 128
                for d in range(2):
                    order = range(nch) if d == 0 else range(nch - 1, -1, -1)
                    for c in order:
                        gdn_chunk(seq0, L, c, d)
                        ssd_chunk(seq0, L, c, d)
                seq0 += L

    def mla_phase(l):
        with k.phase():
            pset(range(8))
            wuq = k.sb([128, 2, 768 + 256], BF16, "wuq")
            wukv = k.sb([128, 2, 1024], BF16, "wukv")
            for c in range(2):
                load_w(wuq, wuq[:, c, 0:768], W["mla_w_uq"][l, c * 128:(c + 1) * 128, :])
                load_w(wukv, wukv[:, c, :], W["mla_w_ukv"][l, c * 128:(c + 1) * 128, :])
                for h in range(4):
                    load_w(wuq, wuq[:, c, 768 + h * 64:768 + h * 64 + 32], W["mla_w_uq"][l, c * 128:(c + 1) * 128, h * 192 + 160:h * 192 + 192], scale=-1.0)
                    load_w(wuq, wuq[:, c, 768 + h * 64 + 32:768 + h * 64 + 64], W["mla_w_uq"][l, c * 128:(c + 1) * 128, h * 192 + 128:h * 192 + 160])
            wv = k.sb([128, 2, 512], BF16, "wv")
            k.op("pool", lambda e: e.tensor_copy(out=wv[:].rearrange("p c (h x) -> p c h x", h=4),
                                                 in_=wukv[:].rearrange("p c (h x) -> p c h x", h=4)[:, :, :, 128:256]), [wukv], [wv])
            qg_ = k.sb([128, 2], F32, "qgain")
            kg_ = k.sb([128, 2], F32, "kvgain")
            load_gain(qg_, W["mla_q_norm"][l], 2)
            load_gain(kg_, W["mla_kv_norm"][l], 2)
            lat = k.ring(2, [128, 4, TT], BF16, "lat")
            krr = k.ring(2, [64, 2, TT], BF16, "krr")
            cs = k.ring(2, [64, 2, TT], F32, "cs")
            sqb = k.sb([128, 2, TT], BF16, "sqb")
            rb = k.sb([128, TT], F32, "rb")
            nrm = k.ring(2, [128, 4, TT], BF16, "nrm")
            stg = k.ring(4, [128, TT], BF16, "stg")
            r1 = k.sb([64, TT], F32, "r1")
            r2 = k.sb([64, TT], F32, "r2")
            rst = k.ring(4, [64, TT], BF16, "rst")
            vst = k.ring(2, [128, 4, 512], BF16, "vst")
            ns = 0
            nr = 0
            seq0 = 0
            ti = 0
            for L in seqs:
                for p0 in range(0, L, TT):
                    t0 = seq0 + p0
                    la, kr_, cs_ = lat[ti % 2], krr[ti % 2], cs[ti % 2]
                    k.dma("sp", la[:], featT[3072:3584, t0:t0 + TT].rearrange("(c p) t -> p c t", p=128), [], [la], f"lat{ti % 2}")
                    k.dma("sp", kr_[:], featT[3584:3712, t0:t0 + TT].rearrange("(c p) t -> p c t", p=64), [], [kr_], f"krr{ti % 2}")
                    k.dma("sp", cs_[:], ropecs[:, :, p0:p0 + TT].rearrange("c p t -> p c t"), [], [cs_], f"cs{ti % 2}")
                    nm = nrm[ti % 2]
                    for which in range(2):
                        gain = qg_ if which == 0 else kg_
                        k.op("pool", lambda e: e.tensor_tensor(out=sqb[:], in0=la[:, 2 * which:2 * which + 2, :], in1=la[:, 2 * which:2 * which + 2, :], op=ALU.mult),
                             [la], [sqb])
                        pb = pget()
                        for c in range(2):
                            k.op("pe", lambda e: e.matmul(pb[:], lhsT=onesb[:], rhs=sqb[:, c, :], start=(c == 0), stop=(c == 1)), [onesb, sqb], [pb])
                        rsqrt(rb, rb[:], pb, pb[:], scale=1.0 / 256)
                        for c in range(2):
                            k.op("dve", lambda e: e.scalar_tensor_tensor(out=nm[:, 2 * which + c, :], in0=la[:, 2 * which + c, :], scalar=gain[:, c:c + 1], in1=rb[:],
                                                                         op0=ALU.mult, op1=ALU.mult), [la, gain, rb], [nm])
                    for h in range(4):
                        pb = pget()
                        for c in range(2):
                            k.op("pe", lambda e: e.matmul(pb[:], lhsT=wuq[:, c, h * 192:h * 192 + 128], rhs=nm[:, c, :], start=(c == 0), stop=(c == 1)), [wuq, nm], [pb])
                        st = stg[ns % 4]
                        k.op("act", lambda e: e.activation(out=st[:], in_=pb[:], func=AF.Copy), [pb], [st])
                        k.dma("pool", qT_d[h, :, t0:t0 + TT], st[:], [st], [], f"mstg{ns % 4}")
                        ns += 1
                        pb = pget()
                        for c in range(2):
                            k.op("pe", lambda e: e.matmul(pb[:], lhsT=wukv[:, c, h * 256:h * 256 + 128], rhs=nm[:, 2 + c, :], start=(c == 0), stop=(c == 1)), [wukv, nm], [pb])
                        st = stg[ns % 4]
                        k.op("act", lambda e: e.activation(out=st[:], in_=pb[:], func=AF.Copy), [pb], [st])
                        k.dma("pool", kT_d[h, :, t0:t0 + TT], st[:], [st], [], f"mstg{ns % 4}")
                        ns += 1
                        pb = pget()
                        pb2 = pget()
                        for c in range(2):
                            k.op("pe", lambda e: e.matmul(pb[0:64, :], lhsT=wuq[:, c, h * 192 + 128:h * 192 + 192], rhs=nm[:, c, :], start=(c == 0), stop=(c == 1)), [wuq, nm], [pb])
                        for c in range(2):
                            k.op("pe", lambda e: e.matmul(pb2[0:64, :], lhsT=wuq[:, c, 768 + h * 64:768 + (h + 1) * 64], rhs=nm[:, c, :], start=(c == 0), stop=(c == 1)), [wuq, nm], [pb2])
                        k.op("dve", lambda e: e.tensor_tensor(out=r1[:], in0=pb[0:64, :], in1=cs_[:, 0, :], op=ALU.mult), [pb, cs_], [r1])
                        k.op("dve", lambda e: e.tensor_tensor(out=r2[:], in0=pb2[0:64, :], in1=cs_[:, 1, :], op=ALU.mult), [pb2, cs_], [r2])
                        rs_ = rst[nr % 4]
                        k.op("pool", lambda e: e.tensor_tensor(out=rs_[:], in0=r1[:], in1=r2[:], op=ALU.add), [r1, r2], [rs_])
                        k.dma("pool", qrT_d[h, :, t0:t0 + TT], rs_[:], [rs_], [], f"rstg{nr % 4}")
                        nr += 1
                    k.op("dve", lambda e: e.tensor_tensor(out=r1[:], in0=kr_[:, 0, :], in1=cs_[:, 0, :], op=ALU.mult), [kr_, cs_], [r1])
                    k.op("dve", lambda e: e.tensor_tensor(out=r2[:], in0=kr_[:, 1, :], in1=cs_[:, 1, :], op=ALU.mult), [kr_, cs_], [r2])
                    rs_ = rst[nr % 4]
                    k.op("pool", lambda e: e.tensor_tensor(out=rs_[:], in0=r1[:], in1=r2[:], op=ALU.add), [r1, r2], [rs_])
                    k.dma("pool", krT_d[:, t0:t0 + TT], rs_[:], [rs_], [], f"rstg{nr % 4}")
                    nr += 1
                    vs = vst[ti % 2]
                    for s in range(4):
                        pb = pget()
                        for c in range(2):
                            k.op("pe", lambda e: e.matmul(pb[:], lhsT=nm[:, 2 + c, s * 128:(s + 1) * 128], rhs=wv[:, c, :], start=(c == 0), stop=(c == 1)), [wv, nm], [pb])
                        k.op("act", lambda e: e.activation(out=vs[:, s, :], in_=pb[:], func=AF.Copy), [pb], [vs])
                    k.dma("pool", v_d[t0:t0 + TT, :].rearrange("(s p) d -> p s d", p=128), vs[:], [vs], [], f"vst{ti % 2}")
                    ti += 1
                seq0 += L
        with k.phase():
            pset([0, 1, 2])
            kT = k.ring(2, [128, LMAX], BF16, "kT")
            vv = k.ring(2, [128, LMAX // 128, 128], BF16, "vv")
            krT = k.ring(2, [64, LMAX], BF16, "krT")
            qt = k.ring(2, [128, TT], BF16, "qt")
            qrt = k.ring(2, [64, TT], BF16, "qrt")
            pt = k.ring(3, [128, TT], BF16, "pt")
            rl = k.sb([128, TT], F32, "rl")
            ot = k.ring(2, [128, TT], BF16, "ot")
            seq0 = 0
            nq = 0
            npb = 0
            for si, L in enumerate(seqs):
                nkb = L // 128
                kr_ = krT[si % 2]
                k.dma("sp", kr_[:, 0:L], krT_d[:, seq0:seq0 + L], [], [kr_], f"krT{si % 2}")
                for h in range(4):
                    hh = (si * 4 + h) % 2
                    kt_, v_ = kT[hh], vv[hh]
                    k.dma("sp", kt_[:, 0:L], kT_d[h, :, seq0:seq0 + L], [], [kt_], f"kT{hh}")
                    for b0 in range(0, nkb, 8):
                        b1 = min(b0 + 8, nkb)
                        k.dma("sp", v_[:, b0:b1, :], v_d[seq0 + b0 * 128:seq0 + b1 * 128, h * 128:(h + 1) * 128].rearrange("(n p) d -> p n d", p=128), [], [v_], f"vv{hh}")
                    for p0 in range(0, L, TT):
                        t0 = seq0 + p0
                        q_, qr_ = qt[nq % 2], qrt[nq % 2]
                        k.dma("sp", q_[:], qT_d[h, :, t0:t0 + TT], [], [q_], f"qt{nq % 2}")
                        k.dma("sp", qr_[:], qrT_d[h, :, t0:t0 + TT], [], [qr_], f"qrt{nq % 2}")
                        po = banks[3 + 2 * (nq % 2)]
                        pl = banks[4 + 2 * (nq % 2)]
                        for kb in range(nkb):
                            ps_ = pget()
                            k.op("pe", lambda e: e.matmul(ps_[:], lhsT=kt_[:, kb * 128:(kb + 1) * 128], rhs=q_[:], start=True, stop=False), [kt_, q_], [ps_])
                            k.op("pe", lambda e: e.matmul(ps_[:], lhsT=kr_[:, kb * 128:(kb + 1) * 128], rhs=qr_[:], start=False, stop=True), [kr_, qr_], [ps_])
                            p_ = pt[npb % 3]
                            npb += 1
                            k.op("act", lambda e: e.activation(out=p_[:], in_=ps_[:], func=AF.Exp, scale=MLA_SCALE), [ps_], [p_])
                            k.op("pe", lambda e: e.matmul(po[:], lhsT=v_[:, kb, :], rhs=p_[:], start=(kb == 0), stop=(kb == nkb - 1)), [v_, p_], [po])
                            k.op("pe", lambda e: e.matmul(pl[:], lhsT=onesb[:], rhs=p_[:], start=(kb == 0), stop=(kb == nkb - 1)), [onesb, p_], [pl])
                        k.op("dve", lambda e: e.reciprocal(out=rl[:], in_=pl[:]), [pl], [rl])
                        o_ = ot[nq % 2]
                        k.op("dve", lambda e: e.tensor_tensor(out=o_[:], in0=po[:], in1=rl[:], op=ALU.mult), [po, rl], [o_])
                        k.dma("pool", brT[2][h * 128:(h + 1) * 128, t0:t0 + TT], o_[:], [o_], [], f"ot{nq % 2}")
                        nq += 1
                seq0 += L

    def merge_phase(l, src, dst):
        with k.phase():
            pset(range(8))
            wg = k.sb([128, 8, 3 * D], BF16, "wg")
            for kk in range(8):
                load_w_wide(wg, wg[:, kk, :], W["w_gate"][l, kk * 128:(kk + 1) * 128, :])
            wbr = k.sb([128, 3, 4, D], BF16, "wbr")
            for bi, wn in enumerate(("w_branch_a", "w_branch_b", "w_branch_c")):
                for kk in range(4):
                    load_w(wbr, wbr[:, bi, kk, :], W[wn][l, kk * 128:(kk + 1) * 128, :])
            wo = k.sb([128, 8, D], BF16, "wo")
            for kk in range(8):
                load_w(wo, wo[:, kk, :], W["w_out"][l, kk * 128:(kk + 1) * 128, :])
            gain = k.sb([128, 8], F32, "gain")
            load_gain(gain, W["mix_norm"][l], 8)
            bg = k.sb([128, 24], F32, "bg")
            k.dma("sp", bg[:], W["b_gate"][l].rearrange("(kk p) -> p kk", p=128), [], [bg], "gain", allow_slow_non_contiguous=True)
            hbs = k.ring(2, [128, 4, D], F32, "hb")
            xnr = k.ring(4, [128, D], BF16, "xn")
            uT = k.sb([128, 8, TT], BF16, "uT")
            ss = k.sb([128, 8], F32, "ss")
            rs = k.sb([128, 8], F32, "rs")
            brs = k.ring(2, [128, 3, 4, TT], BF16, "brs")
            gts = k.ring(3, [128, TT], F32, "gts")
            macc = k.sb([128, TT], F32, "macc")
            mt2 = k.sb([128, TT], F32, "mt2")
            mT = k.sb([128, 8, TT], BF16, "mT")
            ctr = [0]
            ng = 0
            for ti in range(NT // TT):
                t0 = ti * TT
                hb = hbs[ti % 2]
                load_h(src, t0, hb, f"hld{ti % 2}")
                br = brs[ti % 2]
                for bi in range(3):
                    k.dma("sp", br[:, bi, :, :], brT[bi][:, t0:t0 + TT].rearrange("(c p) t -> p c t", p=128), [], [br], f"brs{ti % 2}")
                norm_T(hb, gain, uT, xnr, ss, rs, ctr)
                for c in range(8):
                    for bi in range(3):
                        pgt = pget()
                        for kk in range(8):
                            k.op("pe", lambda e: e.matmul(pgt[:], lhsT=wg[:, kk, bi * D + c * 128: bi * D + (c + 1) * 128], rhs=uT[:, kk, :], start=(kk == 0), stop=(kk == 7)),
                                 [wg, uT], [pgt])
                        py = pget()
                        for kk in range(4):
                            k.op("pe", lambda e: e.matmul(py[:], lhsT=wbr[:, bi, kk, c * 128:(c + 1) * 128], rhs=br[:, bi, kk, :], start=(kk == 0), stop=(kk == 3)),
                                 [wbr, br], [py])
                        gt_ = gts[ng % 3]
                        ng += 1
                        k.op("act", lambda e: e.activation(out=gt_[:], in_=pgt[:], func=AF.Sigmoid, bias=bg[:, bi * 8 + c: bi * 8 + c + 1]), [pgt, bg], [gt_])
                        if bi == 0:
                            k.op("dve", lambda e: e.tensor_tensor(out=macc[:], in0=gt_[:], in1=py[:], op=ALU.mult), [gt_, py], [macc])
                        elif bi == 1:
                            k.op("dve", lambda e: e.tensor_tensor(out=mt2[:], in0=gt_[:], in1=py[:], op=ALU.mult), [gt_, py], [mt2])
                            k.op("pool", lambda e: e.tensor_tensor(out=macc[:], in0=macc[:], in1=mt2[:], op=ALU.add), [macc, mt2], [macc])
                        else:
                            k.op("dve", lambda e: e.tensor_tensor(out=mt2[:], in0=gt_[:], in1=py[:], op=ALU.mult), [gt_, py], [mt2])
                            k.op("pool", lambda e: e.tensor_tensor(out=mT[:, c, :], in0=macc[:], in1=mt2[:], op=ALU.add), [macc, mt2], [mT])
                for s in range(4):
                    for fh in range(2):
                        po = pget()
                        for c in range(8):
                            k.op("pe", lambda e: e.matmul(po[:], lhsT=mT[:, c, s * 128:(s + 1) * 128], rhs=wo[:, c, fh * 512:(fh + 1) * 512], start=(c == 0), stop=(c == 7)),
                                 [mT, wo], [po])
                        k.op("dve", lambda e: e.tensor_tensor(out=hb[:, s, fh * 512:(fh + 1) * 512], in0=po[:], in1=hb[:, s, fh * 512:(fh + 1) * 512], op=ALU.add),
                             [po, hb], [hb])
                k.dma("pool", dst[t0:t0 + TT, :].rearrange("(s p) d -> p s d", p=128), hb[:], [hb], [], f"hst{ti % 2}")

    import os
    stopat = int(os.environ.get("STOPAT", "999"))
    phases = []
    cur = xin
    for l in range(DEPTH):
        last = (l == DEPTH - 1)
        phases.append(lambda l=l, cur=cur: ffn_phase(l, 0, cur, hA, False))
        phases.append(lambda l=l: m1_phase(l, hA))
        phases.append(lambda l=l: scan_phase(l))
        phases.append(lambda l=l: mla_phase(l))
        phases.append(lambda l=l: merge_phase(l, hA, hB))
        phases.append(lambda l=l, last=last: ffn_phase(l, 1, hB, yout if last else hA, last))
        cur = hA
    for i, ph in enumerate(phases):
        if i >= stopat:
            break
        ph()
    if stopat < 999:
        scr = {"hA": hA, "hB": hB, "featT": featT, "zs_tm": zs_tm, "sm_tm": sm_tm, "of_tm": of_tm, "yf_tm": yf_tm,
               "brT0": brT[0], "brT1": brT[1], "brT2": brT[2], "krT_d": krT_d, "v_d": v_d, "qT_d": qT_d.rearrange("h p t -> (h p) t"), "qrT_d": qrT_d.rearrange("h p t -> (h p) t"), "kT_d": kT_d.rearrange("h p t -> (h p) t")}
        for n, ap in scr.items():
            pr = nc.dram_tensor("p_" + n, list(ap.shape), ap.dtype, kind="ExternalOutput").ap()
            for r0 in range(0, ap.shape[0], 128):
                r1 = min(r0 + 128, ap.shape[0])
                k.dma("sp", pr[r0:r1], ap[r0:r1], [], [], "probe")
    if stopat < 999:
        for n, b, shp, dt_ in (("onesb", onesb, [128, 128], BF16), ("identb", identb, [128, 128], BF16), ("onesf", onesf, [128, 128], F32),
                               ("cst", cst, [128, 4], F32), ("cm", cm, [128, 7 * 128], F32)):
            pr = nc.dram_tensor("p_" + n, shp, dt_, kind="ExternalOutput").ap()
            src = b[:] if n != "cm" else b[:].rearrange("p a b -> p (a b)")
            k.dma("sp", pr, src, [b], [], "probe")
    k.barrier()
    return nc, k


def _consts(LMAX):
    idx = np.arange(128)
    j = idx[:, None]
    i = idx[None, :]
    cm = np.zeros((128, 7, 128), np.float32)
    cm[:, 0] = (j == i)
    cm[:, 1] = (j <= i)
    cm[:, 2] = (j >= i)
    NEG = -1e30
    cm[:, 3] = np.where(i > j, 0.0, NEG)
    cm[:, 4] = np.where(i < j, 0.0, NEG)
    cm[:, 5] = np.where(i >= j, 0.0, NEG)
    cm[:, 6] = np.where(i <= j, 0.0, NEG)
    inv_freq = np.power(np.float32(10000.0), -np.arange(0, 64, 2, dtype=np.float32) / np.float32(64)).astype(np.float32)
    ang = np.arange(LMAX, dtype=np.float32)[:, None] * inv_freq[None, :]
    ang = np.concatenate([ang, ang], axis=-1).astype(np.float32)
    rc = np.stack([np.cos(ang).T, np.sin(ang).T]).astype(np.float32)
    return cm, np.ascontiguousarray(rc)


_CACHE = {}


def run(seq_lists, x_per_core, weights):
    seqs = tuple(seq_lists)
    if seqs not in _CACHE:
        _CACHE[seqs] = kernel_build(list(seqs))
    nc, _ = _CACHE[seqs]
    cm, rc = _consts(max(seqs))
    wm = {}
    for n, a in weights.items():
        a = np.asarray(a, np.float32)
        if n in ("gdn_A_log", "gdn_dt_bias"):
            a = a.reshape(DEPTH, 8)
        elif n in ("ssd_A_log", "ssd_dt_bias"):
            a = a.reshape(DEPTH, 16)
        elif n == "final_norm":
            a = a.reshape(1, D)
        wm[n] = np.ascontiguousarray(a)
    in_maps = []
    for x in x_per_core:
        m = dict(wm)
        m["xin"] = np.ascontiguousarray(x, dtype=np.float32)
        m["cmask"] = cm
        m["ropecs"] = rc
        in_maps.append(m)
    res = run_bass_kernel_spmd(nc, in_maps, core_ids=list(range(len(x_per_core))))
    return [r["yout"] for r in res.results]


def kernel(x_prompt, x_sample, **weights):
    x_prompt = np.asarray(x_prompt, np.float32)
    x_sample = np.asarray(x_sample, np.float32)
    B, S, _ = x_prompt.shape
    DB, DS, _ = x_sample.shape
    n = N_CORES
    pp = B // n
    sp = DB // n
    seqs = [S] * pp + [DS] * sp
    xs = []
    for c in range(n):
        parts = [x_prompt[c * pp + i] for i in range(pp)] + [x_sample[c * sp + i] for i in range(sp)]
        xs.append(np.concatenate(parts, axis=0))
    outs = run(seqs, xs, weights)
    y_prompt = np.empty_like(x_prompt)
    y_sample = np.empty_like(x_sample)
    for c in range(n):
        o = outs[c]
        off = 0
        for i in range(pp):
            y_prompt[c * pp + i] = o[off:off + S]
            off += S
        for i in range(sp):
            y_sample[c * sp + i] = o[off:off + DS]
            off += DS
    return (y_prompt, y_sample)
```

```python
import math
import numpy as np
from contextlib import ExitStack, contextmanager
import concourse.bass as bass
import concourse.mybir as mybir
from concourse.bass_utils import run_bass_kernel_spmd

F32 = mybir.dt.float32
BF16 = mybir.dt.bfloat16
AF = mybir.ActivationFunctionType
ALU = mybir.AluOpType
AX = mybir.AxisListType

D = 1024
FH = 2816
NJ = FH // 128
DEPTH = 2
INC = 4192
EPS = 1e-6
MLA_SCALE = 192 ** -0.5
STRICT = True
INV_DT = F32
N_CORES = 8
TT = 512
SCRATCH_EXTERNAL = False
STORE_Q = "sp"


class Res:
    def __init__(self, acc=False):
        self.w = None
        self.r = {}
        self.acc = acc


class Buf(Res):
    def __init__(self, t):
        super().__init__()
        self.t = t

    def __getitem__(self, key):
        return self.t[key]


class K:
    def __init__(self, nc):
        self.nc = nc
        self.es = ExitStack()
        self.eng = {"pe": nc.tensor, "act": nc.scalar, "dve": nc.vector, "pool": nc.gpsimd, "sp": nc.sync}
        self.semh = {}
        self.cnt = {}
        for e in self.eng:
            self.semh[e] = self.es.enter_context(nc.semaphore("se_" + e))
            self.cnt[e] = 0
        self.seen = {e: {} for e in self.eng}
        self.dtot = {}
        self.nops = 0
        self.uid = 0

    @contextmanager
    def phase(self):
        st = ExitStack()
        self._ph = st
        try:
            yield st
        finally:
            self.barrier()
            st.close()

    def sb(self, shape, dt, name=None, glob=False):
        self.uid += 1
        t = (self.es if glob else self._ph).enter_context(
            self.nc.sbuf_tensor(f"{name or 'sb'}_{self.uid}", list(shape), dt))
        return Buf(t)

    def ring(self, n, shape, dt, name=None):
        return [self.sb(shape, dt, name) for _ in range(n)]

    def _dsem(self, key):
        if key not in self.semh:
            self.semh[key] = self.es.enter_context(self.nc.semaphore("sd_" + key))
            self.dtot[key] = 0
        return self.semh[key]

    def _deps(self, E, reads, writes, skip=None):
        evs = {}

        def add(d):
            for k, v in d.items():
                if evs.get(k, 0) < v:
                    evs[k] = v
        for r in reads:
            if r.w:
                add(r.w)
        for r in writes:
            if r.w:
                add(r.w)
            add(r.r)
        if E == "pe" or (not STRICT and E in ("act", "dve")):
            evs.pop(E, None)
        if skip:
            evs.pop(skip, None)
        return evs

    def _wait(self, E, evs):
        for key, val in evs.items():
            if key in self.dtot:
                val = self.dtot[key]
            if self.seen[E].get(key, 0) < val:
                self.eng[E].wait_ge(self.semh[key], val)
                self.seen[E][key] = val

    def op(self, E, fn, reads=(), writes=()):
        self._wait(E, self._deps(E, reads, writes))
        ins = fn(self.eng[E])
        self.cnt[E] += 1
        c = self.cnt[E]
        ins.then_inc(self.semh[E], 1)
        self.nops += 1
        for r in reads:
            if r.r.get(E, 0) < c:
                r.r[E] = c
        for r in writes:
            r.w = {E: c}
            r.r = {}

    def dma(self, Q, out, in_, reads, writes, key, **kw):
        if Q == "pool" and out.dtype == in_.dtype:
            Q = STORE_Q
        self._dsem(key)
        self._wait(Q, self._deps(Q, reads, writes, skip=key))
        ins = self.eng[Q].dma_start(out=out, in_=in_, **kw)
        self.dtot[key] += 16
        v = self.dtot[key]
        ins.then_inc(self.semh[key], 16)
        self.nops += 1
        for r in reads:
            r.r[key] = v
        for r in writes:
            if getattr(r, "acc", False):
                r.w = dict(r.w or {})
                r.w[key] = v
            else:
                r.w = {key: v}
                r.r = {}

    def barrier(self):
        tot = {e: self.cnt[e] for e in self.eng}
        tot.update(self.dtot)
        for E in self.eng:
            ev = dict(tot)
            ev.pop(E, None)
            self._wait(E, ev)


def kernel_build(seqs, want_debug=False):
    NT = sum(seqs)
    LMAX = max(seqs)
    nc = bass.Bass("TRN2", target_bir_lowering=False)
    k = K(nc)
    P = 128

    def din(name, shape, dt=F32):
        return nc.dram_tensor(name, list(shape), dt, kind="ExternalInput").ap()

    def dscr(name, shape, dt):
        ext = want_debug or SCRATCH_EXTERNAL
        return nc.dram_tensor(name, list(shape), dt, kind="ExternalOutput" if ext else "Internal").ap()

    xin = din("xin", [NT, D])
    yout = nc.dram_tensor("yout", [NT, D], F32, kind="ExternalOutput").ap()
    W = {}
    wshapes = {
        "ffn1_norm": [DEPTH, D], "w_ffn1_in": [DEPTH, D, 2 * FH], "w_ffn1_out": [DEPTH, FH, D],
        "mix_norm": [DEPTH, D], "w_in": [DEPTH, D, INC], "gdn_conv": [DEPTH, 5, 1536],
        "gdn_A_log": [DEPTH, 8], "gdn_dt_bias": [DEPTH, 8], "gdn_norm": [DEPTH, 128],
        "ssd_conv": [DEPTH, 5, 1024], "ssd_conv_b": [DEPTH, 1024], "ssd_A_log": [DEPTH, 16],
        "ssd_dt_bias": [DEPTH, 16], "ssd_D": [DEPTH, 8], "ssd_norm": [DEPTH, 512],
        "mla_q_norm": [DEPTH, 256], "mla_w_uq": [DEPTH, 256, 768], "mla_kv_norm": [DEPTH, 256],
        "mla_w_ukv": [DEPTH, 256, 1024], "w_branch_a": [DEPTH, 512, D], "w_branch_b": [DEPTH, 512, D],
        "w_branch_c": [DEPTH, 512, D], "w_gate": [DEPTH, D, 3 * D], "b_gate": [DEPTH, 3 * D],
        "w_out": [DEPTH, D, D], "ffn2_norm": [DEPTH, D], "w_ffn2_in": [DEPTH, D, 2 * FH],
        "w_ffn2_out": [DEPTH, FH, D], "final_norm": [1, D],
    }
    for n, s in wshapes.items():
        W[n] = din(n, s)
    cmask = din("cmask", [128, 7, 128])
    ropecs = din("ropecs", [2, 64, LMAX])

    hA = dscr("hA", [NT, D], F32)
    hB = dscr("hB", [NT, D], F32)
    featT = dscr("featT", [3712, NT], BF16)
    zs_tm = dscr("zs_tm", [NT, 512], F32)
    sm_tm = dscr("sm_tm", [NT, 32], F32)
    of_tm = dscr("of_tm", [NT, 512], F32)
    yf_tm = dscr("yf_tm", [NT, 512], F32)
    brT = [dscr(f"brT{i}", [512, NT], BF16) for i in range(3)]
    qT_d = dscr("qT_d", [4, 128, NT], BF16)
    qrT_d = dscr("qrT_d", [4, 64, NT], BF16)
    kT_d = dscr("kT_d", [4, 128, NT], BF16)
    krT_d = dscr("krT_d", [64, NT], BF16)
    v_d = dscr("v_d", [NT, 512], BF16)
    wffn_bf = dscr("wffn_bf", [DEPTH * 2, NJ, 128, 8 * 256], BF16)

    cm = k.sb([128, 7, 128], F32, "cm", glob=True)
    identb = k.sb([128, 128], BF16, "identb", glob=True)
    onesf = k.sb([128, 128], F32, "onesf", glob=True)
    nonesf = k.sb([128, 128], F32, "nonesf", glob=True)
    onesb = k.sb([128, 128], BF16, "onesb", glob=True)
    cst = k.sb([128, 4], F32, "cst", glob=True)
    banks = []
    for i in range(8):
        banks.append(Buf(k.es.enter_context(nc.psum_tensor(f"psb{i}", [128, 512], F32))))
    pstate = {"i": 0, "set": list(range(8))}

    def pget():
        s = pstate["set"]
        b = banks[s[pstate["i"] % len(s)]]
        pstate["i"] += 1
        return b

    def pset(lst):
        pstate["set"] = list(lst)
        pstate["i"] = 0

    k.dma("sp", cm[:], cmask, [], [cm], "cm")
    k.op("dve", lambda e: e.memset(onesf[:], 1.0), [], [onesf])
    k.op("dve", lambda e: e.memset(nonesf[:], -1.0), [], [nonesf])
    k.op("dve", lambda e: e.memset(onesb[:], 1.0), [], [onesb])
    k.op("dve", lambda e: e.memset(cst[:, 0:1], EPS), [], [cst])
    k.op("dve", lambda e: e.memset(cst[:, 1:2], 1.0), [], [cst])
    k.op("dve", lambda e: e.memset(cst[:, 2:3], 0.0), [], [cst])
    k.op("dve", lambda e: e.tensor_copy(out=identb[:], in_=cm[:, 0, :]), [cm], [identb])
    ident_f = cm[:, 0, :]

    def bc3(ap2, n):
        return ap2.unsqueeze(2).to_broadcast([ap2.shape[0], ap2.shape[1], n])

    def bcm(ap2, n):
        return ap2.unsqueeze(1).to_broadcast([ap2.shape[0], n, ap2.shape[1]])

    def rsqrt(out_buf, out_ap, in_buf, in_ap, scale=1.0):
        k.op("dve", lambda e: e.tensor_scalar(out=out_ap, in0=in_ap, scalar1=scale, scalar2=EPS, op0=ALU.mult, op1=ALU.add),
             [in_buf], [out_buf])
        k.op("act", lambda e: e.activation(out=out_ap, in_=out_ap, func=AF.Ln), [out_buf], [out_buf])
        k.op("act", lambda e: e.activation(out=out_ap, in_=out_ap, func=AF.Exp, scale=-0.5), [out_buf], [out_buf])

    wst = [k.sb([128, 1024], F32, "wst", glob=True) for _ in range(3)]
    wctr = [0]

    def load_w(dst_buf, dst_ap, src_ap, scale=None, view=None):
        i = wctr[0]
        wctr[0] += 1
        stg_ = wst[i % 3]
        if view is None:
            np_, n = src_ap.shape
            sv = stg_[0:np_, 0:n]
        else:
            sv = view(stg_)
        k.dma("sp", sv, src_ap, [], [stg_], f"wst{i % 3}")
        if scale is not None:
            k.op("dve", lambda e: e.tensor_scalar(out=dst_ap, in0=sv, scalar1=scale, scalar2=None, op0=ALU.mult), [stg_], [dst_buf])
        elif i % 3 == 0:
            k.op("act", lambda e: e.activation(out=dst_ap, in_=sv, func=AF.Copy), [stg_], [dst_buf])
        elif i % 3 == 1:
            k.op("dve", lambda e: e.tensor_copy(out=dst_ap, in_=sv), [stg_], [dst_buf])
        else:
            k.op("pool", lambda e: e.tensor_copy(out=dst_ap, in_=sv), [stg_], [dst_buf])

    def load_w_wide(dst_buf, dst2d, src2d):
        n = src2d.shape[1]
        for c0 in range(0, n, 1024):
            c1 = min(c0 + 1024, n)
            load_w(dst_buf, dst2d[:, c0:c1], src2d[:, c0:c1])

    with k.phase():
        stg = k.ring(3, [128, 8, 256], BF16, "wstg")
        n = 0
        for l in range(DEPTH):
            for f, wn in enumerate(("w_ffn1_in", "w_ffn2_in")):
                for j in range(NJ):
                    b = stg[n % 3]
                    n += 1
                    for half in range(2):
                        src = W[wn][l, :, half * FH + j * 128: half * FH + (j + 1) * 128].rearrange("(kk p) n -> p kk n", p=128)
                        load_w(b, b[:, :, half * 128:(half + 1) * 128], src, view=lambda t: t[:, 0:1024].rearrange("p (kk n) -> p kk n", kk=8))
                    k.dma("sp", wffn_bf[l * 2 + f, j].rearrange("p (kk n) -> p kk n", kk=8), b[:], [b], [], f"wstg_o{(n - 1) % 3}")

    def load_h(src, t0, hb, key):
        k.dma("sp", hb[:], src[t0:t0 + TT, :].rearrange("(s p) d -> p s d", p=128), [], [hb], key)

    def norm_T(hb, gain, uT, xn_ring, ss, rs, ctr):
        for s in range(4):
            xn = xn_ring[ctr[0] % len(xn_ring)]
            ctr[0] += 1
            k.op("dve", lambda e: e.scalar_tensor_tensor(out=xn[:], in0=hb[:, s, :], scalar=1.0, in1=hb[:, s, :],
                                                         op0=ALU.mult, op1=ALU.mult, accum_out=ss[:, s:s + 1]),
                 [hb], [xn, ss])
        rsqrt(rs, rs[:, 0:4], ss, ss[:, 0:4], scale=1.0 / D)
        xns = []
        for s in range(4):
            xn = xn_ring[ctr[0] % len(xn_ring)]
            ctr[0] += 1
            k.op("dve", lambda e: e.tensor_scalar(out=xn[:], in0=hb[:, s, :], scalar1=rs[:, s:s + 1], scalar2=None, op0=ALU.mult),
                 [hb, rs], [xn])
            xns.append(xn)
        for kk in range(8):
            pb = pget()
            pv = pb[:].bitcast(BF16)
            for s in range(4):
                k.op("pe", lambda e: e.transpose(out=pv[:, s * 128:(s + 1) * 128], in_=xns[s][:, kk * 128:(kk + 1) * 128], identity=identb[:]),
                     [xns[s], identb], [pb])
            k.op("dve" if kk % 2 else "act",
                 (lambda e: e.tensor_scalar(out=uT[:, kk, :], in0=pv[:, 0:512], scalar1=gain[:, kk:kk + 1], scalar2=None, op0=ALU.mult)) if kk % 2 else
                 (lambda e: e.activation(out=uT[:, kk, :], in_=pv[:, 0:512], func=AF.Copy, scale=gain[:, kk:kk + 1])),
                 [pb, gain], [uT])

    def load_gain(dst, src_row, nk):
        k.dma("sp", dst[:, 0:nk], src_row.rearrange("(kk p) -> p kk", p=128), [], [dst], "gain", allow_slow_non_contiguous=True)

    def ffn_phase(l, f, src, dst, final):
        wn_out = "w_ffn1_out" if f == 0 else "w_ffn2_out"
        gn = "ffn1_norm" if f == 0 else "ffn2_norm"
        with k.phase():
            pset(range(8))
            wout = k.sb([128, NJ, D], BF16, "wout")
            for j in range(NJ):
                load_w(wout, wout[:, j, :], W[wn_out][l, j * 128:(j + 1) * 128, :])
            gain = k.sb([128, 8], F32, "gain")
            load_gain(gain, W[gn][l], 8)
            if final:
                fg = k.sb([128, D], F32, "fg")
                k.dma("sp", fg[:], W["final_norm"][0].partition_broadcast(128), [], [fg], "fg")
            hbs = k.ring(2, [128, 4, D], F32, "hb")
            xnr = k.ring(4, [128, D], BF16, "xn")
            uT = k.sb([128, 8, TT], BF16, "uT")
            hid = k.sb([128, NJ, TT], BF16, "hid")
            sgs = k.ring(2, [128, TT], F32, "sg")
            slabs = k.ring(3, [128, 8, 256], BF16, "slab")
            ss = k.sb([128, 8], F32, "ss")
            rs = k.sb([128, 8], F32, "rs")
            ctr = [0]
            nsl = 0
            for ti in range(NT // TT):
                t0 = ti * TT
                hb = hbs[ti % 2]
                load_h(src, t0, hb, f"hld{ti % 2}")
                norm_T(hb, gain, uT, xnr, ss, rs, ctr)
                for j in range(NJ):
                    sl = slabs[nsl % 3]
                    k.dma("sp", sl[:], wffn_bf[l * 2 + f, j].rearrange("p (kk n) -> p kk n", kk=8), [], [sl], f"slab{nsl % 3}")
                    nsl += 1
                    pg = pget()
                    pu = pget()
                    for kk in range(8):
                        k.op("pe", lambda e: e.matmul(pg[:], lhsT=sl[:, kk, 0:128], rhs=uT[:, kk, :], start=(kk == 0), stop=(kk == 7)),
                             [sl, uT], [pg])
                    for kk in range(8):
                        k.op("pe", lambda e: e.matmul(pu[:], lhsT=sl[:, kk, 128:256], rhs=uT[:, kk, :], start=(kk == 0), stop=(kk == 7)),
                             [sl, uT], [pu])
                    sg = sgs[j % 2]
                    k.op("act", lambda e: e.activation(out=sg[:], in_=pg[:], func=AF.Silu), [pg], [sg])
                    k.op("dve", lambda e: e.tensor_tensor(out=hid[:, j, :], in0=sg[:], in1=pu[:], op=ALU.mult), [sg, pu], [hid])
                for s in range(4):
                    for fh in range(2):
                        po = pget()
                        for j in range(NJ):
                            k.op("pe", lambda e: e.matmul(po[:], lhsT=hid[:, j, s * 128:(s + 1) * 128], rhs=wout[:, j, fh * 512:(fh + 1) * 512],
                                                          start=(j == 0), stop=(j == NJ - 1)), [hid, wout], [po])
                        k.op("dve", lambda e: e.scalar_tensor_tensor(out=hb[:, s, fh * 512:(fh + 1) * 512], in0=po[:], scalar=0.5,
                                                                     in1=hb[:, s, fh * 512:(fh + 1) * 512], op0=ALU.mult, op1=ALU.add),
                             [po, hb], [hb])
                if final:
                    for s in range(4):
                        xn = xnr[ctr[0] % 4]
                        ctr[0] += 1
                        k.op("dve", lambda e: e.scalar_tensor_tensor(out=sgs[0][:], in0=hb[:, s, 0:512], scalar=1.0, in1=hb[:, s, 0:512],
                                                                     op0=ALU.mult, op1=ALU.mult, accum_out=ss[:, s:s + 1]), [hb], [sgs[0], ss])
                        k.op("dve", lambda e: e.scalar_tensor_tensor(out=sgs[0][:], in0=hb[:, s, 512:1024], scalar=1.0, in1=hb[:, s, 512:1024],
                                                                     op0=ALU.mult, op1=ALU.mult, accum_out=ss[:, 4 + s:5 + s]), [hb], [sgs[0], ss])
                    k.op("dve", lambda e: e.tensor_tensor(out=ss[:, 0:4], in0=ss[:, 0:4], in1=ss[:, 4:8], op=ALU.add), [ss], [ss])
                    rsqrt(rs, rs[:, 0:4], ss, ss[:, 0:4], scale=1.0 / D)
                    for s in range(4):
                        k.op("dve", lambda e: e.scalar_tensor_tensor(out=hb[:, s, :], in0=hb[:, s, :], scalar=rs[:, s:s + 1], in1=fg[:],
                                                                     op0=ALU.mult, op1=ALU.mult), [hb, rs, fg], [hb])
                k.dma("pool", dst[t0:t0 + TT, :].rearrange("(s p) d -> p s d", p=128), hb[:], [hb], [], f"hst{ti % 2}")

    def m1_phase(l, src):
        with k.phase():
            pset(range(8))
            win = k.sb([128, 8, INC + 64], BF16, "win")
            for kk in range(8):
                load_w_wide(win, win[:, kk, 0:INC], W["w_in"][l, kk * 128:(kk + 1) * 128, :])
                load_w(win, win[:, kk, INC:INC + 32], W["w_in"][l, kk * 128:(kk + 1) * 128, 4160:4192], scale=-1.0)
                load_w(win, win[:, kk, INC + 32:INC + 64], W["w_in"][l, kk * 128:(kk + 1) * 128, 4128:4160])
            wsm = k.sb([128, 8, 32], BF16, "wsm")
            for kk in range(8):
                load_w(wsm, wsm[:, kk, 0:16], W["w_in"][l, kk * 128:(kk + 1) * 128, 2048:2064])
                load_w(wsm, wsm[:, kk, 16:32], W["w_in"][l, kk * 128:(kk + 1) * 128, 3600:3616])
            gain = k.sb([128, 8], F32, "gain")
            load_gain(gain, W["mix_norm"][l], 8)
            hbs = k.ring(2, [128, 4, D], F32, "hb")
            xnr = k.ring(4, [128, D], BF16, "xn")
            uT = k.sb([128, 8, TT], BF16, "uT")
            ss = k.sb([128, 8], F32, "ss")
            rs = k.sb([128, 8], F32, "rs")
            stg = k.ring(4, [128, TT], BF16, "stg")
            zst = k.ring(2, [128, 4, 512], F32, "zst")
            smt = k.ring(2, [128, 4, 32], F32, "smt")
            ctr = [0]
            groups = [(g, g * 128) for g in range(16)] + [(16 + g, 2576 + g * 128) for g in range(8)] + \
                     [(24 + g, 3616 + g * 128) for g in range(4)] + [(28, 4128)]
            ns = 0
            for ti in range(NT // TT):
                t0 = ti * TT
                hb = hbs[ti % 2]
                load_h(src, t0, hb, f"hld{ti % 2}")
                norm_T(hb, gain, uT, xnr, ss, rs, ctr)
                for (rg, co) in groups:
                    pb = pget()
                    for kk in range(8):
                        k.op("pe", lambda e: e.matmul(pb[:], lhsT=win[:, kk, co:co + 128], rhs=uT[:, kk, :], start=(kk == 0), stop=(kk == 7)),
                             [win, uT], [pb])
                    st = stg[ns % 4]
                    if ns % 2:
                        k.op("dve", lambda e: e.tensor_copy(out=st[:], in_=pb[:]), [pb], [st])
                    else:
                        k.op("act", lambda e: e.activation(out=st[:], in_=pb[:], func=AF.Copy), [pb], [st])
                    k.dma("pool", featT[rg * 128:(rg + 1) * 128, t0:t0 + TT], st[:], [st], [], f"stg{ns % 4}")
                    ns += 1
                z = zst[ti % 2]
                sm = smt[ti % 2]
                for s in range(4):
                    pb = pget()
                    for kk in range(8):
                        k.op("pe", lambda e: e.matmul(pb[:], lhsT=uT[:, kk, s * 128:(s + 1) * 128], rhs=win[:, kk, 2064:2576], start=(kk == 0), stop=(kk == 7)),
                             [win, uT], [pb])
                    k.op("act", lambda e: e.activation(out=z[:, s, :], in_=pb[:], func=AF.Copy), [pb], [z])
                    pb = pget()
                    for kk in range(8):
                        k.op("pe", lambda e: e.matmul(pb[:, 0:32], lhsT=uT[:, kk, s * 128:(s + 1) * 128], rhs=wsm[:, kk, :], start=(kk == 0), stop=(kk == 7)),
                             [wsm, uT], [pb])
                    k.op("dve", lambda e: e.tensor_copy(out=sm[:, s, :], in_=pb[:, 0:32]), [pb], [sm])
                k.dma("pool", zs_tm[t0:t0 + TT, :].rearrange("(s p) d -> p s d", p=128), z[:], [z], [], f"zst{ti % 2}")
                k.dma("pool", sm_tm[t0:t0 + TT, :].rearrange("(s p) d -> p s d", p=128), sm[:], [sm], [], f"smt{ti % 2}")

    def scan_phase(l):
        with k.phase():
            cw_g = k.sb([128, 12, 5], F32, "cw_g")
            for j in range(5):
                k.dma("sp", cw_g[:, :, j], W["gdn_conv"][l, j].rearrange("(g p) -> p g", p=128), [], [cw_g], "cst1", allow_slow_non_contiguous=True)
            cw_s = k.sb([128, 8, 5], F32, "cw_s")
            for j in range(5):
                k.dma("sp", cw_s[:, :, j], W["ssd_conv"][l, j].rearrange("(g p) -> p g", p=128), [], [cw_s], "cst1", allow_slow_non_contiguous=True)
            dg_g = k.sb([128, 12, 5, 128], BF16, "dg_g")
            dg_s = k.sb([128, 8, 5, 128], BF16, "dg_s")
            for g in range(12):
                for j in range(5):
                    k.op("dve", lambda e: e.tensor_scalar(out=dg_g[:, g, j, :], in0=ident_f, scalar1=cw_g[:, g, j:j + 1], scalar2=None, op0=ALU.mult),
                         [cm, cw_g], [dg_g])
            for g in range(8):
                for j in range(5):
                    k.op("dve", lambda e: e.tensor_scalar(out=dg_s[:, g, j, :], in0=ident_f, scalar1=cw_s[:, g, j:j + 1], scalar2=None, op0=ALU.mult),
                         [cm, cw_s], [dg_s])
            cb_row = k.sb([1, 1024], BF16, "cb_row")
            load_w(cb_row, cb_row[:], W["ssd_conv_b"][l:l + 1, :])
            cb_pp = k.sb([128, 8], F32, "cb_pp")
            load_gain(cb_pp, W["ssd_conv_b"][l], 8)
            gA = k.sb([128, 8], F32, "gA")
            gdtb = k.sb([128, 8], F32, "gdtb")
            sA = k.sb([128, 16], F32, "sA")
            sdtb = k.sb([128, 16], F32, "sdtb")
            sD = k.sb([128, 8], F32, "sD")
            gng = k.sb([128, 1], F32, "gng")
            sng = k.sb([128, 512], F32, "sng")
            k.dma("sp", gA[:], W["gdn_A_log"][l].partition_broadcast(128), [], [gA], "cst1")
            k.dma("sp", gdtb[:], W["gdn_dt_bias"][l].partition_broadcast(128), [], [gdtb], "cst1")
            k.dma("sp", sA[:], W["ssd_A_log"][l].partition_broadcast(128), [], [sA], "cst1")
            k.dma("sp", sdtb[:], W["ssd_dt_bias"][l].partition_broadcast(128), [], [sdtb], "cst1")
            k.dma("sp", sD[:], W["ssd_D"][l].partition_broadcast(128), [], [sD], "cst1")
            k.dma("sp", sng[:], W["ssd_norm"][l].partition_broadcast(128), [], [sng], "cst1")
            k.dma("sp", gng[:], W["gdn_norm"][l].rearrange("(p o) -> p o", o=1), [], [gng], "cst1")
            for t in (gA, sA):
                k.op("act", lambda e, t=t: e.activation(out=t[:], in_=t[:], func=AF.Exp), [t], [t])
                k.op("dve", lambda e, t=t: e.tensor_scalar(out=t[:], in0=t[:], scalar1=-1.0, scalar2=None, op0=ALU.mult), [t], [t])
            ones_row = k.sb([1, 128], BF16, "ones_row")
            k.op("dve", lambda e: e.memset(ones_row[:], 1.0), [], [ones_row])

            R = 2
            pre_g = k.ring(R, [128, 12, 132], BF16, "pre_g")
            pre_s = k.ring(R, [128, 8, 132], BF16, "pre_s")
            abt = k.ring(R, [128, 32], F32, "abt")
            qkv = k.ring(R, [128, 12, 128], F32, "qkv")
            vbf = k.ring(R, [128, 4, 128], BF16, "vbf")
            sq = k.sb([128, 8, 128], F32, "sq")
            ssn = k.sb([128, 8], F32, "ssn")
            rinv = k.sb([128, 8], F32, "rinv")
            qkn = k.ring(R, [128, 8, 128], BF16, "qkn")
            gt = k.sb([128, 16], F32, "gt")
            gg = k.ring(R, [128, 8], F32, "gg")
            beta = k.ring(R, [128, 8], F32, "beta")
            nbeta = k.ring(R, [128, 8], F32, "nbeta")
            gcs = k.ring(R, [128, 4], F32, "gcs")
            egc = k.ring(R, [128, 4], F32, "egc")
            edl = k.ring(R, [128, 4], F32, "edl")
            egl = k.ring(R, [128, 4], F32, "egl")
            GL = k.ring(1, [128, 4, 128], F32, "GL")
            Dm = k.ring(1, [128, 4, 128], F32, "Dm")
            GLs = k.ring(1, [128, 8, 128], F32, "GLs")
            Dms = k.ring(1, [128, 8, 128], F32, "Dms")
            ss2 = k.sb([128, 4], F32, "ss2")
            rs2 = k.sb([128, 4], F32, "rs2")
            gst = {"i": 0}
            sst = {"i": 0}

            def pgG():
                b = banks[gst["i"] % 4]
                gst["i"] += 1
                return b

            def pgS():
                b = banks[4 + sst["i"] % 4]
                sst["i"] += 1
                return b
            DmI = k.sb([128, 4, 128], F32, "DmI")
            kqT = k.ring(R, [128, 4, 2, 128], BF16, "kqT")
            qg = k.sb([128, 4, 128], BF16, "qg")
            qgT = k.ring(R, [128, 4, 128], BF16, "qgT")
            kg = k.ring(R, [128, 4, 128], BF16, "kg")
            kd = k.ring(R, [128, 4, 128], BF16, "kd")
            At = k.ring(R, [128, 4, 128], BF16, "At")
            t1 = k.sb([128, 4, 128], F32, "t1")
            Qs = k.ring(2, [128, 4, 128], INV_DT, "Qs")
            QTs = k.ring(2, [128, 4, 128], INV_DT, "QTs")
            IQT = k.ring(2, [128, 4, 128], INV_DT, "IQT")
            Ts = k.ring(2, [128, 4, 128], INV_DT, "Ts")
            Xbs = k.ring(R, [128, 4, 128], BF16, "Xb")
            nWT = k.ring(R, [128, 4, 128], BF16, "nWT")
            vnew = k.ring(R, [128, 4, 128], BF16, "vnew")
            S32 = [k.sb([128, 4, 128], F32, "S32") for _ in range(2)]
            Sbf = [k.sb([128, 4, 128], BF16, "Sbf") for _ in range(2)]
            osb = k.ring(R, [128, 4, 128], F32, "osb")
            ofl = k.ring(R, [128, 4, 128], F32, "ofl")
            on = k.sb([128, 4, 128], BF16, "on")
            zT = k.ring(R, [128, 4, 128], BF16, "zT")
            sz = k.sb([128, 4, 128], F32, "sz")
            goT = k.ring(R, [128, 4, 128], BF16, "goT")
            xB = k.ring(R, [128, 6, 128], F32, "xB")
            Bbf = k.ring(R, [128, 2, 128], BF16, "Bbf")
            BCT = k.ring(R, [128, 4, 128], BF16, "BCT")
            dts = k.ring(R, [128, 16], F32, "dts")
            dA = k.ring(R, [128, 16], F32, "dA")
            acs = k.ring(R, [128, 8], F32, "acs")
            eac = k.ring(R, [128, 8], F32, "eac")
            dst_ = k.ring(R, [128, 8], F32, "dst")
            ecd = k.ring(R, [128, 8], F32, "ecd")
            Mt = k.ring(R, [128, 8, 128], BF16, "Mt")
            xdt = k.ring(R, [128, 8, 64], BF16, "xdt")
            xdd = k.ring(R, [128, 8, 64], BF16, "xdd")
            dd = k.sb([128, 8], F32, "dd")
            H32 = [k.sb([128, 512], F32, "H32") for _ in range(2)]
            Hbf = [k.sb([128, 512], BF16, "Hbf") for _ in range(2)]
            ysb = k.ring(R, [128, 512], F32, "ysb")
            yfl = k.ring(R, [128, 512], F32, "yfl")
            ytmp = k.sb([128, 512], F32, "ytmp")
            zsl = k.ring(R, [128, 512], F32, "zsl")
            ynb = k.sb([128, 512], BF16, "ynb")
            soT = k.ring(R, [128, 4, 128], BF16, "soT")
            ss1 = k.sb([128, 4], F32, "ss1")
            rs1 = k.sb([128, 4], F32, "rs1")
            cnt = {"g": 0, "s": 0}
            r_of = Res(acc=True)
            r_yf = Res(acc=True)

            def load_pre(buf, row0, ngrp, seq0, L, c, key):
                t0 = c * 128
                lo = max(t0 - 2, 0)
                hi = min(t0 + 130, L)
                if lo != t0 - 2 or hi != t0 + 130:
                    k.op("pool", lambda e: e.memset(buf[:], 0.0), [], [buf])
                k.dma("sp", buf[:, :, lo - (t0 - 2): hi - (t0 - 2)],
                      featT[row0:row0 + ngrp * 128, seq0 + lo: seq0 + hi].rearrange("(g p) t -> p g t", p=128), [], [buf], key)

            def gdn_chunk(seq0, L, c, d):
                i = cnt["g"]
                cnt["g"] += 1
                r = i % R
                tg = seq0 + c * 128
                pg_, ab, qv, vb = pre_g[r], abt[r], qkv[r], vbf[r]
                load_pre(pg_, 0, 12, seq0, L, c, f"preg{r}")
                k.dma("sp", ab[:], sm_tm[tg:tg + 128, :], [], [ab], f"ab{r}")
                for part in range(3):
                    pb = pgG()
                    for h in range(4):
                        g = part * 4 + h
                        for j in range(5):
                            k.op("pe", lambda e: e.matmul(pb[:, h * 128:(h + 1) * 128], lhsT=pg_[:, g, j:j + 128], rhs=dg_g[:, g, j, :],
                                                          start=(j == 0), stop=(j == 4)), [pg_, dg_g], [pb])
                    k.op("act", lambda e: e.activation(out=qv[:, part * 4:(part + 1) * 4, :], in_=pb[:].rearrange("p (h d) -> p h d", h=4), func=AF.Silu),
                         [pb], [qv])
                k.op("pool", lambda e: e.tensor_copy(out=vb[:], in_=qv[:, 8:12, :]), [qv], [vb])
                yield
                k.op("dve", lambda e: e.tensor_tensor(out=sq[:], in0=qv[:, 0:8, :], in1=qv[:, 0:8, :], op=ALU.mult), [qv], [sq])
                k.op("dve", lambda e: e.tensor_reduce(out=ssn[:], in_=sq[:], axis=AX.X, op=ALU.add), [sq], [ssn])
                rsqrt(rinv, rinv[:], ssn, ssn[:])
                k.op("dve", lambda e: e.tensor_scalar(out=rinv[:, 0:4], in0=rinv[:, 0:4], scalar1=128 ** -0.5, scalar2=None, op0=ALU.mult), [rinv], [rinv])
                qn = qkn[r]
                k.op("dve", lambda e: e.tensor_tensor(out=qn[:], in0=qv[:, 0:8, :], in1=bc3(rinv[:], 128), op=ALU.mult), [qv, rinv], [qn])
                yield
                g_, be, nbe = gg[r], beta[r], nbeta[r]
                k.op("dve", lambda e: e.tensor_tensor(out=gt[:, 0:8], in0=ab[:, 0:8], in1=gdtb[:], op=ALU.add), [ab, gdtb], [gt])
                k.op("act", lambda e: e.activation(out=gt[:, 0:8], in_=gt[:, 0:8], func=AF.Exp), [gt], [gt])
                k.op("act", lambda e: e.activation(out=gt[:, 0:8], in_=gt[:, 0:8], func=AF.Ln, bias=cst[:, 1:2]), [gt, cst], [gt])
                k.op("dve", lambda e: e.tensor_tensor(out=g_[:], in0=gt[:, 0:8], in1=gA[:], op=ALU.mult), [gt, gA], [g_])
                k.op("act", lambda e: e.activation(out=gt[:, 8:16], in_=ab[:, 8:16], func=AF.Exp, scale=-1.0), [ab], [gt])
                k.op("dve", lambda e: e.tensor_scalar(out=gt[:, 8:16], in0=gt[:, 8:16], scalar1=1.0, scalar2=None, op0=ALU.add), [gt], [gt])
                k.op("dve", lambda e: e.reciprocal(out=be[:], in_=gt[:, 8:16]), [gt], [be])
                k.op("dve", lambda e: e.tensor_scalar(out=nbe[:], in0=be[:], scalar1=-1.0, scalar2=None, op0=ALU.mult), [be], [nbe])
                gd = g_[:, d * 4:(d + 1) * 4]
                tri = cm[:, 1 + d, :]
                yield
                pb = pgG()
                k.op("pe", lambda e: e.matmul(pb[:, 0:4], lhsT=tri, rhs=gd, start=True, stop=True), [cm, g_], [pb])
                k.op("pe", lambda e: e.matmul(pb[:, 4:8], lhsT=onesf[:], rhs=gd, start=True, stop=True), [onesf, g_], [pb])
                gc_, eg, ed, el = gcs[r], egc[r], edl[r], egl[r]
                k.op("act", lambda e: e.activation(out=gc_[:], in_=pb[:, 0:4], func=AF.Copy), [pb], [gc_])
                k.op("act", lambda e: e.activation(out=eg[:], in_=pb[:, 0:4], func=AF.Exp), [pb], [eg])
                k.op("act", lambda e: e.activation(out=el[:], in_=pb[:, 4:8], func=AF.Exp), [pb], [el])
                k.op("dve", lambda e: e.tensor_tensor(out=ed[:], in0=pb[:, 4:8], in1=gc_[:], op=ALU.subtract), [pb, gc_], [ed])
                k.op("act", lambda e: e.activation(out=ed[:], in_=ed[:], func=AF.Exp), [ed], [ed])
                yield
                gl, dm = GL[0], Dm[0]
                k.op("dve", lambda e: e.tensor_tensor(out=gl[:], in0=bcm(tri, 4), in1=bc3(gd, 128), op=ALU.mult), [cm, g_], [gl])
                pb = pgG()
                k.op("pe", lambda e: e.matmul(pb[:], lhsT=onesf[:], rhs=gl[:].rearrange("p h i -> p (h i)"), start=True, stop=False), [onesf, gl], [pb])
                for h in range(4):
                    k.op("pe", lambda e: e.matmul(pb[:, h * 128:(h + 1) * 128], lhsT=gl[:, h, :], rhs=nonesf[:], start=False, stop=True), [gl, nonesf], [pb])
                k.op("dve", lambda e: e.tensor_tensor(out=dm[:], in0=pb[:].rearrange("p (h i) -> p h i", h=4), in1=bcm(cm[:, 3 + d, :], 4), op=ALU.add),
                     [pb, cm], [dm])
                k.op("act", lambda e: e.activation(out=dm[:], in_=dm[:], func=AF.Exp), [dm], [dm])
                k.op("pool", lambda e: e.tensor_tensor(out=DmI[:], in0=dm[:], in1=bcm(ident_f, 4), op=ALU.add), [dm, cm], [DmI])
                yield
                kg_, kd_ = kg[r], kd[r]
                k.op("pool", lambda e: e.tensor_tensor(out=qg[:], in0=qn[:, 0:4, :], in1=bc3(eg[:], 128), op=ALU.mult), [qn, eg], [qg])
                k.op("pool", lambda e: e.tensor_tensor(out=kg_[:], in0=qn[:, 4:8, :], in1=bc3(eg[:], 128), op=ALU.mult), [qn, eg], [kg_])
                k.op("pool", lambda e: e.tensor_tensor(out=kd_[:], in0=qn[:, 4:8, :], in1=bc3(ed[:], 128), op=ALU.mult), [qn, ed], [kd_])
                yield
                kq, qgt = kqT[r], qgT[r]
                pb = pgG()
                pv = pb[:].bitcast(BF16)
                for h in range(4):
                    k.op("pe", lambda e: e.transpose(out=pv[:, (2 * h) * 128:(2 * h + 1) * 128], in_=qn[:, 4 + h, :], identity=identb[:]), [qn, identb], [pb])
                    k.op("pe", lambda e: e.transpose(out=pv[:, (2 * h + 1) * 128:(2 * h + 2) * 128], in_=qn[:, h, :], identity=identb[:]), [qn, identb], [pb])
                k.op("act", lambda e: e.activation(out=kq[:].rearrange("p h t i -> p (h t i)"), in_=pv[:, 0:1024], func=AF.Copy), [pb], [kq])
                pb = pgG()
                pv = pb[:].bitcast(BF16)
                for h in range(4):
                    k.op("pe", lambda e: e.transpose(out=pv[:, h * 128:(h + 1) * 128], in_=qg[:, h, :], identity=identb[:]), [qg, identb], [pb])
                k.op("dve", lambda e: e.tensor_copy(out=qgt[:].rearrange("p h i -> p (h i)"), in_=pv[:, 0:512]), [pb], [qgt])
                yield
                pk = [pgG(), pgG()]
                for h in range(4):
                    b = pk[h // 2]
                    k.op("pe", lambda e: e.matmul(b[:, (h % 2) * 256:(h % 2 + 1) * 256], lhsT=kq[:, h, 0, :], rhs=kq[:, h, :, :].rearrange("p t i -> p (t i)"),
                                                  start=True, stop=True), [kq], [b])
                at = At[r]
                q0, q0t = Qs[0], QTs[0]
                for hh in range(2):
                    b = pk[hh]
                    b4 = b[:].rearrange("p (h t i) -> p h t i", h=2, t=2)
                    k.op("dve", lambda e: e.tensor_tensor(out=at[:, 2 * hh:2 * hh + 2, :], in0=b4[:, :, 1, :], in1=DmI[:, 2 * hh:2 * hh + 2, :], op=ALU.mult),
                         [b, DmI], [at])
                    k.op("dve", lambda e: e.tensor_tensor(out=t1[:, 2 * hh:2 * hh + 2, :], in0=b4[:, :, 0, :], in1=dm[:, 2 * hh:2 * hh + 2, :], op=ALU.mult),
                         [b, dm], [t1])
                nb_d = nbe[:, d * 4:(d + 1) * 4]
                k.op("dve", lambda e: e.tensor_tensor(out=q0[:], in0=t1[:], in1=bc3(nb_d, 128), op=ALU.mult), [t1, nbe], [q0])
                yield
                ident_inv = identb[:] if INV_DT == BF16 else ident_f
                pb = pgG()
                if INV_DT == BF16:
                    pv = pb[:].bitcast(BF16)
                else:
                    pv = pb[:]
                for h in range(4):
                    k.op("pe", lambda e: e.transpose(out=pv[:, h * 128:(h + 1) * 128], in_=q0[:, h, :], identity=ident_inv), [q0, identb, cm], [pb])
                k.op("act", lambda e: e.activation(out=q0t[:].rearrange("p h i -> p (h i)"), in_=pv[:, 0:512], func=AF.Copy), [pb], [q0t])
                T = Ts[0]
                k.op("pool", lambda e: e.tensor_tensor(out=T[:], in0=q0[:], in1=bcm(ident_f, 4), op=ALU.add), [q0, cm], [T])
                Qp, QTp = q0, q0t
                for lev in range(1, 7):
                    yield
                    Qn, QTn, iq, Tn = Qs[lev % 2], QTs[lev % 2], IQT[lev % 2], Ts[lev % 2]
                    pbt = pgG()
                    for h in range(4):
                        k.op("pe", lambda e: e.matmul(pbt[:, h * 128:(h + 1) * 128], lhsT=Qp[:, h, :], rhs=QTp[:, h, :], start=True, stop=True), [Qp, QTp], [pbt])
                    if lev < 6:
                        pbq = pgG()
                        for h in range(4):
                            k.op("pe", lambda e: e.matmul(pbq[:, h * 128:(h + 1) * 128], lhsT=QTp[:, h, :], rhs=Qp[:, h, :], start=True, stop=True), [Qp, QTp], [pbq])
                        k.op("act", lambda e: e.activation(out=Qn[:].rearrange("p h i -> p (h i)"), in_=pbq[:], func=AF.Copy), [pbq], [Qn])
                        k.op("act", lambda e: e.activation(out=QTn[:].rearrange("p h i -> p (h i)"), in_=pbt[:], func=AF.Copy), [pbt], [QTn])
                    k.op("dve", lambda e: e.tensor_tensor(out=iq[:], in0=pbt[:].rearrange("p (h i) -> p h i", h=4), in1=bcm(ident_f, 4), op=ALU.add),
                         [pbt, cm], [iq])
                    yield
                    pbT = pgG()
                    for h in range(4):
                        k.op("pe", lambda e: e.matmul(pbT[:, h * 128:(h + 1) * 128], lhsT=iq[:, h, :], rhs=T[:, h, :], start=True, stop=True), [iq, T], [pbT])
                    k.op("dve", lambda e: e.tensor_copy(out=Tn[:].rearrange("p h i -> p (h i)"), in_=pbT[:]), [pbT], [Tn])
                    T, Qp, QTp = Tn, Qn, QTn
                yield
                X = Xbs[r]
                k.op("pool", lambda e: e.tensor_copy(out=X[:], in_=T[:]), [T], [X])
                nw = nWT[r]
                pb = pgG()
                for h in range(4):
                    k.op("pe", lambda e: e.matmul(pb[:, h * 128:(h + 1) * 128], lhsT=kg_[:, h, :], rhs=X[:, h, :], start=True, stop=True), [kg_, X], [pb])
                k.op("act", lambda e: e.activation(out=nw[:].rearrange("p h i -> p (h i)"), in_=pb[:], func=AF.Copy, scale=-1.0), [pb], [nw])
                yield
                S3, Sb = S32[d], Sbf[d]
                first = (c == 0) if d == 0 else (c == L // 128 - 1)
                if first:
                    k.op("pool", lambda e: e.memset(S3[:], 0.0), [], [S3])
                    k.op("pool", lambda e: e.memset(Sb[:], 0.0), [], [Sb])
                pb = pgG()
                for h in range(4):
                    k.op("pe", lambda e: e.matmul(pb[:, h * 128:(h + 1) * 128], lhsT=X[:, h, :], rhs=vb[:, h, :], start=True, stop=False), [X, vb], [pb])
                    k.op("pe", lambda e: e.matmul(pb[:, h * 128:(h + 1) * 128], lhsT=nw[:, h, :], rhs=Sb[:, h, :], start=False, stop=True), [nw, Sb], [pb])
                vn = vnew[r]
                k.op("dve", lambda e: e.tensor_tensor(out=vn[:], in0=pb[:].rearrange("p (h i) -> p h i", h=4), in1=bc3(be[:, d * 4:(d + 1) * 4], 128), op=ALU.mult),
                     [pb, be], [vn])
                yield
                po = pgG()
                for h in range(4):
                    k.op("pe", lambda e: e.matmul(po[:, h * 128:(h + 1) * 128], lhsT=qgt[:, h, :], rhs=Sb[:, h, :], start=True, stop=False), [qgt, Sb], [po])
                    k.op("pe", lambda e: e.matmul(po[:, h * 128:(h + 1) * 128], lhsT=at[:, h, :], rhs=vn[:, h, :], start=False, stop=True), [at, vn], [po])
                pS = pgG()
                for h in range(4):
                    k.op("pe", lambda e: e.matmul(pS[:, h * 128:(h + 1) * 128], lhsT=kd_[:, h, :], rhs=vn[:, h, :], start=True, stop=True), [kd_, vn], [pS])
                k.op("dve", lambda e: e.tensor_tensor(out=S3[:], in0=S3[:], in1=bc3(el[:], 128), op=ALU.mult), [S3, el], [S3])
                k.op("dve", lambda e: e.tensor_tensor(out=S3[:], in0=S3[:], in1=pS[:].rearrange("p (h i) -> p h i", h=4), op=ALU.add), [S3, pS], [S3])
                k.op("act", lambda e: e.activation(out=Sb[:], in_=S3[:], func=AF.Copy), [S3], [Sb])
                yield
                o = osb[r]
                if d == 0:
                    k.op("act", lambda e: e.activation(out=o[:].rearrange("p h i -> p (h i)"), in_=po[:], func=AF.Copy), [po], [o])
                    k.dma("pool", of_tm[tg:tg + 128, :], o[:].rearrange("p h i -> p (h i)"), [o], [r_of], f"ofst{r}")
                    return
                ofb = ofl[r]
                k.dma("sp", ofb[:].rearrange("p h i -> p (h i)"), of_tm[tg:tg + 128, :], [r_of], [ofb], f"ofld{r}")
                k.op("dve", lambda e: e.tensor_tensor(out=o[:], in0=ofb[:], in1=po[:].rearrange("p (h i) -> p h i", h=4), op=ALU.add), [ofb, po], [o])
                yield
                k.op("dve", lambda e: e.tensor_tensor(out=sq[:, 0:4, :], in0=o[:], in1=o[:], op=ALU.mult), [o], [sq])
                k.op("dve", lambda e: e.tensor_reduce(out=ss1[:], in_=sq[:, 0:4, :], axis=AX.X, op=ALU.add), [sq], [ss1])
                rsqrt(rs1, rs1[:], ss1, ss1[:], scale=1.0 / 128)
                k.op("dve", lambda e: e.tensor_tensor(out=on[:], in0=o[:], in1=bc3(rs1[:], 128), op=ALU.mult), [o, rs1], [on])
                z_ = zT[r]
                k.dma("sp", z_[:], featT[1536:2048, tg:tg + 128].rearrange("(h p) t -> p h t", p=128), [], [z_], f"zT{r}")
                k.op("act", lambda e: e.activation(out=sz[:], in_=z_[:], func=AF.Silu), [z_], [sz])
                pb = pgG()
                pv = pb[:].bitcast(BF16)
                for h in range(4):
                    k.op("pe", lambda e: e.transpose(out=pv[:, h * 128:(h + 1) * 128], in_=on[:, h, :], identity=identb[:]), [on, identb], [pb])
                go = goT[r]
                k.op("dve", lambda e: e.scalar_tensor_tensor(out=go[:].rearrange("p h i -> p (h i)"), in0=pv[:, 0:512], scalar=gng[:, 0:1],
                                                             in1=sz[:].rearrange("p h i -> p (h i)"), op0=ALU.mult, op1=ALU.mult), [pb, gng, sz], [go])
                k.dma("pool", brT[0][:, tg:tg + 128].rearrange("(h p) t -> p h t", p=128), go[:], [go], [], f"gost{r}")

            def ssd_chunk(seq0, L, c, d):
                i = cnt["s"]
                cnt["s"] += 1
                r = i % R
                tg = seq0 + c * 128
                ps_, ab = pre_s[r], abt[r]
                load_pre(ps_, 2048, 8, seq0, L, c, f"pres{r}")
                xb, bb, bct = xB[r], Bbf[r], BCT[r]
                for part, (g0, ng) in enumerate(((0, 4), (4, 2))):
                    pb = pgS()
                    for gi in range(ng):
                        g = g0 + gi
                        for j in range(5):
                            k.op("pe", lambda e: e.matmul(pb[:, gi * 128:(gi + 1) * 128], lhsT=ps_[:, g, j:j + 128], rhs=dg_s[:, g, j, :],
                                                          start=(j == 0), stop=False), [ps_, dg_s], [pb])
                        k.op("pe", lambda e: e.matmul(pb[:, gi * 128:(gi + 1) * 128], lhsT=ones_row[0:1, :], rhs=cb_row[0:1, g * 128:(g + 1) * 128],
                                                      start=False, stop=True), [ones_row, cb_row], [pb])
                    k.op("act", lambda e: e.activation(out=xb[:, g0:g0 + ng, :], in_=pb[:, 0:ng * 128].rearrange("p (h d) -> p h d", h=ng), func=AF.Silu),
                         [pb], [xb])
                k.op("pool", lambda e: e.tensor_copy(out=bb[:], in_=xb[:, 4:6, :]), [xb], [bb])
                yield
                pb = pgS()
                for gi in range(4):
                    g = 4 + gi
                    for j in range(5):
                        k.op("pe", lambda e: e.matmul(pb[:, gi * 128:(gi + 1) * 128], lhsT=dg_s[:, g, j, :], rhs=ps_[:, g, j:j + 128],
                                                      start=(j == 0), stop=(j == 4)), [ps_, dg_s], [pb])
                for gi in range(4):
                    k.op("act", lambda e: e.activation(out=bct[:, gi, :], in_=pb[:, gi * 128:(gi + 1) * 128], func=AF.Silu, bias=cb_pp[:, 4 + gi:5 + gi]),
                         [pb, cb_pp], [bct])
                yield
                dt_, da = dts[r], dA[r]
                k.op("dve", lambda e: e.tensor_tensor(out=dt_[:], in0=ab[:, 16:32], in1=sdtb[:], op=ALU.add), [ab, sdtb], [dt_])
                k.op("act", lambda e: e.activation(out=dt_[:], in_=dt_[:], func=AF.Exp), [dt_], [dt_])
                k.op("act", lambda e: e.activation(out=dt_[:], in_=dt_[:], func=AF.Ln, bias=cst[:, 1:2]), [dt_, cst], [dt_])
                k.op("dve", lambda e: e.tensor_tensor(out=da[:], in0=dt_[:], in1=sA[:], op=ALU.mult), [dt_, sA], [da])
                dad = da[:, d * 8:(d + 1) * 8]
                dtd = dt_[:, d * 8:(d + 1) * 8]
                tri = cm[:, 1 + d, :]
                pb = pgS()
                k.op("pe", lambda e: e.matmul(pb[:, 0:8], lhsT=tri, rhs=dad, start=True, stop=True), [cm, da], [pb])
                k.op("pe", lambda e: e.matmul(pb[:, 8:16], lhsT=onesf[:], rhs=dad, start=True, stop=True), [onesf, da], [pb])
                ac, ea, ds, ec = acs[r], eac[r], dst_[r], ecd[r]
                k.op("act", lambda e: e.activation(out=ac[:], in_=pb[:, 0:8], func=AF.Copy), [pb], [ac])
                k.op("act", lambda e: e.activation(out=ea[:], in_=pb[:, 0:8], func=AF.Exp), [pb], [ea])
                k.op("act", lambda e: e.activation(out=ec[:], in_=pb[:, 8:16], func=AF.Exp), [pb], [ec])
                k.op("dve", lambda e: e.tensor_tensor(out=ds[:], in0=pb[:, 8:16], in1=ac[:], op=ALU.subtract), [pb, ac], [ds])
                k.op("act", lambda e: e.activation(out=ds[:], in_=ds[:], func=AF.Exp), [ds], [ds])
                yield
                gl, dm = GLs[0], Dms[0]
                k.op("dve", lambda e: e.tensor_tensor(out=gl[:], in0=bcm(tri, 8), in1=bc3(dad, 128), op=ALU.mult), [cm, da], [gl])
                pbs = [pgS(), pgS()]
                for hh in range(2):
                    b = pbs[hh]
                    k.op("pe", lambda e: e.matmul(b[:], lhsT=onesf[:], rhs=gl[:, hh * 4:(hh + 1) * 4, :].rearrange("p h i -> p (h i)"), start=True, stop=False),
                         [onesf, gl], [b])
                    for h in range(4):
                        k.op("pe", lambda e: e.matmul(b[:, h * 128:(h + 1) * 128], lhsT=gl[:, hh * 4 + h, :], rhs=nonesf[:], start=False, stop=True),
                             [gl, nonesf], [b])
                    k.op("dve", lambda e: e.tensor_tensor(out=dm[:, hh * 4:(hh + 1) * 4, :], in0=b[:].rearrange("p (h i) -> p h i", h=4),
                                                          in1=bcm(cm[:, 5 + d, :], 4), op=ALU.add), [b, cm], [dm])
                k.op("act", lambda e: e.activation(out=dm[:], in_=dm[:], func=AF.Exp), [dm], [dm])
                yield
                pb = pgS()
                for g in range(2):
                    k.op("pe", lambda e: e.matmul(pb[:, g * 128:(g + 1) * 128], lhsT=bct[:, g, :], rhs=bct[:, 2 + g, :], start=True, stop=True), [bct], [pb])
                mt = Mt[r]
                for g in range(2):
                    k.op("dve", lambda e: e.tensor_tensor(out=mt[:, g * 4:(g + 1) * 4, :], in0=dm[:, g * 4:(g + 1) * 4, :],
                                                          in1=bcm(pb[:, g * 128:(g + 1) * 128], 4), op=ALU.mult), [dm, pb], [mt])
                yield
                xd, xe = xdt[r], xdd[r]
                x3 = xb[:, 0:4, :].rearrange("p g (h2 q) -> p (g h2) q", h2=2)
                k.op("pool", lambda e: e.tensor_tensor(out=xd[:], in0=x3, in1=bc3(dtd, 64), op=ALU.mult), [xb, dt_], [xd])
                k.op("pool", lambda e: e.tensor_tensor(out=dd[:], in0=dtd, in1=ds[:], op=ALU.mult), [dt_, ds], [dd])
                k.op("pool", lambda e: e.tensor_tensor(out=xe[:], in0=x3, in1=bc3(dd[:], 64), op=ALU.mult), [xb, dd], [xe])
                H3, Hb = H32[d], Hbf[d]
                first = (c == 0) if d == 0 else (c == L // 128 - 1)
                if first:
                    k.op("pool", lambda e: e.memset(H3[:], 0.0), [], [H3])
                    k.op("pool", lambda e: e.memset(Hb[:], 0.0), [], [Hb])
                yield
                py = pgS()
                for h in range(8):
                    k.op("pe", lambda e: e.matmul(py[:, h * 64:(h + 1) * 64], lhsT=mt[:, h, :], rhs=xd[:, h, :], start=True, stop=True), [mt, xd], [py])
                pyo = pgS()
                for g in range(2):
                    k.op("pe", lambda e: e.matmul(pyo[:, g * 256:(g + 1) * 256], lhsT=bct[:, 2 + g, :], rhs=Hb[:, g * 256:(g + 1) * 256], start=True, stop=True),
                         [bct, Hb], [pyo])
                pst = pgS()
                for g in range(2):
                    k.op("pe", lambda e: e.matmul(pst[:, g * 256:(g + 1) * 256], lhsT=bb[:, g, :], rhs=xe[:, g * 4:(g + 1) * 4, :].rearrange("p h q -> p (h q)"),
                                                  start=True, stop=True), [bb, xe], [pst])
                yield
                y = ysb[r]
                k.op("dve", lambda e: e.tensor_tensor(out=ytmp[:].rearrange("p (h q) -> p h q", h=8), in0=pyo[:].rearrange("p (h q) -> p h q", h=8),
                                                      in1=bc3(ea[:], 64), op=ALU.mult), [pyo, ea], [ytmp])
                k.op("dve", lambda e: e.tensor_tensor(out=y[:], in0=ytmp[:], in1=py[:], op=ALU.add), [ytmp, py], [y])
                k.op("dve", lambda e: e.tensor_tensor(out=H3[:].rearrange("p (h q) -> p h q", h=8), in0=H3[:].rearrange("p (h q) -> p h q", h=8),
                                                      in1=bc3(ec[:], 64), op=ALU.mult), [H3, ec], [H3])
                k.op("dve", lambda e: e.tensor_tensor(out=H3[:], in0=H3[:], in1=pst[:], op=ALU.add), [H3, pst], [H3])
                k.op("act", lambda e: e.activation(out=Hb[:], in_=H3[:], func=AF.Copy), [H3], [Hb])
                yield
                if d == 0:
                    k.dma("pool", yf_tm[tg:tg + 128, :], y[:], [y], [r_yf], f"yfst{r}")
                    return
                yf = yfl[r]
                zl = zsl[r]
                k.dma("sp", yf[:], yf_tm[tg:tg + 128, :], [r_yf], [yf], f"yfld{r}")
                k.dma("sp", zl[:], zs_tm[tg:tg + 128, :], [], [zl], f"zsl{r}")
                k.op("dve", lambda e: e.tensor_tensor(out=y[:], in0=y[:], in1=yf[:], op=ALU.add), [y, yf], [y])
                k.op("pool", lambda e: e.tensor_tensor(out=ytmp[:].rearrange("p (h q) -> p h q", h=8), in0=x3, in1=bc3(sD[:], 64), op=ALU.mult), [xb, sD], [ytmp])
                k.op("dve", lambda e: e.tensor_tensor(out=y[:], in0=y[:], in1=ytmp[:], op=ALU.add), [y, ytmp], [y])
                k.op("act", lambda e: e.activation(out=zl[:], in_=zl[:], func=AF.Silu), [zl], [zl])
                k.op("dve", lambda e: e.tensor_tensor(out=y[:], in0=y[:], in1=zl[:], op=ALU.mult), [y, zl], [y])
                k.op("dve", lambda e: e.scalar_tensor_tensor(out=ytmp[:], in0=y[:], scalar=1.0, in1=y[:], op0=ALU.mult, op1=ALU.mult, accum_out=ss2[:, 0:1]),
                     [y], [ytmp, ss2])
                rsqrt(rs2, rs2[:, 0:1], ss2, ss2[:, 0:1], scale=1.0 / 512)
                k.op("dve", lambda e: e.scalar_tensor_tensor(out=ynb[:], in0=y[:], scalar=rs2[:, 0:1], in1=sng[:], op0=ALU.mult, op1=ALU.mult),
                     [y, rs2, sng], [ynb])
                pb = pgS()
                pv = pb[:].bitcast(BF16)
                for cc in range(4):
                    k.op("pe", lambda e: e.transpose(out=pv[:, cc * 128:(cc + 1) * 128], in_=ynb[:, cc * 128:(cc + 1) * 128], identity=identb[:]), [ynb, identb], [pb])
                so = soT[r]
                k.op("act", lambda e: e.activation(out=so[:].rearrange("p h i -> p (h i)"), in_=pv[:, 0:512], func=AF.Copy), [pb], [so])
                k.dma("pool", brT[1][:, tg:tg + 128].rearrange("(h p) t -> p h t", p=128), so[:], [so], [], f"sost{r}")

            seq0 = 0
            for L in seqs:
                nch = L // 128
                for d in range(2):
                    order = range(nch) if d == 0 else range(nch - 1, -1, -1)
                    for c in order:
                        tasks = [gdn_chunk(seq0, L, c, d), ssd_chunk(seq0, L, c, d)]
                        while tasks:
                            for t in list(tasks):
                                try:
                                    next(t)
                                except StopIteration:
                                    tasks.remove(t)
                seq0 += L

    def mla_phase(l):
        with k.phase():
            pset(range(8))
            wuq = k.sb([128, 2, 768 + 256], BF16, "wuq")
            wukv = k.sb([128, 2, 1024], BF16, "wukv")
            for c in range(2):
                load_w(wuq, wuq[:, c, 0:768], W["mla_w_uq"][l, c * 128:(c + 1) * 128, :])
                load_w(wukv, wukv[:, c, :], W["mla_w_ukv"][l, c * 128:(c + 1) * 128, :])
                for h in range(4):
                    load_w(wuq, wuq[:, c, 768 + h * 64:768 + h * 64 + 32], W["mla_w_uq"][l, c * 128:(c + 1) * 128, h * 192 + 160:h * 192 + 192], scale=-1.0)
                    load_w(wuq, wuq[:, c, 768 + h * 64 + 32:768 + h * 64 + 64], W["mla_w_uq"][l, c * 128:(c + 1) * 128, h * 192 + 128:h * 192 + 160])
            wv = k.sb([128, 2, 512], BF16, "wv")
            k.op("pool", lambda e: e.tensor_copy(out=wv[:].rearrange("p c (h x) -> p c h x", h=4),
                                                 in_=wukv[:].rearrange("p c (h x) -> p c h x", h=4)[:, :, :, 128:256]), [wukv], [wv])
            qg_ = k.sb([128, 2], F32, "qgain")
            kg_ = k.sb([128, 2], F32, "kvgain")
            load_gain(qg_, W["mla_q_norm"][l], 2)
            load_gain(kg_, W["mla_kv_norm"][l], 2)
            lat = k.ring(2, [128, 4, TT], BF16, "lat")
            krr = k.ring(2, [64, 2, TT], BF16, "krr")
            cs = k.ring(2, [64, 2, TT], F32, "cs")
            sqb = k.sb([128, 2, TT], BF16, "sqb")
            rb = k.sb([128, TT], F32, "rb")
            nrm = k.ring(2, [128, 4, TT], BF16, "nrm")
            stg = k.ring(4, [128, TT], BF16, "stg")
            r1 = k.sb([64, TT], F32, "r1")
            r2 = k.sb([64, TT], F32, "r2")
            rst = k.ring(4, [64, TT], BF16, "rst")
            vst = k.ring(2, [128, 4, 512], BF16, "vst")
            ns = 0
            nr = 0
            seq0 = 0
            ti = 0
            for L in seqs:
                for p0 in range(0, L, TT):
                    t0 = seq0 + p0
                    la, kr_, cs_ = lat[ti % 2], krr[ti % 2], cs[ti % 2]
                    k.dma("sp", la[:], featT[3072:3584, t0:t0 + TT].rearrange("(c p) t -> p c t", p=128), [], [la], f"lat{ti % 2}")
                    k.dma("sp", kr_[:], featT[3584:3712, t0:t0 + TT].rearrange("(c p) t -> p c t", p=64), [], [kr_], f"krr{ti % 2}")
                    k.dma("sp", cs_[:], ropecs[:, :, p0:p0 + TT].rearrange("c p t -> p c t"), [], [cs_], f"cs{ti % 2}")
                    nm = nrm[ti % 2]
                    for which in range(2):
                        gain = qg_ if which == 0 else kg_
                        k.op("pool", lambda e: e.tensor_tensor(out=sqb[:], in0=la[:, 2 * which:2 * which + 2, :], in1=la[:, 2 * which:2 * which + 2, :], op=ALU.mult),
                             [la], [sqb])
                        pb = pget()
                        for c in range(2):
                            k.op("pe", lambda e: e.matmul(pb[:], lhsT=onesb[:], rhs=sqb[:, c, :], start=(c == 0), stop=(c == 1)), [onesb, sqb], [pb])
                        rsqrt(rb, rb[:], pb, pb[:], scale=1.0 / 256)
                        for c in range(2):
                            k.op("dve", lambda e: e.scalar_tensor_tensor(out=nm[:, 2 * which + c, :], in0=la[:, 2 * which + c, :], scalar=gain[:, c:c + 1], in1=rb[:],
                                                                         op0=ALU.mult, op1=ALU.mult), [la, gain, rb], [nm])
                    for h in range(4):
                        pb = pget()
                        for c in range(2):
                            k.op("pe", lambda e: e.matmul(pb[:], lhsT=wuq[:, c, h * 192:h * 192 + 128], rhs=nm[:, c, :], start=(c == 0), stop=(c == 1)), [wuq, nm], [pb])
                        st = stg[ns % 4]
                        k.op("act", lambda e: e.activation(out=st[:], in_=pb[:], func=AF.Copy), [pb], [st])
                        k.dma("pool", qT_d[h, :, t0:t0 + TT], st[:], [st], [], f"mstg{ns % 4}")
                        ns += 1
                        pb = pget()
                        for c in range(2):
                            k.op("pe", lambda e: e.matmul(pb[:], lhsT=wukv[:, c, h * 256:h * 256 + 128], rhs=nm[:, 2 + c, :], start=(c == 0), stop=(c == 1)), [wukv, nm], [pb])
                        st = stg[ns % 4]
                        k.op("act", lambda e: e.activation(out=st[:], in_=pb[:], func=AF.Copy), [pb], [st])
                        k.dma("pool", kT_d[h, :, t0:t0 + TT], st[:], [st], [], f"mstg{ns % 4}")
                        ns += 1
                        pb = pget()
                        pb2 = pget()
                        for c in range(2):
                            k.op("pe", lambda e: e.matmul(pb[0:64, :], lhsT=wuq[:, c, h * 192 + 128:h * 192 + 192], rhs=nm[:, c, :], start=(c == 0), stop=(c == 1)), [wuq, nm], [pb])
                        for c in range(2):
                            k.op("pe", lambda e: e.matmul(pb2[0:64, :], lhsT=wuq[:, c, 768 + h * 64:768 + (h + 1) * 64], rhs=nm[:, c, :], start=(c == 0), stop=(c == 1)), [wuq, nm], [pb2])
                        k.op("dve", lambda e: e.tensor_tensor(out=r1[:], in0=pb[0:64, :], in1=cs_[:, 0, :], op=ALU.mult), [pb, cs_], [r1])
                        k.op("dve", lambda e: e.tensor_tensor(out=r2[:], in0=pb2[0:64, :], in1=cs_[:, 1, :], op=ALU.mult), [pb2, cs_], [r2])
                        rs_ = rst[nr % 4]
                        k.op("pool", lambda e: e.tensor_tensor(out=rs_[:], in0=r1[:], in1=r2[:], op=ALU.add), [r1, r2], [rs_])
                        k.dma("pool", qrT_d[h, :, t0:t0 + TT], rs_[:], [rs_], [], f"rstg{nr % 4}")
                        nr += 1
                    k.op("dve", lambda e: e.tensor_tensor(out=r1[:], in0=kr_[:, 0, :], in1=cs_[:, 0, :], op=ALU.mult), [kr_, cs_], [r1])
                    k.op("dve", lambda e: e.tensor_tensor(out=r2[:], in0=kr_[:, 1, :], in1=cs_[:, 1, :], op=ALU.mult), [kr_, cs_], [r2])
                    rs_ = rst[nr % 4]
                    k.op("pool", lambda e: e.tensor_tensor(out=rs_[:], in0=r1[:], in1=r2[:], op=ALU.add), [r1, r2], [rs_])
                    k.dma("pool", krT_d[:, t0:t0 + TT], rs_[:], [rs_], [], f"rstg{nr % 4}")
                    nr += 1
                    vs = vst[ti % 2]
                    for s in range(4):
                        pb = pget()
                        for c in range(2):
                            k.op("pe", lambda e: e.matmul(pb[:], lhsT=nm[:, 2 + c, s * 128:(s + 1) * 128], rhs=wv[:, c, :], start=(c == 0), stop=(c == 1)), [wv, nm], [pb])
                        k.op("act", lambda e: e.activation(out=vs[:, s, :], in_=pb[:], func=AF.Copy), [pb], [vs])
                    k.dma("pool", v_d[t0:t0 + TT, :].rearrange("(s p) d -> p s d", p=128), vs[:], [vs], [], f"vst{ti % 2}")
                    ti += 1
                seq0 += L
        with k.phase():
            pset([0, 1, 2])
            kT = k.ring(2, [128, LMAX], BF16, "kT")
            vv = k.ring(2, [128, LMAX // 128, 128], BF16, "vv")
            krT = k.ring(2, [64, LMAX], BF16, "krT")
            qt = k.ring(2, [128, TT], BF16, "qt")
            qrt = k.ring(2, [64, TT], BF16, "qrt")
            pt = k.ring(3, [128, TT], BF16, "pt")
            rl = k.sb([128, TT], F32, "rl")
            ot = k.ring(2, [128, TT], BF16, "ot")
            seq0 = 0
            nq = 0
            npb = 0
            for si, L in enumerate(seqs):
                nkb = L // 128
                kr_ = krT[si % 2]
                k.dma("sp", kr_[:, 0:L], krT_d[:, seq0:seq0 + L], [], [kr_], f"krT{si % 2}")
                for h in range(4):
                    hh = (si * 4 + h) % 2
                    kt_, v_ = kT[hh], vv[hh]
                    k.dma("sp", kt_[:, 0:L], kT_d[h, :, seq0:seq0 + L], [], [kt_], f"kT{hh}")
                    for b0 in range(0, nkb, 8):
                        b1 = min(b0 + 8, nkb)
                        k.dma("sp", v_[:, b0:b1, :], v_d[seq0 + b0 * 128:seq0 + b1 * 128, h * 128:(h + 1) * 128].rearrange("(n p) d -> p n d", p=128), [], [v_], f"vv{hh}")
                    for p0 in range(0, L, TT):
                        t0 = seq0 + p0
                        q_, qr_ = qt[nq % 2], qrt[nq % 2]
                        k.dma("sp", q_[:], qT_d[h, :, t0:t0 + TT], [], [q_], f"qt{nq % 2}")
                        k.dma("sp", qr_[:], qrT_d[h, :, t0:t0 + TT], [], [qr_], f"qrt{nq % 2}")
                        po = banks[3 + 2 * (nq % 2)]
                        pl = banks[4 + 2 * (nq % 2)]
                        def s_step(kb):
                            ps_ = pget()
                            k.op("pe", lambda e: e.matmul(ps_[:], lhsT=kt_[:, kb * 128:(kb + 1) * 128], rhs=q_[:], start=True, stop=False), [kt_, q_], [ps_])
                            k.op("pe", lambda e: e.matmul(ps_[:], lhsT=kr_[:, kb * 128:(kb + 1) * 128], rhs=qr_[:], start=False, stop=True), [kr_, qr_], [ps_])
                            p_ = pt[s_step.n % 3]
                            s_step.n += 1
                            k.op("act", lambda e: e.activation(out=p_[:], in_=ps_[:], func=AF.Exp, scale=MLA_SCALE), [ps_], [p_])
                            return p_

                        def pv_step(kb, p_):
                            k.op("pe", lambda e: e.matmul(po[:], lhsT=v_[:, kb, :], rhs=p_[:], start=(kb == 0), stop=(kb == nkb - 1)), [v_, p_], [po])
                            k.op("pe", lambda e: e.matmul(pl[:], lhsT=onesb[:], rhs=p_[:], start=(kb == 0), stop=(kb == nkb - 1)), [onesb, p_], [pl])
                        s_step.n = npb
                        pend = [s_step(0)]
                        for kb in range(nkb):
                            if kb + 1 < nkb:
                                pend.append(s_step(kb + 1))
                            pv_step(kb, pend.pop(0))
                        npb = s_step.n
                        k.op("dve", lambda e: e.reciprocal(out=rl[:], in_=pl[:]), [pl], [rl])
                        o_ = ot[nq % 2]
                        k.op("dve", lambda e: e.tensor_tensor(out=o_[:], in0=po[:], in1=rl[:], op=ALU.mult), [po, rl], [o_])
                        k.dma("pool", brT[2][h * 128:(h + 1) * 128, t0:t0 + TT], o_[:], [o_], [], f"ot{nq % 2}")
                        nq += 1
                seq0 += L

    def merge_phase(l, src, dst):
        with k.phase():
            pset(range(8))
            wg = k.sb([128, 8, 3 * D], BF16, "wg")
            for kk in range(8):
                load_w_wide(wg, wg[:, kk, :], W["w_gate"][l, kk * 128:(kk + 1) * 128, :])
            wbr = k.sb([128, 3, 4, D], BF16, "wbr")
            for bi, wn in enumerate(("w_branch_a", "w_branch_b", "w_branch_c")):
                for kk in range(4):
                    load_w(wbr, wbr[:, bi, kk, :], W[wn][l, kk * 128:(kk + 1) * 128, :])
            wo = k.sb([128, 8, D], BF16, "wo")
            for kk in range(8):
                load_w(wo, wo[:, kk, :], W["w_out"][l, kk * 128:(kk + 1) * 128, :])
            gain = k.sb([128, 8], F32, "gain")
            load_gain(gain, W["mix_norm"][l], 8)
            bg = k.sb([128, 24], F32, "bg")
            k.dma("sp", bg[:], W["b_gate"][l].rearrange("(kk p) -> p kk", p=128), [], [bg], "gain", allow_slow_non_contiguous=True)
            hbs = k.ring(2, [128, 4, D], F32, "hb")
            xnr = k.ring(4, [128, D], BF16, "xn")
            uT = k.sb([128, 8, TT], BF16, "uT")
            ss = k.sb([128, 8], F32, "ss")
            rs = k.sb([128, 8], F32, "rs")
            brs = k.ring(2, [128, 3, 4, TT], BF16, "brs")
            gts = k.ring(3, [128, TT], F32, "gts")
            macc = k.sb([128, TT], F32, "macc")
            mt2 = k.sb([128, TT], F32, "mt2")
            mT = k.sb([128, 8, TT], BF16, "mT")
            ctr = [0]
            ng = 0
            for ti in range(NT // TT):
                t0 = ti * TT
                hb = hbs[ti % 2]
                load_h(src, t0, hb, f"hld{ti % 2}")
                br = brs[ti % 2]
                for bi in range(3):
                    k.dma("sp", br[:, bi, :, :], brT[bi][:, t0:t0 + TT].rearrange("(c p) t -> p c t", p=128), [], [br], f"brs{ti % 2}")
                norm_T(hb, gain, uT, xnr, ss, rs, ctr)
                for c in range(8):
                    for bi in range(3):
                        pgt = pget()
                        for kk in range(8):
                            k.op("pe", lambda e: e.matmul(pgt[:], lhsT=wg[:, kk, bi * D + c * 128: bi * D + (c + 1) * 128], rhs=uT[:, kk, :], start=(kk == 0), stop=(kk == 7)),
                                 [wg, uT], [pgt])
                        py = pget()
                        for kk in range(4):
                            k.op("pe", lambda e: e.matmul(py[:], lhsT=wbr[:, bi, kk, c * 128:(c + 1) * 128], rhs=br[:, bi, kk, :], start=(kk == 0), stop=(kk == 3)),
                                 [wbr, br], [py])
                        gt_ = gts[ng % 3]
                        ng += 1
                        k.op("act", lambda e: e.activation(out=gt_[:], in_=pgt[:], func=AF.Sigmoid, bias=bg[:, bi * 8 + c: bi * 8 + c + 1]), [pgt, bg], [gt_])
                        if bi == 0:
                            k.op("dve", lambda e: e.tensor_tensor(out=macc[:], in0=gt_[:], in1=py[:], op=ALU.mult), [gt_, py], [macc])
                        elif bi == 1:
                            k.op("dve", lambda e: e.tensor_tensor(out=mt2[:], in0=gt_[:], in1=py[:], op=ALU.mult), [gt_, py], [mt2])
                            k.op("pool", lambda e: e.tensor_tensor(out=macc[:], in0=macc[:], in1=mt2[:], op=ALU.add), [macc, mt2], [macc])
                        else:
                            k.op("dve", lambda e: e.tensor_tensor(out=mt2[:], in0=gt_[:], in1=py[:], op=ALU.mult), [gt_, py], [mt2])
                            k.op("pool", lambda e: e.tensor_tensor(out=mT[:, c, :], in0=macc[:], in1=mt2[:], op=ALU.add), [macc, mt2], [mT])
                for s in range(4):
                    for fh in range(2):
                        po = pget()
                        for c in range(8):
                            k.op("pe", lambda e: e.matmul(po[:], lhsT=mT[:, c, s * 128:(s + 1) * 128], rhs=wo[:, c, fh * 512:(fh + 1) * 512], start=(c == 0), stop=(c == 7)),
                                 [mT, wo], [po])
                        k.op("dve", lambda e: e.tensor_tensor(out=hb[:, s, fh * 512:(fh + 1) * 512], in0=po[:], in1=hb[:, s, fh * 512:(fh + 1) * 512], op=ALU.add),
                             [po, hb], [hb])
                k.dma("pool", dst[t0:t0 + TT, :].rearrange("(s p) d -> p s d", p=128), hb[:], [hb], [], f"hst{ti % 2}")

    import os
    stopat = int(os.environ.get("STOPAT", "999"))
    phases = []
    cur = xin
    for l in range(DEPTH):
        last = (l == DEPTH - 1)
        phases.append(lambda l=l, cur=cur: ffn_phase(l, 0, cur, hA, False))
        phases.append(lambda l=l: m1_phase(l, hA))
        phases.append(lambda l=l: scan_phase(l))
        phases.append(lambda l=l: mla_phase(l))
        phases.append(lambda l=l: merge_phase(l, hA, hB))
        phases.append(lambda l=l, last=last: ffn_phase(l, 1, hB, yout if last else hA, last))
        cur = hA
    for i, ph in enumerate(phases):
        if i >= stopat:
            break
        ph()
    if stopat < 999:
        scr = {"hA": hA, "hB": hB, "featT": featT, "zs_tm": zs_tm, "sm_tm": sm_tm, "of_tm": of_tm, "yf_tm": yf_tm,
               "brT0": brT[0], "brT1": brT[1], "brT2": brT[2], "krT_d": krT_d, "v_d": v_d, "qT_d": qT_d.rearrange("h p t -> (h p) t"), "qrT_d": qrT_d.rearrange("h p t -> (h p) t"), "kT_d": kT_d.rearrange("h p t -> (h p) t")}
        for n, ap in scr.items():
            pr = nc.dram_tensor("p_" + n, list(ap.shape), ap.dtype, kind="ExternalOutput").ap()
            for r0 in range(0, ap.shape[0], 128):
                r1 = min(r0 + 128, ap.shape[0])
                k.dma("sp", pr[r0:r1], ap[r0:r1], [], [], "probe")
    if stopat < 999:
        for n, b, shp, dt_ in (("onesb", onesb, [128, 128], BF16), ("identb", identb, [128, 128], BF16), ("onesf", onesf, [128, 128], F32),
                               ("cst", cst, [128, 4], F32), ("cm", cm, [128, 7 * 128], F32)):
            pr = nc.dram_tensor("p_" + n, shp, dt_, kind="ExternalOutput").ap()
            src = b[:] if n != "cm" else b[:].rearrange("p a b -> p (a b)")
            k.dma("sp", pr, src, [b], [], "probe")
    k.barrier()
    return nc, k


def _consts(LMAX):
    idx = np.arange(128)
    j = idx[:, None]
    i = idx[None, :]
    cm = np.zeros((128, 7, 128), np.float32)
    cm[:, 0] = (j == i)
    cm[:, 1] = (j <= i)
    cm[:, 2] = (j >= i)
    NEG = -1e30
    cm[:, 3] = np.where(i > j, 0.0, NEG)
    cm[:, 4] = np.where(i < j, 0.0, NEG)
    cm[:, 5] = np.where(i >= j, 0.0, NEG)
    cm[:, 6] = np.where(i <= j, 0.0, NEG)
    inv_freq = np.power(np.float32(10000.0), -np.arange(0, 64, 2, dtype=np.float32) / np.float32(64)).astype(np.float32)
    ang = np.arange(LMAX, dtype=np.float32)[:, None] * inv_freq[None, :]
    ang = np.concatenate([ang, ang], axis=-1).astype(np.float32)
    rc = np.stack([np.cos(ang).T, np.sin(ang).T]).astype(np.float32)
    return cm, np.ascontiguousarray(rc)


_CACHE = {}


def run(seq_lists, x_per_core, weights):
    seqs = tuple(seq_lists)
    if seqs not in _CACHE:
        _CACHE[seqs] = kernel_build(list(seqs))
    nc, _ = _CACHE[seqs]
    cm, rc = _consts(max(seqs))
    wm = {}
    for n, a in weights.items():
        a = np.asarray(a, np.float32)
        if n in ("gdn_A_log", "gdn_dt_bias"):
            a = a.reshape(DEPTH, 8)
        elif n in ("ssd_A_log", "ssd_dt_bias"):
            a = a.reshape(DEPTH, 16)
        elif n == "final_norm":
            a = a.reshape(1, D)
        wm[n] = np.ascontiguousarray(a)
    in_maps = []
    for x in x_per_core:
        m = dict(wm)
        m["xin"] = np.ascontiguousarray(x, dtype=np.float32)
        m["cmask"] = cm
        m["ropecs"] = rc
        in_maps.append(m)
    res = run_bass_kernel_spmd(nc, in_maps, core_ids=list(range(len(x_per_core))))
    return [r["yout"] for r in res.results]


def kernel(x_prompt, x_sample, **weights):
    x_prompt = np.asarray(x_prompt, np.float32)
    x_sample = np.asarray(x_sample, np.float32)
    B, S, _ = x_prompt.shape
    DB, DS, _ = x_sample.shape
    n = N_CORES
    pp = B // n
    sp = DB // n
    seqs = [S] * pp + [DS] * sp
    xs = []
    for c in range(n):
        parts = [x_prompt[c * pp + i] for i in range(pp)] + [x_sample[c * sp + i] for i in range(sp)]
        xs.append(np.concatenate(parts, axis=0))
    outs = run(seqs, xs, weights)
    y_prompt = np.empty_like(x_prompt)
    y_sample = np.empty_like(x_sample)
    for c in range(n):
        o = outs[c]
        off = 0
        for i in range(pp):
            y_prompt[c * pp + i] = o[off:off + S]
            off += S
        for i in range(sp):
            y_sample[c * sp + i] = o[off:off + DS]
            off += DS
    return (y_prompt, y_sample)
```

```python
import math
import numpy as np
from contextlib import ExitStack, contextmanager
import concourse.bass as bass
import concourse.mybir as mybir
from concourse.bass_utils import run_bass_kernel_spmd

F32 = mybir.dt.float32
BF16 = mybir.dt.bfloat16
AF = mybir.ActivationFunctionType
ALU = mybir.AluOpType
AX = mybir.AxisListType

D = 1024
FH = 2816
NJ = FH // 128
DEPTH = 2
INC = 4192
EPS = 1e-6
MLA_SCALE = 192 ** -0.5
STRICT = True
INV_DT = F32
N_CORES = 8
TT = 512
SCRATCH_EXTERNAL = False
STORE_Q = "sp"


class Res:
    def __init__(self, acc=False):
        self.w = None
        self.r = {}
        self.acc = acc


class Buf(Res):
    def __init__(self, t):
        super().__init__()
        self.t = t

    def __getitem__(self, key):
        return self.t[key]


class K:
    def __init__(self, nc):
        self.nc = nc
        self.es = ExitStack()
        self.eng = {"pe": nc.tensor, "act": nc.scalar, "dve": nc.vector, "pool": nc.gpsimd, "sp": nc.sync}
        self.semh = {}
        self.cnt = {}
        for e in self.eng:
            self.semh[e] = self.es.enter_context(nc.semaphore("se_" + e))
            self.cnt[e] = 0
        self.seen = {e: {} for e in self.eng}
        self.dtot = {}
        self.nops = 0
        self.uid = 0

    @contextmanager
    def phase(self):
        st = ExitStack()
        self._ph = st
        try:
            yield st
        finally:
            self.barrier()
            st.close()

    def sb(self, shape, dt, name=None, glob=False):
        self.uid += 1
        t = (self.es if glob else self._ph).enter_context(
            self.nc.sbuf_tensor(f"{name or 'sb'}_{self.uid}", list(shape), dt))
        return Buf(t)

    def ring(self, n, shape, dt, name=None):
        return [self.sb(shape, dt, name) for _ in range(n)]

    def _dsem(self, key):
        if key not in self.semh:
            self.semh[key] = self.es.enter_context(self.nc.semaphore("sd_" + key))
            self.dtot[key] = 0
        return self.semh[key]

    def _deps(self, E, reads, writes, skip=None):
        evs = {}

        def add(d):
            for k, v in d.items():
                if evs.get(k, 0) < v:
                    evs[k] = v
        for r in reads:
            if r.w:
                add(r.w)
        for r in writes:
            if r.w:
                add(r.w)
            add(r.r)
        if E == "pe" or (not STRICT and E in ("act", "dve")):
            evs.pop(E, None)
        if skip:
            evs.pop(skip, None)
        return evs

    def _wait(self, E, evs):
        for key, val in evs.items():
            if key in self.dtot:
                val = self.dtot[key]
            if self.seen[E].get(key, 0) < val:
                self.eng[E].wait_ge(self.semh[key], val)
                self.seen[E][key] = val

    def op(self, E, fn, reads=(), writes=()):
        self._wait(E, self._deps(E, reads, writes))
        ins = fn(self.eng[E])
        self.cnt[E] += 1
        c = self.cnt[E]
        ins.then_inc(self.semh[E], 1)
        self.nops += 1
        for r in reads:
            if r.r.get(E, 0) < c:
                r.r[E] = c
        for r in writes:
            r.w = {E: c}
            r.r = {}

    def dma(self, Q, out, in_, reads, writes, key, **kw):
        if Q == "pool" and out.dtype == in_.dtype:
            Q = STORE_Q
        self._dsem(key)
        self._wait(Q, self._deps(Q, reads, writes, skip=key))
        ins = self.eng[Q].dma_start(out=out, in_=in_, **kw)
        self.dtot[key] += 16
        v = self.dtot[key]
        ins.then_inc(self.semh[key], 16)
        self.nops += 1
        for r in reads:
            r.r[key] = v
        for r in writes:
            if getattr(r, "acc", False):
                r.w = dict(r.w or {})
                r.w[key] = v
            else:
                r.w = {key: v}
                r.r = {}

    def barrier(self):
        tot = {e: self.cnt[e] for e in self.eng}
        tot.update(self.dtot)
        for E in self.eng:
            ev = dict(tot)
            ev.pop(E, None)
            self._wait(E, ev)


def kernel_build(seqs, want_debug=False):
    NT = sum(seqs)
    LMAX = max(seqs)
    nc = bass.Bass("TRN2", target_bir_lowering=False)
    k = K(nc)
    P = 128

    def din(name, shape, dt=F32):
        return nc.dram_tensor(name, list(shape), dt, kind="ExternalInput").ap()

    def dscr(name, shape, dt):
        ext = want_debug or SCRATCH_EXTERNAL
        return nc.dram_tensor(name, list(shape), dt, kind="ExternalOutput" if ext else "Internal").ap()

    xin = din("xin", [NT, D])
    yout = nc.dram_tensor("yout", [NT, D], F32, kind="ExternalOutput").ap()
    W = {}
    wshapes = {
        "ffn1_norm": [DEPTH, D], "w_ffn1_in": [DEPTH, D, 2 * FH], "w_ffn1_out": [DEPTH, FH, D],
        "mix_norm": [DEPTH, D], "w_in": [DEPTH, D, INC], "gdn_conv": [DEPTH, 5, 1536],
        "gdn_A_log": [DEPTH, 8], "gdn_dt_bias": [DEPTH, 8], "gdn_norm": [DEPTH, 128],
        "ssd_conv": [DEPTH, 5, 1024], "ssd_conv_b": [DEPTH, 1024], "ssd_A_log": [DEPTH, 16],
        "ssd_dt_bias": [DEPTH, 16], "ssd_D": [DEPTH, 8], "ssd_norm": [DEPTH, 512],
        "mla_q_norm": [DEPTH, 256], "mla_w_uq": [DEPTH, 256, 768], "mla_kv_norm": [DEPTH, 256],
        "mla_w_ukv": [DEPTH, 256, 1024], "w_branch_a": [DEPTH, 512, D], "w_branch_b": [DEPTH, 512, D],
        "w_branch_c": [DEPTH, 512, D], "w_gate": [DEPTH, D, 3 * D], "b_gate": [DEPTH, 3 * D],
        "w_out": [DEPTH, D, D], "ffn2_norm": [DEPTH, D], "w_ffn2_in": [DEPTH, D, 2 * FH],
        "w_ffn2_out": [DEPTH, FH, D], "final_norm": [1, D],
    }
    for n, s in wshapes.items():
        W[n] = din(n, s)
    cmask = din("cmask", [128, 7, 128])
    ropecs = din("ropecs", [2, 64, LMAX])

    hA = dscr("hA", [NT, D], F32)
    hB = dscr("hB", [NT, D], F32)
    featT = dscr("featT", [3712, NT], BF16)
    zs_tm = dscr("zs_tm", [NT, 512], F32)
    sm_tm = dscr("sm_tm", [NT, 32], F32)
    of_tm = dscr("of_tm", [NT, 512], F32)
    yf_tm = dscr("yf_tm", [NT, 512], F32)
    brT = [dscr(f"brT{i}", [512, NT], BF16) for i in range(3)]
    qT_d = dscr("qT_d", [4, 128, NT], BF16)
    qrT_d = dscr("qrT_d", [4, 64, NT], BF16)
    kT_d = dscr("kT_d", [4, 128, NT], BF16)
    krT_d = dscr("krT_d", [64, NT], BF16)
    v_d = dscr("v_d", [NT, 512], BF16)
    wffn_bf = dscr("wffn_bf", [DEPTH * 2, NJ, 128, 8 * 256], BF16)

    cm = k.sb([128, 7, 128], F32, "cm", glob=True)
    identb = k.sb([128, 128], BF16, "identb", glob=True)
    onesf = k.sb([128, 128], F32, "onesf", glob=True)
    nonesf = k.sb([128, 128], F32, "nonesf", glob=True)
    onesb = k.sb([128, 128], BF16, "onesb", glob=True)
    cst = k.sb([128, 4], F32, "cst", glob=True)
    banks = []
    for i in range(8):
        banks.append(Buf(k.es.enter_context(nc.psum_tensor(f"psb{i}", [128, 512], F32))))
    pstate = {"i": 0, "set": list(range(8))}

    def pget():
        s = pstate["set"]
        b = banks[s[pstate["i"] % len(s)]]
        pstate["i"] += 1
        return b

    def pset(lst):
        pstate["set"] = list(lst)
        pstate["i"] = 0

    k.dma("sp", cm[:], cmask, [], [cm], "cm")
    k.op("dve", lambda e: e.memset(onesf[:], 1.0), [], [onesf])
    k.op("dve", lambda e: e.memset(nonesf[:], -1.0), [], [nonesf])
    k.op("dve", lambda e: e.memset(onesb[:], 1.0), [], [onesb])
    k.op("dve", lambda e: e.memset(cst[:, 0:1], EPS), [], [cst])
    k.op("dve", lambda e: e.memset(cst[:, 1:2], 1.0), [], [cst])
    k.op("dve", lambda e: e.memset(cst[:, 2:3], 0.0), [], [cst])
    k.op("dve", lambda e: e.tensor_copy(out=identb[:], in_=cm[:, 0, :]), [cm], [identb])
    ident_f = cm[:, 0, :]

    def bc3(ap2, n):
        return ap2.unsqueeze(2).to_broadcast([ap2.shape[0], ap2.shape[1], n])

    def bcm(ap2, n):
        return ap2.unsqueeze(1).to_broadcast([ap2.shape[0], n, ap2.shape[1]])

    def rsqrt(out_buf, out_ap, in_buf, in_ap, scale=1.0):
        k.op("dve", lambda e: e.tensor_scalar(out=out_ap, in0=in_ap, scalar1=scale, scalar2=EPS, op0=ALU.mult, op1=ALU.add),
             [in_buf], [out_buf])
        k.op("act", lambda e: e.activation(out=out_ap, in_=out_ap, func=AF.Ln), [out_buf], [out_buf])
        k.op("act", lambda e: e.activation(out=out_ap, in_=out_ap, func=AF.Exp, scale=-0.5), [out_buf], [out_buf])

    wst = [k.sb([128, 1024], F32, "wst", glob=True) for _ in range(3)]
    wctr = [0]

    def load_w(dst_buf, dst_ap, src_ap, scale=None, view=None):
        i = wctr[0]
        wctr[0] += 1
        stg_ = wst[i % 3]
        if view is None:
            np_, n = src_ap.shape
            sv = stg_[0:np_, 0:n]
        else:
            sv = view(stg_)
        k.dma("sp", sv, src_ap, [], [stg_], f"wst{i % 3}")
        if scale is not None:
            k.op("dve", lambda e: e.tensor_scalar(out=dst_ap, in0=sv, scalar1=scale, scalar2=None, op0=ALU.mult), [stg_], [dst_buf])
        elif i % 3 == 0:
            k.op("act", lambda e: e.activation(out=dst_ap, in_=sv, func=AF.Copy), [stg_], [dst_buf])
        elif i % 3 == 1:
            k.op("dve", lambda e: e.tensor_copy(out=dst_ap, in_=sv), [stg_], [dst_buf])
        else:
            k.op("pool", lambda e: e.tensor_copy(out=dst_ap, in_=sv), [stg_], [dst_buf])

    def load_w_wide(dst_buf, dst2d, src2d):
        n = src2d.shape[1]
        for c0 in range(0, n, 1024):
            c1 = min(c0 + 1024, n)
            load_w(dst_buf, dst2d[:, c0:c1], src2d[:, c0:c1])

    with k.phase():
        stg = k.ring(3, [128, 8, 256], BF16, "wstg")
        n = 0
        for l in range(DEPTH):
            for f, wn in enumerate(("w_ffn1_in", "w_ffn2_in")):
                for j in range(NJ):
                    b = stg[n % 3]
                    n += 1
                    for half in range(2):
                        src = W[wn][l, :, half * FH + j * 128: half * FH + (j + 1) * 128].rearrange("(kk p) n -> p kk n", p=128)
                        load_w(b, b[:, :, half * 128:(half + 1) * 128], src, view=lambda t: t[:, 0:1024].rearrange("p (kk n) -> p kk n", kk=8))
                    k.dma("sp", wffn_bf[l * 2 + f, j].rearrange("p (kk n) -> p kk n", kk=8), b[:], [b], [], f"wstg_o{(n - 1) % 3}")

    def load_h(src, t0, hb, key):
        k.dma("sp", hb[:], src[t0:t0 + TT, :].rearrange("(s p) d -> p s d", p=128), [], [hb], key)

    def norm_A(hb, xn_ring, ss, rs, ctr):
        for s in range(4):
            xn = xn_ring[ctr[0] % len(xn_ring)]
            ctr[0] += 1
            k.op("dve", lambda e: e.scalar_tensor_tensor(out=xn[:], in0=hb[:, s, :], scalar=1.0, in1=hb[:, s, :],
                                                         op0=ALU.mult, op1=ALU.mult, accum_out=ss[:, s:s + 1]),
                 [hb], [xn, ss])
        rsqrt(rs, rs[:, 0:4], ss, ss[:, 0:4], scale=1.0 / D)
        xns = []
        for s in range(4):
            xn = xn_ring[ctr[0] % len(xn_ring)]
            ctr[0] += 1
            k.op("dve", lambda e: e.tensor_scalar(out=xn[:], in0=hb[:, s, :], scalar1=rs[:, s:s + 1], scalar2=None, op0=ALU.mult),
                 [hb, rs], [xn])
            xns.append(xn)
        return xns

    def norm_B(xns, gain, uT):
        for kk in range(8):
            pb = pget()
            pv = pb[:].bitcast(BF16)
            for s in range(4):
                k.op("pe", lambda e: e.transpose(out=pv[:, s * 128:(s + 1) * 128], in_=xns[s][:, kk * 128:(kk + 1) * 128], identity=identb[:]),
                     [xns[s], identb], [pb])
            k.op("dve" if kk % 2 else "act",
                 (lambda e: e.tensor_scalar(out=uT[:, kk, :], in0=pv[:, 0:512], scalar1=gain[:, kk:kk + 1], scalar2=None, op0=ALU.mult)) if kk % 2 else
                 (lambda e: e.activation(out=uT[:, kk, :], in_=pv[:, 0:512], func=AF.Copy, scale=gain[:, kk:kk + 1])),
                 [pb, gain], [uT])

    def norm_T(hb, gain, uT, xn_ring, ss, rs, ctr):
        norm_B(norm_A(hb, xn_ring, ss, rs, ctr), gain, uT)

    def load_gain(dst, src_row, nk):
        k.dma("sp", dst[:, 0:nk], src_row.rearrange("(kk p) -> p kk", p=128), [], [dst], "gain", allow_slow_non_contiguous=True)

    def ffn_phase(l, f, src, dst, final):
        wn_out = "w_ffn1_out" if f == 0 else "w_ffn2_out"
        gn = "ffn1_norm" if f == 0 else "ffn2_norm"
        with k.phase():
            pset(range(8))
            wout = k.sb([128, NJ, D], BF16, "wout")
            for j in range(NJ):
                load_w(wout, wout[:, j, :], W[wn_out][l, j * 128:(j + 1) * 128, :])
            gain = k.sb([128, 8], F32, "gain")
            load_gain(gain, W[gn][l], 8)
            if final:
                fg = k.sb([128, D], F32, "fg")
                k.dma("sp", fg[:], W["final_norm"][0].partition_broadcast(128), [], [fg], "fg")
            hbs = k.ring(2, [128, 4, D], F32, "hb")
            xnr = k.ring(4, [128, D], BF16, "xn")
            uTs = k.ring(2, [128, 8, TT], BF16, "uT")
            hid = k.sb([128, NJ, TT], BF16, "hid")
            ss_b = k.sb([128, 8], F32, "ss_b")
            rs_b = k.sb([128, 8], F32, "rs_b")
            sgs = k.ring(2, [128, TT], F32, "sg")
            slabs = k.ring(3, [128, 8, 256], BF16, "slab")
            ss = k.sb([128, 8], F32, "ss")
            rs = k.sb([128, 8], F32, "rs")
            ctr = [0]
            nsl = 0
            ntl = NT // TT
            load_h(src, 0, hbs[0], "hld0")
            norm_T(hbs[0], gain, uTs[0], xnr, ss_b, rs_b, ctr)
            for ti in range(ntl):
                t0 = ti * TT
                hb = hbs[ti % 2]
                uT = uTs[ti % 2]
                nxt = None
                for j in range(NJ):
                    if ti + 1 < ntl:
                        if j == 1:
                            load_h(src, t0 + TT, hbs[(ti + 1) % 2], f"hld{(ti + 1) % 2}")
                        if j == 6:
                            nxt = norm_A(hbs[(ti + 1) % 2], xnr, ss_b, rs_b, ctr)
                        if j == 16:
                            norm_B(nxt, gain, uTs[(ti + 1) % 2])
                    sl = slabs[nsl % 3]
                    k.dma("sp", sl[:], wffn_bf[l * 2 + f, j].rearrange("p (kk n) -> p kk n", kk=8), [], [sl], f"slab{nsl % 3}")
                    nsl += 1
                    pg = pget()
                    pu = pget()
                    for kk in range(8):
                        k.op("pe", lambda e: e.matmul(pg[:], lhsT=sl[:, kk, 0:128], rhs=uT[:, kk, :], start=(kk == 0), stop=(kk == 7)),
                             [sl, uT], [pg])
                    for kk in range(8):
                        k.op("pe", lambda e: e.matmul(pu[:], lhsT=sl[:, kk, 128:256], rhs=uT[:, kk, :], start=(kk == 0), stop=(kk == 7)),
                             [sl, uT], [pu])
                    sg = sgs[j % 2]
                    k.op("act", lambda e: e.activation(out=sg[:], in_=pg[:], func=AF.Silu), [pg], [sg])
                    k.op("dve", lambda e: e.tensor_tensor(out=hid[:, j, :], in0=sg[:], in1=pu[:], op=ALU.mult), [sg, pu], [hid])
                for s in range(4):
                    for fh in range(2):
                        po = pget()
                        for j in range(NJ):
                            k.op("pe", lambda e: e.matmul(po[:], lhsT=hid[:, j, s * 128:(s + 1) * 128], rhs=wout[:, j, fh * 512:(fh + 1) * 512],
                                                          start=(j == 0), stop=(j == NJ - 1)), [hid, wout], [po])
                        k.op("dve", lambda e: e.scalar_tensor_tensor(out=hb[:, s, fh * 512:(fh + 1) * 512], in0=po[:], scalar=0.5,
                                                                     in1=hb[:, s, fh * 512:(fh + 1) * 512], op0=ALU.mult, op1=ALU.add),
                             [po, hb], [hb])
                if final:
                    for s in range(4):
                        xn = xnr[ctr[0] % 4]
                        ctr[0] += 1
                        k.op("dve", lambda e: e.scalar_tensor_tensor(out=sgs[0][:], in0=hb[:, s, 0:512], scalar=1.0, in1=hb[:, s, 0:512],
                                                                     op0=ALU.mult, op1=ALU.mult, accum_out=ss[:, s:s + 1]), [hb], [sgs[0], ss])
                        k.op("dve", lambda e: e.scalar_tensor_tensor(out=sgs[0][:], in0=hb[:, s, 512:1024], scalar=1.0, in1=hb[:, s, 512:1024],
                                                                     op0=ALU.mult, op1=ALU.mult, accum_out=ss[:, 4 + s:5 + s]), [hb], [sgs[0], ss])
                    k.op("dve", lambda e: e.tensor_tensor(out=ss[:, 0:4], in0=ss[:, 0:4], in1=ss[:, 4:8], op=ALU.add), [ss], [ss])
                    rsqrt(rs, rs[:, 0:4], ss, ss[:, 0:4], scale=1.0 / D)
                    for s in range(4):
                        k.op("dve", lambda e: e.scalar_tensor_tensor(out=hb[:, s, :], in0=hb[:, s, :], scalar=rs[:, s:s + 1], in1=fg[:],
                                                                     op0=ALU.mult, op1=ALU.mult), [hb, rs, fg], [hb])
                k.dma("pool", dst[t0:t0 + TT, :].rearrange("(s p) d -> p s d", p=128), hb[:], [hb], [], f"hst{ti % 2}")

    def m1_phase(l, src):
        with k.phase():
            pset(range(8))
            win = k.sb([128, 8, INC + 64], BF16, "win")
            for kk in range(8):
                load_w_wide(win, win[:, kk, 0:INC], W["w_in"][l, kk * 128:(kk + 1) * 128, :])
                load_w(win, win[:, kk, INC:INC + 32], W["w_in"][l, kk * 128:(kk + 1) * 128, 4160:4192], scale=-1.0)
                load_w(win, win[:, kk, INC + 32:INC + 64], W["w_in"][l, kk * 128:(kk + 1) * 128, 4128:4160])
            wsm = k.sb([128, 8, 32], BF16, "wsm")
            for kk in range(8):
                load_w(wsm, wsm[:, kk, 0:16], W["w_in"][l, kk * 128:(kk + 1) * 128, 2048:2064])
                load_w(wsm, wsm[:, kk, 16:32], W["w_in"][l, kk * 128:(kk + 1) * 128, 3600:3616])
            gain = k.sb([128, 8], F32, "gain")
            load_gain(gain, W["mix_norm"][l], 8)
            hbs = k.ring(2, [128, 4, D], F32, "hb")
            xnr = k.ring(4, [128, D], BF16, "xn")
            uT = k.sb([128, 8, TT], BF16, "uT")
            ss = k.sb([128, 8], F32, "ss")
            rs = k.sb([128, 8], F32, "rs")
            stg = k.ring(4, [128, TT], BF16, "stg")
            zst = k.ring(2, [128, 4, 512], F32, "zst")
            smt = k.ring(2, [128, 4, 32], F32, "smt")
            ctr = [0]
            groups = [(g, g * 128) for g in range(16)] + [(16 + g, 2576 + g * 128) for g in range(8)] + \
                     [(24 + g, 3616 + g * 128) for g in range(4)] + [(28, 4128)]
            ns = 0
            for ti in range(NT // TT):
                t0 = ti * TT
                hb = hbs[ti % 2]
                load_h(src, t0, hb, f"hld{ti % 2}")
                norm_T(hb, gain, uT, xnr, ss, rs, ctr)
                for (rg, co) in groups:
                    pb = pget()
                    for kk in range(8):
                        k.op("pe", lambda e: e.matmul(pb[:], lhsT=win[:, kk, co:co + 128], rhs=uT[:, kk, :], start=(kk == 0), stop=(kk == 7)),
                             [win, uT], [pb])
                    st = stg[ns % 4]
                    if ns % 2:
                        k.op("dve", lambda e: e.tensor_copy(out=st[:], in_=pb[:]), [pb], [st])
                    else:
                        k.op("act", lambda e: e.activation(out=st[:], in_=pb[:], func=AF.Copy), [pb], [st])
                    k.dma("pool", featT[rg * 128:(rg + 1) * 128, t0:t0 + TT], st[:], [st], [], f"stg{ns % 4}")
                    ns += 1
                z = zst[ti % 2]
                sm = smt[ti % 2]
                for s in range(4):
                    pb = pget()
                    for kk in range(8):
                        k.op("pe", lambda e: e.matmul(pb[:], lhsT=uT[:, kk, s * 128:(s + 1) * 128], rhs=win[:, kk, 2064:2576], start=(kk == 0), stop=(kk == 7)),
                             [win, uT], [pb])
                    k.op("act", lambda e: e.activation(out=z[:, s, :], in_=pb[:], func=AF.Copy), [pb], [z])
                    pb = pget()
                    for kk in range(8):
                        k.op("pe", lambda e: e.matmul(pb[:, 0:32], lhsT=uT[:, kk, s * 128:(s + 1) * 128], rhs=wsm[:, kk, :], start=(kk == 0), stop=(kk == 7)),
                             [wsm, uT], [pb])
                    k.op("dve", lambda e: e.tensor_copy(out=sm[:, s, :], in_=pb[:, 0:32]), [pb], [sm])
                k.dma("pool", zs_tm[t0:t0 + TT, :].rearrange("(s p) d -> p s d", p=128), z[:], [z], [], f"zst{ti % 2}")
                k.dma("pool", sm_tm[t0:t0 + TT, :].rearrange("(s p) d -> p s d", p=128), sm[:], [sm], [], f"smt{ti % 2}")

    def scan_phase(l):
        with k.phase():
            cw_g = k.sb([128, 12, 5], F32, "cw_g")
            for j in range(5):
                k.dma("sp", cw_g[:, :, j], W["gdn_conv"][l, j].rearrange("(g p) -> p g", p=128), [], [cw_g], "cst1", allow_slow_non_contiguous=True)
            cw_s = k.sb([128, 8, 5], F32, "cw_s")
            for j in range(5):
                k.dma("sp", cw_s[:, :, j], W["ssd_conv"][l, j].rearrange("(g p) -> p g", p=128), [], [cw_s], "cst1", allow_slow_non_contiguous=True)
            dg_g = k.sb([128, 12, 5, 128], BF16, "dg_g")
            dg_s = k.sb([128, 8, 5, 128], BF16, "dg_s")
            for g in range(12):
                for j in range(5):
                    k.op("dve", lambda e: e.tensor_scalar(out=dg_g[:, g, j, :], in0=ident_f, scalar1=cw_g[:, g, j:j + 1], scalar2=None, op0=ALU.mult),
                         [cm, cw_g], [dg_g])
            for g in range(8):
                for j in range(5):
                    k.op("dve", lambda e: e.tensor_scalar(out=dg_s[:, g, j, :], in0=ident_f, scalar1=cw_s[:, g, j:j + 1], scalar2=None, op0=ALU.mult),
                         [cm, cw_s], [dg_s])
            cb_row = k.sb([1, 1024], BF16, "cb_row")
            load_w(cb_row, cb_row[:], W["ssd_conv_b"][l:l + 1, :])
            cb_pp = k.sb([128, 8], F32, "cb_pp")
            load_gain(cb_pp, W["ssd_conv_b"][l], 8)
            gA = k.sb([128, 8], F32, "gA")
            gdtb = k.sb([128, 8], F32, "gdtb")
            sA = k.sb([128, 16], F32, "sA")
            sdtb = k.sb([128, 16], F32, "sdtb")
            sD = k.sb([128, 8], F32, "sD")
            gng = k.sb([128, 1], F32, "gng")
            sng = k.sb([128, 512], F32, "sng")
            k.dma("sp", gA[:], W["gdn_A_log"][l].partition_broadcast(128), [], [gA], "cst1")
            k.dma("sp", gdtb[:], W["gdn_dt_bias"][l].partition_broadcast(128), [], [gdtb], "cst1")
            k.dma("sp", sA[:], W["ssd_A_log"][l].partition_broadcast(128), [], [sA], "cst1")
            k.dma("sp", sdtb[:], W["ssd_dt_bias"][l].partition_broadcast(128), [], [sdtb], "cst1")
            k.dma("sp", sD[:], W["ssd_D"][l].partition_broadcast(128), [], [sD], "cst1")
            k.dma("sp", sng[:], W["ssd_norm"][l].partition_broadcast(128), [], [sng], "cst1")
            k.dma("sp", gng[:], W["gdn_norm"][l].rearrange("(p o) -> p o", o=1), [], [gng], "cst1")
            for t in (gA, sA):
                k.op("act", lambda e, t=t: e.activation(out=t[:], in_=t[:], func=AF.Exp), [t], [t])
                k.op("dve", lambda e, t=t: e.tensor_scalar(out=t[:], in0=t[:], scalar1=-1.0, scalar2=None, op0=ALU.mult), [t], [t])
            ones_row = k.sb([1, 128], BF16, "ones_row")
            k.op("dve", lambda e: e.memset(ones_row[:], 1.0), [], [ones_row])

            R = 2
            pre_g = k.ring(R, [128, 12, 132], BF16, "pre_g")
            pre_s = k.ring(R, [128, 8, 132], BF16, "pre_s")
            abt = k.ring(R, [128, 32], F32, "abt")
            qkv = k.ring(R, [128, 12, 128], F32, "qkv")
            vbf = k.ring(R, [128, 4, 128], BF16, "vbf")
            sq = k.sb([128, 8, 128], F32, "sq")
            ssn = k.sb([128, 8], F32, "ssn")
            rinv = k.sb([128, 8], F32, "rinv")
            qkn = k.ring(R, [128, 8, 128], BF16, "qkn")
            gt = k.sb([128, 16], F32, "gt")
            gg = k.ring(R, [128, 8], F32, "gg")
            beta = k.ring(R, [128, 8], F32, "beta")
            nbeta = k.ring(R, [128, 8], F32, "nbeta")
            gcs = k.ring(R, [128, 4], F32, "gcs")
            egc = k.ring(R, [128, 4], F32, "egc")
            edl = k.ring(R, [128, 4], F32, "edl")
            egl = k.ring(R, [128, 4], F32, "egl")
            GL = k.ring(1, [128, 4, 128], F32, "GL")
            Dm = k.ring(1, [128, 4, 128], F32, "Dm")
            GLs = k.ring(1, [128, 8, 128], F32, "GLs")
            Dms = k.ring(1, [128, 8, 128], F32, "Dms")
            ss2 = k.sb([128, 4], F32, "ss2")
            rs2 = k.sb([128, 4], F32, "rs2")
            gst = {"i": 0}
            sst = {"i": 0}

            def pgG():
                b = banks[gst["i"] % 4]
                gst["i"] += 1
                return b

            def pgS():
                b = banks[4 + sst["i"] % 4]
                sst["i"] += 1
                return b
            DmI = k.sb([128, 4, 128], F32, "DmI")
            kqT = k.ring(R, [128, 4, 2, 128], BF16, "kqT")
            qg = k.sb([128, 4, 128], BF16, "qg")
            qgT = k.ring(R, [128, 4, 128], BF16, "qgT")
            kg = k.ring(R, [128, 4, 128], BF16, "kg")
            kd = k.ring(R, [128, 4, 128], BF16, "kd")
            At = k.ring(R, [128, 4, 128], BF16, "At")
            t1 = k.sb([128, 4, 128], F32, "t1")
            Qs = k.ring(2, [128, 4, 128], INV_DT, "Qs")
            QTs = k.ring(2, [128, 4, 128], INV_DT, "QTs")
            IQT = k.ring(2, [128, 4, 128], INV_DT, "IQT")
            Ts = k.ring(2, [128, 4, 128], INV_DT, "Ts")
            Xbs = k.ring(R, [128, 4, 128], BF16, "Xb")
            nWT = k.ring(R, [128, 4, 128], BF16, "nWT")
            vnew = k.ring(R, [128, 4, 128], BF16, "vnew")
            S32 = [k.sb([128, 4, 128], F32, "S32") for _ in range(2)]
            Sbf = [k.sb([128, 4, 128], BF16, "Sbf") for _ in range(2)]
            osb = k.ring(R, [128, 4, 128], F32, "osb")
            ofl = k.ring(R, [128, 4, 128], F32, "ofl")
            on = k.sb([128, 4, 128], BF16, "on")
            zT = k.ring(R, [128, 4, 128], BF16, "zT")
            sz = k.sb([128, 4, 128], F32, "sz")
            goT = k.ring(R, [128, 4, 128], BF16, "goT")
            xB = k.ring(R, [128, 6, 128], F32, "xB")
            Bbf = k.ring(R, [128, 2, 128], BF16, "Bbf")
            BCT = k.ring(R, [128, 4, 128], BF16, "BCT")
            dts = k.ring(R, [128, 16], F32, "dts")
            dA = k.ring(R, [128, 16], F32, "dA")
            acs = k.ring(R, [128, 8], F32, "acs")
            eac = k.ring(R, [128, 8], F32, "eac")
            dst_ = k.ring(R, [128, 8], F32, "dst")
            ecd = k.ring(R, [128, 8], F32, "ecd")
            Mt = k.ring(R, [128, 8, 128], BF16, "Mt")
            xdt = k.ring(R, [128, 8, 64], BF16, "xdt")
            xdd = k.ring(R, [128, 8, 64], BF16, "xdd")
            dd = k.sb([128, 8], F32, "dd")
            H32 = [k.sb([128, 512], F32, "H32") for _ in range(2)]
            Hbf = [k.sb([128, 512], BF16, "Hbf") for _ in range(2)]
            ysb = k.ring(R, [128, 512], F32, "ysb")
            yfl = k.ring(R, [128, 512], F32, "yfl")
            ytmp = k.sb([128, 512], F32, "ytmp")
            zsl = k.ring(R, [128, 512], F32, "zsl")
            ynb = k.sb([128, 512], BF16, "ynb")
            soT = k.ring(R, [128, 4, 128], BF16, "soT")
            ss1 = k.sb([128, 4], F32, "ss1")
            rs1 = k.sb([128, 4], F32, "rs1")
            cnt = {"g": 0, "s": 0}
            r_of = Res(acc=True)
            r_yf = Res(acc=True)

            def load_pre(buf, row0, ngrp, seq0, L, c, key):
                t0 = c * 128
                lo = max(t0 - 2, 0)
                hi = min(t0 + 130, L)
                if lo != t0 - 2 or hi != t0 + 130:
                    k.op("pool", lambda e: e.memset(buf[:], 0.0), [], [buf])
                k.dma("sp", buf[:, :, lo - (t0 - 2): hi - (t0 - 2)],
                      featT[row0:row0 + ngrp * 128, seq0 + lo: seq0 + hi].rearrange("(g p) t -> p g t", p=128), [], [buf], key)

            def gdn_chunk(seq0, L, c, d):
                i = cnt["g"]
                cnt["g"] += 1
                r = i % R
                tg = seq0 + c * 128
                pg_, ab, qv, vb = pre_g[r], abt[r], qkv[r], vbf[r]
                load_pre(pg_, 0, 12, seq0, L, c, f"preg{r}")
                k.dma("sp", ab[:], sm_tm[tg:tg + 128, :], [], [ab], f"ab{r}")
                for part in range(3):
                    pb = pgG()
                    for h in range(4):
                        g = part * 4 + h
                        for j in range(5):
                            k.op("pe", lambda e: e.matmul(pb[:, h * 128:(h + 1) * 128], lhsT=pg_[:, g, j:j + 128], rhs=dg_g[:, g, j, :],
                                                          start=(j == 0), stop=(j == 4)), [pg_, dg_g], [pb])
                    k.op("act", lambda e: e.activation(out=qv[:, part * 4:(part + 1) * 4, :], in_=pb[:].rearrange("p (h d) -> p h d", h=4), func=AF.Silu),
                         [pb], [qv])
                k.op("pool", lambda e: e.tensor_copy(out=vb[:], in_=qv[:, 8:12, :]), [qv], [vb])
                yield
                k.op("dve", lambda e: e.tensor_tensor(out=sq[:], in0=qv[:, 0:8, :], in1=qv[:, 0:8, :], op=ALU.mult), [qv], [sq])
                k.op("dve", lambda e: e.tensor_reduce(out=ssn[:], in_=sq[:], axis=AX.X, op=ALU.add), [sq], [ssn])
                rsqrt(rinv, rinv[:], ssn, ssn[:])
                k.op("dve", lambda e: e.tensor_scalar(out=rinv[:, 0:4], in0=rinv[:, 0:4], scalar1=128 ** -0.5, scalar2=None, op0=ALU.mult), [rinv], [rinv])
                qn = qkn[r]
                k.op("dve", lambda e: e.tensor_tensor(out=qn[:], in0=qv[:, 0:8, :], in1=bc3(rinv[:], 128), op=ALU.mult), [qv, rinv], [qn])
                yield
                g_, be, nbe = gg[r], beta[r], nbeta[r]
                k.op("dve", lambda e: e.tensor_tensor(out=gt[:, 0:8], in0=ab[:, 0:8], in1=gdtb[:], op=ALU.add), [ab, gdtb], [gt])
                k.op("act", lambda e: e.activation(out=gt[:, 0:8], in_=gt[:, 0:8], func=AF.Exp), [gt], [gt])
                k.op("act", lambda e: e.activation(out=gt[:, 0:8], in_=gt[:, 0:8], func=AF.Ln, bias=cst[:, 1:2]), [gt, cst], [gt])
                k.op("dve", lambda e: e.tensor_tensor(out=g_[:], in0=gt[:, 0:8], in1=gA[:], op=ALU.mult), [gt, gA], [g_])
                k.op("act", lambda e: e.activation(out=gt[:, 8:16], in_=ab[:, 8:16], func=AF.Exp, scale=-1.0), [ab], [gt])
                k.op("dve", lambda e: e.tensor_scalar(out=gt[:, 8:16], in0=gt[:, 8:16], scalar1=1.0, scalar2=None, op0=ALU.add), [gt], [gt])
                k.op("dve", lambda e: e.reciprocal(out=be[:], in_=gt[:, 8:16]), [gt], [be])
                k.op("dve", lambda e: e.tensor_scalar(out=nbe[:], in0=be[:], scalar1=-1.0, scalar2=None, op0=ALU.mult), [be], [nbe])
                gd = g_[:, d * 4:(d + 1) * 4]
                tri = cm[:, 1 + d, :]
                yield
                pb = pgG()
                k.op("pe", lambda e: e.matmul(pb[:, 0:4], lhsT=tri, rhs=gd, start=True, stop=True), [cm, g_], [pb])
                k.op("pe", lambda e: e.matmul(pb[:, 4:8], lhsT=onesf[:], rhs=gd, start=True, stop=True), [onesf, g_], [pb])
                gc_, eg, ed, el = gcs[r], egc[r], edl[r], egl[r]
                k.op("act", lambda e: e.activation(out=gc_[:], in_=pb[:, 0:4], func=AF.Copy), [pb], [gc_])
                k.op("act", lambda e: e.activation(out=eg[:], in_=pb[:, 0:4], func=AF.Exp), [pb], [eg])
                k.op("act", lambda e: e.activation(out=el[:], in_=pb[:, 4:8], func=AF.Exp), [pb], [el])
                k.op("dve", lambda e: e.tensor_tensor(out=ed[:], in0=pb[:, 4:8], in1=gc_[:], op=ALU.subtract), [pb, gc_], [ed])
                k.op("act", lambda e: e.activation(out=ed[:], in_=ed[:], func=AF.Exp), [ed], [ed])
                yield
                gl, dm = GL[0], Dm[0]
                k.op("dve", lambda e: e.tensor_tensor(out=gl[:], in0=bcm(tri, 4), in1=bc3(gd, 128), op=ALU.mult), [cm, g_], [gl])
                pb = pgG()
                k.op("pe", lambda e: e.matmul(pb[:], lhsT=onesf[:], rhs=gl[:].rearrange("p h i -> p (h i)"), start=True, stop=False), [onesf, gl], [pb])
                for h in range(4):
                    k.op("pe", lambda e: e.matmul(pb[:, h * 128:(h + 1) * 128], lhsT=gl[:, h, :], rhs=nonesf[:], start=False, stop=True), [gl, nonesf], [pb])
                k.op("dve", lambda e: e.tensor_tensor(out=dm[:], in0=pb[:].rearrange("p (h i) -> p h i", h=4), in1=bcm(cm[:, 3 + d, :], 4), op=ALU.add),
                     [pb, cm], [dm])
                k.op("act", lambda e: e.activation(out=dm[:], in_=dm[:], func=AF.Exp), [dm], [dm])
                k.op("pool", lambda e: e.tensor_tensor(out=DmI[:], in0=dm[:], in1=bcm(ident_f, 4), op=ALU.add), [dm, cm], [DmI])
                yield
                kg_, kd_ = kg[r], kd[r]
                k.op("pool", lambda e: e.tensor_tensor(out=qg[:], in0=qn[:, 0:4, :], in1=bc3(eg[:], 128), op=ALU.mult), [qn, eg], [qg])
                k.op("pool", lambda e: e.tensor_tensor(out=kg_[:], in0=qn[:, 4:8, :], in1=bc3(eg[:], 128), op=ALU.mult), [qn, eg], [kg_])
                k.op("pool", lambda e: e.tensor_tensor(out=kd_[:], in0=qn[:, 4:8, :], in1=bc3(ed[:], 128), op=ALU.mult), [qn, ed], [kd_])
                yield
                kq, qgt = kqT[r], qgT[r]
                pb = pgG()
                pv = pb[:].bitcast(BF16)
                for h in range(4):
                    k.op("pe", lambda e: e.transpose(out=pv[:, (2 * h) * 128:(2 * h + 1) * 128], in_=qn[:, 4 + h, :], identity=identb[:]), [qn, identb], [pb])
                    k.op("pe", lambda e: e.transpose(out=pv[:, (2 * h + 1) * 128:(2 * h + 2) * 128], in_=qn[:, h, :], identity=identb[:]), [qn, identb], [pb])
                k.op("act", lambda e: e.activation(out=kq[:].rearrange("p h t i -> p (h t i)"), in_=pv[:, 0:1024], func=AF.Copy), [pb], [kq])
                pb = pgG()
                pv = pb[:].bitcast(BF16)
                for h in range(4):
                    k.op("pe", lambda e: e.transpose(out=pv[:, h * 128:(h + 1) * 128], in_=qg[:, h, :], identity=identb[:]), [qg, identb], [pb])
                k.op("dve", lambda e: e.tensor_copy(out=qgt[:].rearrange("p h i -> p (h i)"), in_=pv[:, 0:512]), [pb], [qgt])
                yield
                pk = [pgG(), pgG()]
                for h in range(4):
                    b = pk[h // 2]
                    k.op("pe", lambda e: e.matmul(b[:, (h % 2) * 256:(h % 2 + 1) * 256], lhsT=kq[:, h, 0, :], rhs=kq[:, h, :, :].rearrange("p t i -> p (t i)"),
                                                  start=True, stop=True), [kq], [b])
                at = At[r]
                q0, q0t = Qs[0], QTs[0]
                for hh in range(2):
                    b = pk[hh]
                    b4 = b[:].rearrange("p (h t i) -> p h t i", h=2, t=2)
                    k.op("dve", lambda e: e.tensor_tensor(out=at[:, 2 * hh:2 * hh + 2, :], in0=b4[:, :, 1, :], in1=DmI[:, 2 * hh:2 * hh + 2, :], op=ALU.mult),
                         [b, DmI], [at])
                    k.op("dve", lambda e: e.tensor_tensor(out=t1[:, 2 * hh:2 * hh + 2, :], in0=b4[:, :, 0, :], in1=dm[:, 2 * hh:2 * hh + 2, :], op=ALU.mult),
                         [b, dm], [t1])
                nb_d = nbe[:, d * 4:(d + 1) * 4]
                k.op("dve", lambda e: e.tensor_tensor(out=q0[:], in0=t1[:], in1=bc3(nb_d, 128), op=ALU.mult), [t1, nbe], [q0])
                yield
                ident_inv = identb[:] if INV_DT == BF16 else ident_f
                pb = pgG()
                if INV_DT == BF16:
                    pv = pb[:].bitcast(BF16)
                else:
                    pv = pb[:]
                for h in range(4):
                    k.op("pe", lambda e: e.transpose(out=pv[:, h * 128:(h + 1) * 128], in_=q0[:, h, :], identity=ident_inv), [q0, identb, cm], [pb])
                k.op("act", lambda e: e.activation(out=q0t[:].rearrange("p h i -> p (h i)"), in_=pv[:, 0:512], func=AF.Copy), [pb], [q0t])
                T = Ts[0]
                k.op("pool", lambda e: e.tensor_tensor(out=T[:], in0=q0[:], in1=bcm(ident_f, 4), op=ALU.add), [q0, cm], [T])
                Qp, QTp = q0, q0t
                for lev in range(1, 7):
                    yield
                    Qn, QTn, iq, Tn = Qs[lev % 2], QTs[lev % 2], IQT[lev % 2], Ts[lev % 2]
                    pbt = pgG()
                    for h in range(4):
                        k.op("pe", lambda e: e.matmul(pbt[:, h * 128:(h + 1) * 128], lhsT=Qp[:, h, :], rhs=QTp[:, h, :], start=True, stop=True), [Qp, QTp], [pbt])
                    if lev < 6:
                        pbq = pgG()
                        for h in range(4):
                            k.op("pe", lambda e: e.matmul(pbq[:, h * 128:(h + 1) * 128], lhsT=QTp[:, h, :], rhs=Qp[:, h, :], start=True, stop=True), [Qp, QTp], [pbq])
                        k.op("act", lambda e: e.activation(out=Qn[:].rearrange("p h i -> p (h i)"), in_=pbq[:], func=AF.Copy), [pbq], [Qn])
                        k.op("act", lambda e: e.activation(out=QTn[:].rearrange("p h i -> p (h i)"), in_=pbt[:], func=AF.Copy), [pbt], [QTn])
                    k.op("dve", lambda e: e.tensor_tensor(out=iq[:], in0=pbt[:].rearrange("p (h i) -> p h i", h=4), in1=bcm(ident_f, 4), op=ALU.add),
                         [pbt, cm], [iq])
                    yield
                    pbT = pgG()
                    for h in range(4):
                        k.op("pe", lambda e: e.matmul(pbT[:, h * 128:(h + 1) * 128], lhsT=iq[:, h, :], rhs=T[:, h, :], start=True, stop=True), [iq, T], [pbT])
                    k.op("dve", lambda e: e.tensor_copy(out=Tn[:].rearrange("p h i -> p (h i)"), in_=pbT[:]), [pbT], [Tn])
                    T, Qp, QTp = Tn, Qn, QTn
                yield
                X = Xbs[r]
                k.op("pool", lambda e: e.tensor_copy(out=X[:], in_=T[:]), [T], [X])
                nw = nWT[r]
                pb = pgG()
                for h in range(4):
                    k.op("pe", lambda e: e.matmul(pb[:, h * 128:(h + 1) * 128], lhsT=kg_[:, h, :], rhs=X[:, h, :], start=True, stop=True), [kg_, X], [pb])
                k.op("act", lambda e: e.activation(out=nw[:].rearrange("p h i -> p (h i)"), in_=pb[:], func=AF.Copy, scale=-1.0), [pb], [nw])
                yield
                S3, Sb = S32[d], Sbf[d]
                first = (c == 0) if d == 0 else (c == L // 128 - 1)
                if first:
                    k.op("pool", lambda e: e.memset(S3[:], 0.0), [], [S3])
                    k.op("pool", lambda e: e.memset(Sb[:], 0.0), [], [Sb])
                pb = pgG()
                for h in range(4):
                    k.op("pe", lambda e: e.matmul(pb[:, h * 128:(h + 1) * 128], lhsT=X[:, h, :], rhs=vb[:, h, :], start=True, stop=False), [X, vb], [pb])
                    k.op("pe", lambda e: e.matmul(pb[:, h * 128:(h + 1) * 128], lhsT=nw[:, h, :], rhs=Sb[:, h, :], start=False, stop=True), [nw, Sb], [pb])
                vn = vnew[r]
                k.op("dve", lambda e: e.tensor_tensor(out=vn[:], in0=pb[:].rearrange("p (h i) -> p h i", h=4), in1=bc3(be[:, d * 4:(d + 1) * 4], 128), op=ALU.mult),
                     [pb, be], [vn])
                yield
                po = pgG()
                for h in range(4):
                    k.op("pe", lambda e: e.matmul(po[:, h * 128:(h + 1) * 128], lhsT=qgt[:, h, :], rhs=Sb[:, h, :], start=True, stop=False), [qgt, Sb], [po])
                    k.op("pe", lambda e: e.matmul(po[:, h * 128:(h + 1) * 128], lhsT=at[:, h, :], rhs=vn[:, h, :], start=False, stop=True), [at, vn], [po])
                pS = pgG()
                for h in range(4):
                    k.op("pe", lambda e: e.matmul(pS[:, h * 128:(h + 1) * 128], lhsT=kd_[:, h, :], rhs=vn[:, h, :], start=True, stop=True), [kd_, vn], [pS])
                k.op("dve", lambda e: e.tensor_tensor(out=S3[:], in0=S3[:], in1=bc3(el[:], 128), op=ALU.mult), [S3, el], [S3])
                k.op("dve", lambda e: e.tensor_tensor(out=S3[:], in0=S3[:], in1=pS[:].rearrange("p (h i) -> p h i", h=4), op=ALU.add), [S3, pS], [S3])
                k.op("act", lambda e: e.activation(out=Sb[:], in_=S3[:], func=AF.Copy), [S3], [Sb])
                yield
                o = osb[r]
                if d == 0:
                    k.op("act", lambda e: e.activation(out=o[:].rearrange("p h i -> p (h i)"), in_=po[:], func=AF.Copy), [po], [o])
                    k.dma("pool", of_tm[tg:tg + 128, :], o[:].rearrange("p h i -> p (h i)"), [o], [r_of], f"ofst{r}")
                    return
                ofb = ofl[r]
                k.dma("sp", ofb[:].rearrange("p h i -> p (h i)"), of_tm[tg:tg + 128, :], [r_of], [ofb], f"ofld{r}")
                k.op("dve", lambda e: e.tensor_tensor(out=o[:], in0=ofb[:], in1=po[:].rearrange("p (h i) -> p h i", h=4), op=ALU.add), [ofb, po], [o])
                yield
                k.op("dve", lambda e: e.tensor_tensor(out=sq[:, 0:4, :], in0=o[:], in1=o[:], op=ALU.mult), [o], [sq])
                k.op("dve", lambda e: e.tensor_reduce(out=ss1[:], in_=sq[:, 0:4, :], axis=AX.X, op=ALU.add), [sq], [ss1])
                rsqrt(rs1, rs1[:], ss1, ss1[:], scale=1.0 / 128)
                k.op("dve", lambda e: e.tensor_tensor(out=on[:], in0=o[:], in1=bc3(rs1[:], 128), op=ALU.mult), [o, rs1], [on])
                z_ = zT[r]
                k.dma("sp", z_[:], featT[1536:2048, tg:tg + 128].rearrange("(h p) t -> p h t", p=128), [], [z_], f"zT{r}")
                k.op("act", lambda e: e.activation(out=sz[:], in_=z_[:], func=AF.Silu), [z_], [sz])
                pb = pgG()
                pv = pb[:].bitcast(BF16)
                for h in range(4):
                    k.op("pe", lambda e: e.transpose(out=pv[:, h * 128:(h + 1) * 128], in_=on[:, h, :], identity=identb[:]), [on, identb], [pb])
                go = goT[r]
                k.op("dve", lambda e: e.scalar_tensor_tensor(out=go[:].rearrange("p h i -> p (h i)"), in0=pv[:, 0:512], scalar=gng[:, 0:1],
                                                             in1=sz[:].rearrange("p h i -> p (h i)"), op0=ALU.mult, op1=ALU.mult), [pb, gng, sz], [go])
                k.dma("pool", brT[0][:, tg:tg + 128].rearrange("(h p) t -> p h t", p=128), go[:], [go], [], f"gost{r}")

            def ssd_chunk(seq0, L, c, d):
                i = cnt["s"]
                cnt["s"] += 1
                r = i % R
                tg = seq0 + c * 128
                ps_, ab = pre_s[r], abt[r]
                load_pre(ps_, 2048, 8, seq0, L, c, f"pres{r}")
                xb, bb, bct = xB[r], Bbf[r], BCT[r]
                for part, (g0, ng) in enumerate(((0, 4), (4, 2))):
                    pb = pgS()
                    for gi in range(ng):
                        g = g0 + gi
                        for j in range(5):
                            k.op("pe", lambda e: e.matmul(pb[:, gi * 128:(gi + 1) * 128], lhsT=ps_[:, g, j:j + 128], rhs=dg_s[:, g, j, :],
                                                          start=(j == 0), stop=False), [ps_, dg_s], [pb])
                        k.op("pe", lambda e: e.matmul(pb[:, gi * 128:(gi + 1) * 128], lhsT=ones_row[0:1, :], rhs=cb_row[0:1, g * 128:(g + 1) * 128],
                                                      start=False, stop=True), [ones_row, cb_row], [pb])
                    k.op("act", lambda e: e.activation(out=xb[:, g0:g0 + ng, :], in_=pb[:, 0:ng * 128].rearrange("p (h d) -> p h d", h=ng), func=AF.Silu),
                         [pb], [xb])
                k.op("pool", lambda e: e.tensor_copy(out=bb[:], in_=xb[:, 4:6, :]), [xb], [bb])
                yield
                pb = pgS()
                for gi in range(4):
                    g = 4 + gi
                    for j in range(5):
                        k.op("pe", lambda e: e.matmul(pb[:, gi * 128:(gi + 1) * 128], lhsT=dg_s[:, g, j, :], rhs=ps_[:, g, j:j + 128],
                                                      start=(j == 0), stop=(j == 4)), [ps_, dg_s], [pb])
                for gi in range(4):
                    k.op("act", lambda e: e.activation(out=bct[:, gi, :], in_=pb[:, gi * 128:(gi + 1) * 128], func=AF.Silu, bias=cb_pp[:, 4 + gi:5 + gi]),
                         [pb, cb_pp], [bct])
                yield
                dt_, da = dts[r], dA[r]
                k.op("dve", lambda e: e.tensor_tensor(out=dt_[:], in0=ab[:, 16:32], in1=sdtb[:], op=ALU.add), [ab, sdtb], [dt_])
                k.op("act", lambda e: e.activation(out=dt_[:], in_=dt_[:], func=AF.Exp), [dt_], [dt_])
                k.op("act", lambda e: e.activation(out=dt_[:], in_=dt_[:], func=AF.Ln, bias=cst[:, 1:2]), [dt_, cst], [dt_])
                k.op("dve", lambda e: e.tensor_tensor(out=da[:], in0=dt_[:], in1=sA[:], op=ALU.mult), [dt_, sA], [da])
                dad = da[:, d * 8:(d + 1) * 8]
                dtd = dt_[:, d * 8:(d + 1) * 8]
                tri = cm[:, 1 + d, :]
                pb = pgS()
                k.op("pe", lambda e: e.matmul(pb[:, 0:8], lhsT=tri, rhs=dad, start=True, stop=True), [cm, da], [pb])
                k.op("pe", lambda e: e.matmul(pb[:, 8:16], lhsT=onesf[:], rhs=dad, start=True, stop=True), [onesf, da], [pb])
                ac, ea, ds, ec = acs[r], eac[r], dst_[r], ecd[r]
                k.op("act", lambda e: e.activation(out=ac[:], in_=pb[:, 0:8], func=AF.Copy), [pb], [ac])
                k.op("act", lambda e: e.activation(out=ea[:], in_=pb[:, 0:8], func=AF.Exp), [pb], [ea])
                k.op("act", lambda e: e.activation(out=ec[:], in_=pb[:, 8:16], func=AF.Exp), [pb], [ec])
                k.op("dve", lambda e: e.tensor_tensor(out=ds[:], in0=pb[:, 8:16], in1=ac[:], op=ALU.subtract), [pb, ac], [ds])
                k.op("act", lambda e: e.activation(out=ds[:], in_=ds[:], func=AF.Exp), [ds], [ds])
                yield
                gl, dm = GLs[0], Dms[0]
                k.op("dve", lambda e: e.tensor_tensor(out=gl[:], in0=bcm(tri, 8), in1=bc3(dad, 128), op=ALU.mult), [cm, da], [gl])
                pbs = [pgS(), pgS()]
                for hh in range(2):
                    b = pbs[hh]
                    k.op("pe", lambda e: e.matmul(b[:], lhsT=onesf[:], rhs=gl[:, hh * 4:(hh + 1) * 4, :].rearrange("p h i -> p (h i)"), start=True, stop=False),
                         [onesf, gl], [b])
                    for h in range(4):
                        k.op("pe", lambda e: e.matmul(b[:, h * 128:(h + 1) * 128], lhsT=gl[:, hh * 4 + h, :], rhs=nonesf[:], start=False, stop=True),
                             [gl, nonesf], [b])
                    k.op("dve", lambda e: e.tensor_tensor(out=dm[:, hh * 4:(hh + 1) * 4, :], in0=b[:].rearrange("p (h i) -> p h i", h=4),
                                                          in1=bcm(cm[:, 5 + d, :], 4), op=ALU.add), [b, cm], [dm])
                k.op("act", lambda e: e.activation(out=dm[:], in_=dm[:], func=AF.Exp), [dm], [dm])
                yield
                pb = pgS()
                for g in range(2):
                    k.op("pe", lambda e: e.matmul(pb[:, g * 128:(g + 1) * 128], lhsT=bct[:, g, :], rhs=bct[:, 2 + g, :], start=True, stop=True), [bct], [pb])
                mt = Mt[r]
                for g in range(2):
                    k.op("dve", lambda e: e.tensor_tensor(out=mt[:, g * 4:(g + 1) * 4, :], in0=dm[:, g * 4:(g + 1) * 4, :],
                                                          in1=bcm(pb[:, g * 128:(g + 1) * 128], 4), op=ALU.mult), [dm, pb], [mt])
                yield
                xd, xe = xdt[r], xdd[r]
                x3 = xb[:, 0:4, :].rearrange("p g (h2 q) -> p (g h2) q", h2=2)
                k.op("pool", lambda e: e.tensor_tensor(out=xd[:], in0=x3, in1=bc3(dtd, 64), op=ALU.mult), [xb, dt_], [xd])
                k.op("pool", lambda e: e.tensor_tensor(out=dd[:], in0=dtd, in1=ds[:], op=ALU.mult), [dt_, ds], [dd])
                k.op("pool", lambda e: e.tensor_tensor(out=xe[:], in0=x3, in1=bc3(dd[:], 64), op=ALU.mult), [xb, dd], [xe])
                H3, Hb = H32[d], Hbf[d]
                first = (c == 0) if d == 0 else (c == L // 128 - 1)
                if first:
                    k.op("pool", lambda e: e.memset(H3[:], 0.0), [], [H3])
                    k.op("pool", lambda e: e.memset(Hb[:], 0.0), [], [Hb])
                yield
                py = pgS()
                for h in range(8):
                    k.op("pe", lambda e: e.matmul(py[:, h * 64:(h + 1) * 64], lhsT=mt[:, h, :], rhs=xd[:, h, :], start=True, stop=True), [mt, xd], [py])
                pyo = pgS()
                for g in range(2):
                    k.op("pe", lambda e: e.matmul(pyo[:, g * 256:(g + 1) * 256], lhsT=bct[:, 2 + g, :], rhs=Hb[:, g * 256:(g + 1) * 256], start=True, stop=True),
                         [bct, Hb], [pyo])
                pst = pgS()
                for g in range(2):
                    k.op("pe", lambda e: e.matmul(pst[:, g * 256:(g + 1) * 256], lhsT=bb[:, g, :], rhs=xe[:, g * 4:(g + 1) * 4, :].rearrange("p h q -> p (h q)"),
                                                  start=True, stop=True), [bb, xe], [pst])
                yield
                y = ysb[r]
                k.op("dve", lambda e: e.tensor_tensor(out=ytmp[:].rearrange("p (h q) -> p h q", h=8), in0=pyo[:].rearrange("p (h q) -> p h q", h=8),
                                                      in1=bc3(ea[:], 64), op=ALU.mult), [pyo, ea], [ytmp])
                k.op("dve", lambda e: e.tensor_tensor(out=y[:], in0=ytmp[:], in1=py[:], op=ALU.add), [ytmp, py], [y])
                k.op("dve", lambda e: e.tensor_tensor(out=H3[:].rearrange("p (h q) -> p h q", h=8), in0=H3[:].rearrange("p (h q) -> p h q", h=8),
                                                      in1=bc3(ec[:], 64), op=ALU.mult), [H3, ec], [H3])
                k.op("dve", lambda e: e.tensor_tensor(out=H3[:], in0=H3[:], in1=pst[:], op=ALU.add), [H3, pst], [H3])
                k.op("act", lambda e: e.activation(out=Hb[:], in_=H3[:], func=AF.Copy), [H3], [Hb])
                yield
                if d == 0:
                    k.dma("pool", yf_tm[tg:tg + 128, :], y[:], [y], [r_yf], f"yfst{r}")
                    return
                yf = yfl[r]
                zl = zsl[r]
                k.dma("sp", yf[:], yf_tm[tg:tg + 128, :], [r_yf], [yf], f"yfld{r}")
                k.dma("sp", zl[:], zs_tm[tg:tg + 128, :], [], [zl], f"zsl{r}")
                k.op("dve", lambda e: e.tensor_tensor(out=y[:], in0=y[:], in1=yf[:], op=ALU.add), [y, yf], [y])
                k.op("pool", lambda e: e.tensor_tensor(out=ytmp[:].rearrange("p (h q) -> p h q", h=8), in0=x3, in1=bc3(sD[:], 64), op=ALU.mult), [xb, sD], [ytmp])
                k.op("dve", lambda e: e.tensor_tensor(out=y[:], in0=y[:], in1=ytmp[:], op=ALU.add), [y, ytmp], [y])
                k.op("act", lambda e: e.activation(out=zl[:], in_=zl[:], func=AF.Silu), [zl], [zl])
                k.op("dve", lambda e: e.tensor_tensor(out=y[:], in0=y[:], in1=zl[:], op=ALU.mult), [y, zl], [y])
                k.op("dve", lambda e: e.scalar_tensor_tensor(out=ytmp[:], in0=y[:], scalar=1.0, in1=y[:], op0=ALU.mult, op1=ALU.mult, accum_out=ss2[:, 0:1]),
                     [y], [ytmp, ss2])
                rsqrt(rs2, rs2[:, 0:1], ss2, ss2[:, 0:1], scale=1.0 / 512)
                k.op("dve", lambda e: e.scalar_tensor_tensor(out=ynb[:], in0=y[:], scalar=rs2[:, 0:1], in1=sng[:], op0=ALU.mult, op1=ALU.mult),
                     [y, rs2, sng], [ynb])
                pb = pgS()
                pv = pb[:].bitcast(BF16)
                for cc in range(4):
                    k.op("pe", lambda e: e.transpose(out=pv[:, cc * 128:(cc + 1) * 128], in_=ynb[:, cc * 128:(cc + 1) * 128], identity=identb[:]), [ynb, identb], [pb])
                so = soT[r]
                k.op("act", lambda e: e.activation(out=so[:].rearrange("p h i -> p (h i)"), in_=pv[:, 0:512], func=AF.Copy), [pb], [so])
                k.dma("pool", brT[1][:, tg:tg + 128].rearrange("(h p) t -> p h t", p=128), so[:], [so], [], f"sost{r}")

            seq0 = 0
            for L in seqs:
                nch = L // 128
                for d in range(2):
                    order = range(nch) if d == 0 else range(nch - 1, -1, -1)
                    for c in order:
                        tasks = [gdn_chunk(seq0, L, c, d), ssd_chunk(seq0, L, c, d)]
                        while tasks:
                            for t in list(tasks):
                                try:
                                    next(t)
                                except StopIteration:
                                    tasks.remove(t)
                seq0 += L

    def mla_phase(l):
        with k.phase():
            pset(range(8))
            wuq = k.sb([128, 2, 768 + 256], BF16, "wuq")
            wukv = k.sb([128, 2, 1024], BF16, "wukv")
            for c in range(2):
                load_w(wuq, wuq[:, c, 0:768], W["mla_w_uq"][l, c * 128:(c + 1) * 128, :])
                load_w(wukv, wukv[:, c, :], W["mla_w_ukv"][l, c * 128:(c + 1) * 128, :])
                for h in range(4):
                    load_w(wuq, wuq[:, c, 768 + h * 64:768 + h * 64 + 32], W["mla_w_uq"][l, c * 128:(c + 1) * 128, h * 192 + 160:h * 192 + 192], scale=-1.0)
                    load_w(wuq, wuq[:, c, 768 + h * 64 + 32:768 + h * 64 + 64], W["mla_w_uq"][l, c * 128:(c + 1) * 128, h * 192 + 128:h * 192 + 160])
            wv = k.sb([128, 2, 512], BF16, "wv")
            k.op("pool", lambda e: e.tensor_copy(out=wv[:].rearrange("p c (h x) -> p c h x", h=4),
                                                 in_=wukv[:].rearrange("p c (h x) -> p c h x", h=4)[:, :, :, 128:256]), [wukv], [wv])
            qg_ = k.sb([128, 2], F32, "qgain")
            kg_ = k.sb([128, 2], F32, "kvgain")
            load_gain(qg_, W["mla_q_norm"][l], 2)
            load_gain(kg_, W["mla_kv_norm"][l], 2)
            lat = k.ring(2, [128, 4, TT], BF16, "lat")
            krr = k.ring(2, [64, 2, TT], BF16, "krr")
            cs = k.ring(2, [64, 2, TT], F32, "cs")
            sqb = k.sb([128, 2, TT], BF16, "sqb")
            rb = k.sb([128, TT], F32, "rb")
            nrm = k.ring(2, [128, 4, TT], BF16, "nrm")
            stg = k.ring(4, [128, TT], BF16, "stg")
            r1 = k.sb([64, TT], F32, "r1")
            r2 = k.sb([64, TT], F32, "r2")
            rst = k.ring(4, [64, TT], BF16, "rst")
            vst = k.ring(2, [128, 4, 512], BF16, "vst")
            ns = 0
            nr = 0
            seq0 = 0
            ti = 0
            for L in seqs:
                for p0 in range(0, L, TT):
                    t0 = seq0 + p0
                    la, kr_, cs_ = lat[ti % 2], krr[ti % 2], cs[ti % 2]
                    k.dma("sp", la[:], featT[3072:3584, t0:t0 + TT].rearrange("(c p) t -> p c t", p=128), [], [la], f"lat{ti % 2}")
                    k.dma("sp", kr_[:], featT[3584:3712, t0:t0 + TT].rearrange("(c p) t -> p c t", p=64), [], [kr_], f"krr{ti % 2}")
                    k.dma("sp", cs_[:], ropecs[:, :, p0:p0 + TT].rearrange("c p t -> p c t"), [], [cs_], f"cs{ti % 2}")
                    nm = nrm[ti % 2]
                    for which in range(2):
                        gain = qg_ if which == 0 else kg_
                        k.op("pool", lambda e: e.tensor_tensor(out=sqb[:], in0=la[:, 2 * which:2 * which + 2, :], in1=la[:, 2 * which:2 * which + 2, :], op=ALU.mult),
                             [la], [sqb])
                        pb = pget()
                        for c in range(2):
                            k.op("pe", lambda e: e.matmul(pb[:], lhsT=onesb[:], rhs=sqb[:, c, :], start=(c == 0), stop=(c == 1)), [onesb, sqb], [pb])
                        rsqrt(rb, rb[:], pb, pb[:], scale=1.0 / 256)
                        for c in range(2):
                            k.op("dve", lambda e: e.scalar_tensor_tensor(out=nm[:, 2 * which + c, :], in0=la[:, 2 * which + c, :], scalar=gain[:, c:c + 1], in1=rb[:],
                                                                         op0=ALU.mult, op1=ALU.mult), [la, gain, rb], [nm])
                    for h in range(4):
                        pb = pget()
                        for c in range(2):
                            k.op("pe", lambda e: e.matmul(pb[:], lhsT=wuq[:, c, h * 192:h * 192 + 128], rhs=nm[:, c, :], start=(c == 0), stop=(c == 1)), [wuq, nm], [pb])
                        st = stg[ns % 4]
                        k.op("act", lambda e: e.activation(out=st[:], in_=pb[:], func=AF.Copy), [pb], [st])
                        k.dma("pool", qT_d[h, :, t0:t0 + TT], st[:], [st], [], f"mstg{ns % 4}")
                        ns += 1
                        pb = pget()
                        for c in range(2):
                            k.op("pe", lambda e: e.matmul(pb[:], lhsT=wukv[:, c, h * 256:h * 256 + 128], rhs=nm[:, 2 + c, :], start=(c == 0), stop=(c == 1)), [wukv, nm], [pb])
                        st = stg[ns % 4]
                        k.op("act", lambda e: e.activation(out=st[:], in_=pb[:], func=AF.Copy), [pb], [st])
                        k.dma("pool", kT_d[h, :, t0:t0 + TT], st[:], [st], [], f"mstg{ns % 4}")
                        ns += 1
                        pb = pget()
                        pb2 = pget()
                        for c in range(2):
                            k.op("pe", lambda e: e.matmul(pb[0:64, :], lhsT=wuq[:, c, h * 192 + 128:h * 192 + 192], rhs=nm[:, c, :], start=(c == 0), stop=(c == 1)), [wuq, nm], [pb])
                        for c in range(2):
                            k.op("pe", lambda e: e.matmul(pb2[0:64, :], lhsT=wuq[:, c, 768 + h * 64:768 + (h + 1) * 64], rhs=nm[:, c, :], start=(c == 0), stop=(c == 1)), [wuq, nm], [pb2])
                        k.op("dve", lambda e: e.tensor_tensor(out=r1[:], in0=pb[0:64, :], in1=cs_[:, 0, :], op=ALU.mult), [pb, cs_], [r1])
                        k.op("dve", lambda e: e.tensor_tensor(out=r2[:], in0=pb2[0:64, :], in1=cs_[:, 1, :], op=ALU.mult), [pb2, cs_], [r2])
                        rs_ = rst[nr % 4]
                        k.op("pool", lambda e: e.tensor_tensor(out=rs_[:], in0=r1[:], in1=r2[:], op=ALU.add), [r1, r2], [rs_])
                        k.dma("pool", qrT_d[h, :, t0:t0 + TT], rs_[:], [rs_], [], f"rstg{nr % 4}")
                        nr += 1
                    k.op("dve", lambda e: e.tensor_tensor(out=r1[:], in0=kr_[:, 0, :], in1=cs_[:, 0, :], op=ALU.mult), [kr_, cs_], [r1])
                    k.op("dve", lambda e: e.tensor_tensor(out=r2[:], in0=kr_[:, 1, :], in1=cs_[:, 1, :], op=ALU.mult), [kr_, cs_], [r2])
                    rs_ = rst[nr % 4]
                    k.op("pool", lambda e: e.tensor_tensor(out=rs_[:], in0=r1[:], in1=r2[:], op=ALU.add), [r1, r2], [rs_])
                    k.dma("pool", krT_d[:, t0:t0 + TT], rs_[:], [rs_], [], f"rstg{nr % 4}")
                    nr += 1
                    vs = vst[ti % 2]
                    for s in range(4):
                        pb = pget()
                        for c in range(2):
                            k.op("pe", lambda e: e.matmul(pb[:], lhsT=nm[:, 2 + c, s * 128:(s + 1) * 128], rhs=wv[:, c, :], start=(c == 0), stop=(c == 1)), [wv, nm], [pb])
                        k.op("act", lambda e: e.activation(out=vs[:, s, :], in_=pb[:], func=AF.Copy), [pb], [vs])
                    k.dma("pool", v_d[t0:t0 + TT, :].rearrange("(s p) d -> p s d", p=128), vs[:], [vs], [], f"vst{ti % 2}")
                    ti += 1
                seq0 += L
        with k.phase():
            pset([0, 1, 2])
            kT = k.ring(2, [128, LMAX], BF16, "kT")
            vv = k.ring(2, [128, LMAX // 128, 128], BF16, "vv")
            krT = k.ring(2, [64, LMAX], BF16, "krT")
            qt = k.ring(2, [128, TT], BF16, "qt")
            qrt = k.ring(2, [64, TT], BF16, "qrt")
            pt = k.ring(3, [128, TT], BF16, "pt")
            rl = k.sb([128, TT], F32, "rl")
            ot = k.ring(2, [128, TT], BF16, "ot")
            seq0 = 0
            nq = 0
            npb = 0
            for si, L in enumerate(seqs):
                nkb = L // 128
                kr_ = krT[si % 2]
                k.dma("sp", kr_[:, 0:L], krT_d[:, seq0:seq0 + L], [], [kr_], f"krT{si % 2}")
                for h in range(4):
                    hh = (si * 4 + h) % 2
                    kt_, v_ = kT[hh], vv[hh]
                    k.dma("sp", kt_[:, 0:L], kT_d[h, :, seq0:seq0 + L], [], [kt_], f"kT{hh}")
                    for b0 in range(0, nkb, 8):
                        b1 = min(b0 + 8, nkb)
                        k.dma("sp", v_[:, b0:b1, :], v_d[seq0 + b0 * 128:seq0 + b1 * 128, h * 128:(h + 1) * 128].rearrange("(n p) d -> p n d", p=128), [], [v_], f"vv{hh}")
                    for p0 in range(0, L, TT):
                        t0 = seq0 + p0
                        q_, qr_ = qt[nq % 2], qrt[nq % 2]
                        k.dma("sp", q_[:], qT_d[h, :, t0:t0 + TT], [], [q_], f"qt{nq % 2}")
                        k.dma("sp", qr_[:], qrT_d[h, :, t0:t0 + TT], [], [qr_], f"qrt{nq % 2}")
                        po = banks[3 + 2 * (nq % 2)]
                        pl = banks[4 + 2 * (nq % 2)]
                        def s_step(kb):
                            ps_ = pget()
                            k.op("pe", lambda e: e.matmul(ps_[:], lhsT=kt_[:, kb * 128:(kb + 1) * 128], rhs=q_[:], start=True, stop=False), [kt_, q_], [ps_])
                            k.op("pe", lambda e: e.matmul(ps_[:], lhsT=kr_[:, kb * 128:(kb + 1) * 128], rhs=qr_[:], start=False, stop=True), [kr_, qr_], [ps_])
                            p_ = pt[s_step.n % 3]
                            s_step.n += 1
                            k.op("act", lambda e: e.activation(out=p_[:], in_=ps_[:], func=AF.Exp, scale=MLA_SCALE), [ps_], [p_])
                            return p_

                        def pv_step(kb, p_):
                            k.op("pe", lambda e: e.matmul(po[:], lhsT=v_[:, kb, :], rhs=p_[:], start=(kb == 0), stop=(kb == nkb - 1)), [v_, p_], [po])
                            k.op("pe", lambda e: e.matmul(pl[:], lhsT=onesb[:], rhs=p_[:], start=(kb == 0), stop=(kb == nkb - 1)), [onesb, p_], [pl])
                        s_step.n = npb
                        pend = [s_step(0)]
                        for kb in range(nkb):
                            if kb + 1 < nkb:
                                pend.append(s_step(kb + 1))
                            pv_step(kb, pend.pop(0))
                        npb = s_step.n
                        k.op("dve", lambda e: e.reciprocal(out=rl[:], in_=pl[:]), [pl], [rl])
                        o_ = ot[nq % 2]
                        k.op("dve", lambda e: e.tensor_tensor(out=o_[:], in0=po[:], in1=rl[:], op=ALU.mult), [po, rl], [o_])
                        k.dma("pool", brT[2][h * 128:(h + 1) * 128, t0:t0 + TT], o_[:], [o_], [], f"ot{nq % 2}")
                        nq += 1
                seq0 += L

    def merge_phase(l, src, dst):
        with k.phase():
            pset(range(8))
            wg = k.sb([128, 8, 3 * D], BF16, "wg")
            for kk in range(8):
                load_w_wide(wg, wg[:, kk, :], W["w_gate"][l, kk * 128:(kk + 1) * 128, :])
            wbr = k.sb([128, 3, 4, D], BF16, "wbr")
            for bi, wn in enumerate(("w_branch_a", "w_branch_b", "w_branch_c")):
                for kk in range(4):
                    load_w(wbr, wbr[:, bi, kk, :], W[wn][l, kk * 128:(kk + 1) * 128, :])
            wo = k.sb([128, 8, D], BF16, "wo")
            for kk in range(8):
                load_w(wo, wo[:, kk, :], W["w_out"][l, kk * 128:(kk + 1) * 128, :])
            gain = k.sb([128, 8], F32, "gain")
            load_gain(gain, W["mix_norm"][l], 8)
            bg = k.sb([128, 24], F32, "bg")
            k.dma("sp", bg[:], W["b_gate"][l].rearrange("(kk p) -> p kk", p=128), [], [bg], "gain", allow_slow_non_contiguous=True)
            hbs = k.ring(2, [128, 4, D], F32, "hb")
            xnr = k.ring(4, [128, D], BF16, "xn")
            uT = k.sb([128, 8, TT], BF16, "uT")
            ss = k.sb([128, 8], F32, "ss")
            rs = k.sb([128, 8], F32, "rs")
            brs = k.ring(2, [128, 3, 4, TT], BF16, "brs")
            gts = k.ring(3, [128, TT], F32, "gts")
            macc = k.sb([128, TT], F32, "macc")
            mt2 = k.sb([128, TT], F32, "mt2")
            mT = k.sb([128, 8, TT], BF16, "mT")
            ctr = [0]
            ng = 0
            for ti in range(NT // TT):
                t0 = ti * TT
                hb = hbs[ti % 2]
                load_h(src, t0, hb, f"hld{ti % 2}")
                br = brs[ti % 2]
                for bi in range(3):
                    k.dma("sp", br[:, bi, :, :], brT[bi][:, t0:t0 + TT].rearrange("(c p) t -> p c t", p=128), [], [br], f"brs{ti % 2}")
                norm_T(hb, gain, uT, xnr, ss, rs, ctr)
                for c in range(8):
                    for bi in range(3):
                        pgt = pget()
                        for kk in range(8):
                            k.op("pe", lambda e: e.matmul(pgt[:], lhsT=wg[:, kk, bi * D + c * 128: bi * D + (c + 1) * 128], rhs=uT[:, kk, :], start=(kk == 0), stop=(kk == 7)),
                                 [wg, uT], [pgt])
                        py = pget()
                        for kk in range(4):
                            k.op("pe", lambda e: e.matmul(py[:], lhsT=wbr[:, bi, kk, c * 128:(c + 1) * 128], rhs=br[:, bi, kk, :], start=(kk == 0), stop=(kk == 3)),
                                 [wbr, br], [py])
                        gt_ = gts[ng % 3]
                        ng += 1
                        k.op("act", lambda e: e.activation(out=gt_[:], in_=pgt[:], func=AF.Sigmoid, bias=bg[:, bi * 8 + c: bi * 8 + c + 1]), [pgt, bg], [gt_])
                        if bi == 0:
                            k.op("dve", lambda e: e.tensor_tensor(out=macc[:], in0=gt_[:], in1=py[:], op=ALU.mult), [gt_, py], [macc])
                        elif bi == 1:
                            k.op("dve", lambda e: e.tensor_tensor(out=mt2[:], in0=gt_[:], in1=py[:], op=ALU.mult), [gt_, py], [mt2])
                            k.op("pool", lambda e: e.tensor_tensor(out=macc[:], in0=macc[:], in1=mt2[:], op=ALU.add), [macc, mt2], [macc])
                        else:
                            k.op("dve", lambda e: e.tensor_tensor(out=mt2[:], in0=gt_[:], in1=py[:], op=ALU.mult), [gt_, py], [mt2])
                            k.op("pool", lambda e: e.tensor_tensor(out=mT[:, c, :], in0=macc[:], in1=mt2[:], op=ALU.add), [macc, mt2], [mT])
                for s in range(4):
                    for fh in range(2):
                        po = pget()
                        for c in range(8):
                            k.op("pe", lambda e: e.matmul(po[:], lhsT=mT[:, c, s * 128:(s + 1) * 128], rhs=wo[:, c, fh * 512:(fh + 1) * 512], start=(c == 0), stop=(c == 7)),
                                 [mT, wo], [po])
                        k.op("dve", lambda e: e.tensor_tensor(out=hb[:, s, fh * 512:(fh + 1) * 512], in0=po[:], in1=hb[:, s, fh * 512:(fh + 1) * 512], op=ALU.add),
                             [po, hb], [hb])
                k.dma("pool", dst[t0:t0 + TT, :].rearrange("(s p) d -> p s d", p=128), hb[:], [hb], [], f"hst{ti % 2}")

    import os
    stopat = int(os.environ.get("STOPAT", "999"))
    phases = []
    cur = xin
    for l in range(DEPTH):
        last = (l == DEPTH - 1)
        phases.append(lambda l=l, cur=cur: ffn_phase(l, 0, cur, hA, False))
        phases.append(lambda l=l: m1_phase(l, hA))
        phases.append(lambda l=l: scan_phase(l))
        phases.append(lambda l=l: mla_phase(l))
        phases.append(lambda l=l: merge_phase(l, hA, hB))
        phases.append(lambda l=l, last=last: ffn_phase(l, 1, hB, yout if last else hA, last))
        cur = hA
    for i, ph in enumerate(phases):
        if i >= stopat:
            break
        ph()
    if stopat < 999:
        scr = {"hA": hA, "hB": hB, "featT": featT, "zs_tm": zs_tm, "sm_tm": sm_tm, "of_tm": of_tm, "yf_tm": yf_tm,
               "brT0": brT[0], "brT1": brT[1], "brT2": brT[2], "krT_d": krT_d, "v_d": v_d, "qT_d": qT_d.rearrange("h p t -> (h p) t"), "qrT_d": qrT_d.rearrange("h p t -> (h p) t"), "kT_d": kT_d.rearrange("h p t -> (h p) t")}
        for n, ap in scr.items():
            pr = nc.dram_tensor("p_" + n, list(ap.shape), ap.dtype, kind="ExternalOutput").ap()
            for r0 in range(0, ap.shape[0], 128):
                r1 = min(r0 + 128, ap.shape[0])
                k.dma("sp", pr[r0:r1], ap[r0:r1], [], [], "probe")
    if stopat < 999:
        for n, b, shp, dt_ in (("onesb", onesb, [128, 128], BF16), ("identb", identb, [128, 128], BF16), ("onesf", onesf, [128, 128], F32),
                               ("cst", cst, [128, 4], F32), ("cm", cm, [128, 7 * 128], F32)):
            pr = nc.dram_tensor("p_" + n, shp, dt_, kind="ExternalOutput").ap()
            src = b[:] if n != "cm" else b[:].rearrange("p a b -> p (a b)")
            k.dma("sp", pr, src, [b], [], "probe")
    k.barrier()
    return nc, k


def _consts(LMAX):
    idx = np.arange(128)
    j = idx[:, None]
    i = idx[None, :]
    cm = np.zeros((128, 7, 128), np.float32)
    cm[:, 0] = (j == i)
    cm[:, 1] = (j <= i)
    cm[:, 2] = (j >= i)
    NEG = -1e30
    cm[:, 3] = np.where(i > j, 0.0, NEG)
    cm[:, 4] = np.where(i < j, 0.0, NEG)
    cm[:, 5] = np.where(i >= j, 0.0, NEG)
    cm[:, 6] = np.where(i <= j, 0.0, NEG)
    inv_freq = np.power(np.float32(10000.0), -np.arange(0, 64, 2, dtype=np.float32) / np.float32(64)).astype(np.float32)
    ang = np.arange(LMAX, dtype=np.float32)[:, None] * inv_freq[None, :]
    ang = np.concatenate([ang, ang], axis=-1).astype(np.float32)
    rc = np.stack([np.cos(ang).T, np.sin(ang).T]).astype(np.float32)
    return cm, np.ascontiguousarray(rc)


_CACHE = {}


def run(seq_lists, x_per_core, weights):
    seqs = tuple(seq_lists)
    if seqs not in _CACHE:
        _CACHE[seqs] = kernel_build(list(seqs))
    nc, _ = _CACHE[seqs]
    cm, rc = _consts(max(seqs))
    wm = {}
    for n, a in weights.items():
        a = np.asarray(a, np.float32)
        if n in ("gdn_A_log", "gdn_dt_bias"):
            a = a.reshape(DEPTH, 8)
        elif n in ("ssd_A_log", "ssd_dt_bias"):
            a = a.reshape(DEPTH, 16)
        elif n == "final_norm":
            a = a.reshape(1, D)
        wm[n] = np.ascontiguousarray(a)
    in_maps = []
    for x in x_per_core:
        m = dict(wm)
        m["xin"] = np.ascontiguousarray(x, dtype=np.float32)
        m["cmask"] = cm
        m["ropecs"] = rc
        in_maps.append(m)
    res = run_bass_kernel_spmd(nc, in_maps, core_ids=list(range(len(x_per_core))))
    return [r["yout"] for r in res.results]


def kernel(x_prompt, x_sample, **weights):
    x_prompt = np.asarray(x_prompt, np.float32)
    x_sample = np.asarray(x_sample, np.float32)
    B, S, _ = x_prompt.shape
    DB, DS, _ = x_sample.shape
    n = N_CORES
    pp = B // n
    sp = DB // n
    seqs = [S] * pp + [DS] * sp
    xs = []
    for c in range(n):
        parts = [x_prompt[c * pp + i] for i in range(pp)] + [x_sample[c * sp + i] for i in range(sp)]
        xs.append(np.concatenate(parts, axis=0))
    outs = run(seqs, xs, weights)
    y_prompt = np.empty_like(x_prompt)
    y_sample = np.empty_like(x_sample)
    for c in range(n):
        o = outs[c]
        off = 0
        for i in range(pp):
            y_prompt[c * pp + i] = o[off:off + S]
            off += S
        for i in range(sp):
            y_sample[c * sp + i] = o[off:off + DS]
            off += DS
    return (y_prompt, y_sample)
```

```python
import math
import numpy as np
from contextlib import ExitStack, contextmanager
import concourse.bass as bass
import concourse.mybir as mybir
from concourse.bass_utils import run_bass_kernel_spmd

F32 = mybir.dt.float32
BF16 = mybir.dt.bfloat16
AF = mybir.ActivationFunctionType
ALU = mybir.AluOpType
AX = mybir.AxisListType

D = 1024
FH = 2816
NJ = FH // 128
DEPTH = 2
INC = 4192
EPS = 1e-6
MLA_SCALE = 192 ** -0.5
STRICT = True
INV_DT = F32
N_CORES = 8
TT = 512
SCRATCH_EXTERNAL = False
STORE_Q = "sp"


class Res:
    def __init__(self, acc=False):
        self.w = None
        self.r = {}
        self.acc = acc


class Buf(Res):
    def __init__(self, t):
        super().__init__()
        self.t = t

    def __getitem__(self, key):
        return self.t[key]


class K:
    def __init__(self, nc):
        self.nc = nc
        self.es = ExitStack()
        self.eng = {"pe": nc.tensor, "act": nc.scalar, "dve": nc.vector, "pool": nc.gpsimd, "sp": nc.sync}
        self.semh = {}
        self.cnt = {}
        for e in self.eng:
            self.semh[e] = self.es.enter_context(nc.semaphore("se_" + e))
            self.cnt[e] = 0
        self.seen = {e: {} for e in self.eng}
        self.dtot = {}
        self.nops = 0
        self.uid = 0

    @contextmanager
    def phase(self):
        st = ExitStack()
        self._ph = st
        try:
            yield st
        finally:
            self.barrier()
            st.close()

    def sb(self, shape, dt, name=None, glob=False):
        self.uid += 1
        t = (self.es if glob else self._ph).enter_context(
            self.nc.sbuf_tensor(f"{name or 'sb'}_{self.uid}", list(shape), dt))
        return Buf(t)

    def ring(self, n, shape, dt, name=None):
        return [self.sb(shape, dt, name) for _ in range(n)]

    def _dsem(self, key):
        if key not in self.semh:
            self.semh[key] = self.es.enter_context(self.nc.semaphore("sd_" + key))
            self.dtot[key] = 0
        return self.semh[key]

    def _deps(self, E, reads, writes, skip=None):
        evs = {}

        def add(d):
            for k, v in d.items():
                if evs.get(k, 0) < v:
                    evs[k] = v
        for r in reads:
            if r.w:
                add(r.w)
        for r in writes:
            if r.w:
                add(r.w)
            add(r.r)
        if E == "pe" or (not STRICT and E in ("act", "dve")):
            evs.pop(E, None)
        if skip:
            evs.pop(skip, None)
        return evs

    def _wait(self, E, evs):
        for key, val in evs.items():
            if key in self.dtot:
                val = self.dtot[key]
            if self.seen[E].get(key, 0) < val:
                self.eng[E].wait_ge(self.semh[key], val)
                self.seen[E][key] = val

    def op(self, E, fn, reads=(), writes=()):
        self._wait(E, self._deps(E, reads, writes))
        ins = fn(self.eng[E])
        self.cnt[E] += 1
        c = self.cnt[E]
        ins.then_inc(self.semh[E], 1)
        self.nops += 1
        for r in reads:
            if r.r.get(E, 0) < c:
                r.r[E] = c
        for r in writes:
            r.w = {E: c}
            r.r = {}

    def dma(self, Q, out, in_, reads, writes, key, **kw):
        if Q == "pool" and out.dtype == in_.dtype:
            Q = STORE_Q
        self._dsem(key)
        self._wait(Q, self._deps(Q, reads, writes, skip=key))
        ins = self.eng[Q].dma_start(out=out, in_=in_, **kw)
        self.dtot[key] += 16
        v = self.dtot[key]
        ins.then_inc(self.semh[key], 16)
        self.nops += 1
        for r in reads:
            r.r[key] = v
        for r in writes:
            if getattr(r, "acc", False):
                r.w = dict(r.w or {})
                r.w[key] = v
            else:
                r.w = {key: v}
                r.r = {}

    def barrier(self):
        tot = {e: self.cnt[e] for e in self.eng}
        tot.update(self.dtot)
        for E in self.eng:
            ev = dict(tot)
            ev.pop(E, None)
            self._wait(E, ev)


def kernel_build(seqs, want_debug=False):
    NT = sum(seqs)
    LMAX = max(seqs)
    nc = bass.Bass("TRN2", target_bir_lowering=False)
    k = K(nc)
    P = 128

    def din(name, shape, dt=F32):
        return nc.dram_tensor(name, list(shape), dt, kind="ExternalInput").ap()

    def dscr(name, shape, dt):
        ext = want_debug or SCRATCH_EXTERNAL
        return nc.dram_tensor(name, list(shape), dt, kind="ExternalOutput" if ext else "Internal").ap()

    xin = din("xin", [NT, D])
    yout = nc.dram_tensor("yout", [NT, D], F32, kind="ExternalOutput").ap()
    W = {}
    wshapes = {
        "ffn1_norm": [DEPTH, D], "w_ffn1_in": [DEPTH, D, 2 * FH], "w_ffn1_out": [DEPTH, FH, D],
        "mix_norm": [DEPTH, D], "w_in": [DEPTH, D, INC], "gdn_conv": [DEPTH, 5, 1536],
        "gdn_A_log": [DEPTH, 8], "gdn_dt_bias": [DEPTH, 8], "gdn_norm": [DEPTH, 128],
        "ssd_conv": [DEPTH, 5, 1024], "ssd_conv_b": [DEPTH, 1024], "ssd_A_log": [DEPTH, 16],
        "ssd_dt_bias": [DEPTH, 16], "ssd_D": [DEPTH, 8], "ssd_norm": [DEPTH, 512],
        "mla_q_norm": [DEPTH, 256], "mla_w_uq": [DEPTH, 256, 768], "mla_kv_norm": [DEPTH, 256],
        "mla_w_ukv": [DEPTH, 256, 1024], "w_branch_a": [DEPTH, 512, D], "w_branch_b": [DEPTH, 512, D],
        "w_branch_c": [DEPTH, 512, D], "w_gate": [DEPTH, D, 3 * D], "b_gate": [DEPTH, 3 * D],
        "w_out": [DEPTH, D, D], "ffn2_norm": [DEPTH, D], "w_ffn2_in": [DEPTH, D, 2 * FH],
        "w_ffn2_out": [DEPTH, FH, D], "final_norm": [1, D],
    }
    for n, s in wshapes.items():
        W[n] = din(n, s)
    cmask = din("cmask", [128, 7, 128])
    ropecs = din("ropecs", [2, 64, LMAX])

    hA = dscr("hA", [NT, D], F32)
    hB = dscr("hB", [NT, D], F32)
    featT = dscr("featT", [3712, NT], BF16)
    zs_tm = dscr("zs_tm", [NT, 512], F32)
    sm_tm = dscr("sm_tm", [NT, 32], F32)
    of_tm = dscr("of_tm", [NT, 512], F32)
    yf_tm = dscr("yf_tm", [NT, 512], F32)
    brT = [dscr(f"brT{i}", [512, NT], BF16) for i in range(3)]
    qT_d = dscr("qT_d", [4, 128, NT], BF16)
    qrT_d = dscr("qrT_d", [4, 64, NT], BF16)
    kT_d = dscr("kT_d", [4, 128, NT], BF16)
    krT_d = dscr("krT_d", [64, NT], BF16)
    v_d = dscr("v_d", [NT, 512], BF16)
    wffn_bf = dscr("wffn_bf", [DEPTH * 2, NJ, 128, 8 * 256], BF16)

    cm = k.sb([128, 7, 128], F32, "cm", glob=True)
    identb = k.sb([128, 128], BF16, "identb", glob=True)
    onesf = k.sb([128, 128], F32, "onesf", glob=True)
    nonesf = k.sb([128, 128], F32, "nonesf", glob=True)
    onesb = k.sb([128, 128], BF16, "onesb", glob=True)
    cst = k.sb([128, 4], F32, "cst", glob=True)
    banks = []
    for i in range(8):
        banks.append(Buf(k.es.enter_context(nc.psum_tensor(f"psb{i}", [128, 512], F32))))
    pstate = {"i": 0, "set": list(range(8))}

    def pget():
        s = pstate["set"]
        b = banks[s[pstate["i"] % len(s)]]
        pstate["i"] += 1
        return b

    def pset(lst):
        pstate["set"] = list(lst)
        pstate["i"] = 0

    k.dma("sp", cm[:], cmask, [], [cm], "cm")
    k.op("dve", lambda e: e.memset(onesf[:], 1.0), [], [onesf])
    k.op("dve", lambda e: e.memset(nonesf[:], -1.0), [], [nonesf])
    k.op("dve", lambda e: e.memset(onesb[:], 1.0), [], [onesb])
    k.op("dve", lambda e: e.memset(cst[:, 0:1], EPS), [], [cst])
    k.op("dve", lambda e: e.memset(cst[:, 1:2], 1.0), [], [cst])
    k.op("dve", lambda e: e.memset(cst[:, 2:3], 0.0), [], [cst])
    k.op("dve", lambda e: e.tensor_copy(out=identb[:], in_=cm[:, 0, :]), [cm], [identb])
    ident_f = cm[:, 0, :]

    def bc3(ap2, n):
        return ap2.unsqueeze(2).to_broadcast([ap2.shape[0], ap2.shape[1], n])

    def bcm(ap2, n):
        return ap2.unsqueeze(1).to_broadcast([ap2.shape[0], n, ap2.shape[1]])

    def rsqrt(out_buf, out_ap, in_buf, in_ap, scale=1.0):
        k.op("dve", lambda e: e.tensor_scalar(out=out_ap, in0=in_ap, scalar1=scale, scalar2=EPS, op0=ALU.mult, op1=ALU.add),
             [in_buf], [out_buf])
        k.op("act", lambda e: e.activation(out=out_ap, in_=out_ap, func=AF.Ln), [out_buf], [out_buf])
        k.op("act", lambda e: e.activation(out=out_ap, in_=out_ap, func=AF.Exp, scale=-0.5), [out_buf], [out_buf])

    wst = [k.sb([128, 1024], F32, "wst", glob=True) for _ in range(3)]
    wctr = [0]

    def load_w(dst_buf, dst_ap, src_ap, scale=None, view=None):
        i = wctr[0]
        wctr[0] += 1
        stg_ = wst[i % 3]
        if view is None:
            np_, n = src_ap.shape
            sv = stg_[0:np_, 0:n]
        else:
            sv = view(stg_)
        k.dma("sp", sv, src_ap, [], [stg_], f"wst{i % 3}")
        if scale is not None:
            k.op("dve", lambda e: e.tensor_scalar(out=dst_ap, in0=sv, scalar1=scale, scalar2=None, op0=ALU.mult), [stg_], [dst_buf])
        elif i % 3 == 0:
            k.op("act", lambda e: e.activation(out=dst_ap, in_=sv, func=AF.Copy), [stg_], [dst_buf])
        elif i % 3 == 1:
            k.op("dve", lambda e: e.tensor_copy(out=dst_ap, in_=sv), [stg_], [dst_buf])
        else:
            k.op("pool", lambda e: e.tensor_copy(out=dst_ap, in_=sv), [stg_], [dst_buf])

    def load_w_wide(dst_buf, dst2d, src2d):
        n = src2d.shape[1]
        for c0 in range(0, n, 1024):
            c1 = min(c0 + 1024, n)
            load_w(dst_buf, dst2d[:, c0:c1], src2d[:, c0:c1])

    with k.phase():
        stg = k.ring(3, [128, 8, 256], BF16, "wstg")
        n = 0
        for l in range(DEPTH):
            for f, wn in enumerate(("w_ffn1_in", "w_ffn2_in")):
                for j in range(NJ):
                    b = stg[n % 3]
                    n += 1
                    for half in range(2):
                        src = W[wn][l, :, half * FH + j * 128: half * FH + (j + 1) * 128].rearrange("(kk p) n -> p kk n", p=128)
                        load_w(b, b[:, :, half * 128:(half + 1) * 128], src, view=lambda t: t[:, 0:1024].rearrange("p (kk n) -> p kk n", kk=8))
                    k.dma("sp", wffn_bf[l * 2 + f, j].rearrange("p (kk n) -> p kk n", kk=8), b[:], [b], [], f"wstg_o{(n - 1) % 3}")

    def load_h(src, t0, hb, key):
        k.dma("sp", hb[:], src[t0:t0 + TT, :].rearrange("(s p) d -> p s d", p=128), [], [hb], key)

    def norm_A(hb, xn_ring, ss, rs, ctr):
        for s in range(4):
            xn = xn_ring[ctr[0] % len(xn_ring)]
            ctr[0] += 1
            k.op("dve", lambda e: e.scalar_tensor_tensor(out=xn[:], in0=hb[:, s, :], scalar=1.0, in1=hb[:, s, :],
                                                         op0=ALU.mult, op1=ALU.mult, accum_out=ss[:, s:s + 1]),
                 [hb], [xn, ss])
        rsqrt(rs, rs[:, 0:4], ss, ss[:, 0:4], scale=1.0 / D)
        xns = []
        for s in range(4):
            xn = xn_ring[ctr[0] % len(xn_ring)]
            ctr[0] += 1
            k.op("dve", lambda e: e.tensor_scalar(out=xn[:], in0=hb[:, s, :], scalar1=rs[:, s:s + 1], scalar2=None, op0=ALU.mult),
                 [hb, rs], [xn])
            xns.append(xn)
        return xns

    def norm_B(xns, gain, uT):
        for kk in range(8):
            pb = pget()
            pv = pb[:].bitcast(BF16)
            for s in range(4):
                k.op("pe", lambda e: e.transpose(out=pv[:, s * 128:(s + 1) * 128], in_=xns[s][:, kk * 128:(kk + 1) * 128], identity=identb[:]),
                     [xns[s], identb], [pb])
            k.op("dve" if kk % 2 else "act",
                 (lambda e: e.tensor_scalar(out=uT[:, kk, :], in0=pv[:, 0:512], scalar1=gain[:, kk:kk + 1], scalar2=None, op0=ALU.mult)) if kk % 2 else
                 (lambda e: e.activation(out=uT[:, kk, :], in_=pv[:, 0:512], func=AF.Copy, scale=gain[:, kk:kk + 1])),
                 [pb, gain], [uT])

    def norm_T(hb, gain, uT, xn_ring, ss, rs, ctr):
        norm_B(norm_A(hb, xn_ring, ss, rs, ctr), gain, uT)

    def load_gain(dst, src_row, nk):
        k.dma("sp", dst[:, 0:nk], src_row.rearrange("(kk p) -> p kk", p=128), [], [dst], "gain", allow_slow_non_contiguous=True)

    def ffn_phase(l, f, src, dst, final):
        wn_out = "w_ffn1_out" if f == 0 else "w_ffn2_out"
        gn = "ffn1_norm" if f == 0 else "ffn2_norm"
        with k.phase():
            pset(range(8))
            wout = k.sb([128, NJ, D], BF16, "wout")
            for j in range(NJ):
                load_w(wout, wout[:, j, :], W[wn_out][l, j * 128:(j + 1) * 128, :])
            gain = k.sb([128, 8], F32, "gain")
            load_gain(gain, W[gn][l], 8)
            if final:
                fg = k.sb([128, D], F32, "fg")
                k.dma("sp", fg[:], W["final_norm"][0].partition_broadcast(128), [], [fg], "fg")
            hbs = k.ring(2, [128, 4, D], F32, "hb")
            xnr = k.ring(4, [128, D], BF16, "xn")
            uTs = k.ring(2, [128, 8, TT], BF16, "uT")
            hid = k.sb([128, NJ, TT], BF16, "hid")
            ss_b = k.sb([128, 8], F32, "ss_b")
            rs_b = k.sb([128, 8], F32, "rs_b")
            sgs = k.ring(2, [128, TT], F32, "sg")
            slabs = k.ring(3, [128, 8, 256], BF16, "slab")
            ss = k.sb([128, 8], F32, "ss")
            rs = k.sb([128, 8], F32, "rs")
            ctr = [0]
            nsl = 0
            ntl = NT // TT
            load_h(src, 0, hbs[0], "hld0")
            norm_T(hbs[0], gain, uTs[0], xnr, ss_b, rs_b, ctr)
            for ti in range(ntl):
                t0 = ti * TT
                hb = hbs[ti % 2]
                uT = uTs[ti % 2]
                nxt = None
                for j in range(NJ):
                    if ti + 1 < ntl:
                        if j == 1:
                            load_h(src, t0 + TT, hbs[(ti + 1) % 2], f"hld{(ti + 1) % 2}")
                        if j == 6:
                            nxt = norm_A(hbs[(ti + 1) % 2], xnr, ss_b, rs_b, ctr)
                        if j == 16:
                            norm_B(nxt, gain, uTs[(ti + 1) % 2])
                    sl = slabs[nsl % 3]
                    k.dma("sp", sl[:], wffn_bf[l * 2 + f, j].rearrange("p (kk n) -> p kk n", kk=8), [], [sl], f"slab{nsl % 3}")
                    nsl += 1
                    pg = pget()
                    pu = pget()
                    for kk in range(8):
                        k.op("pe", lambda e: e.matmul(pg[:], lhsT=sl[:, kk, 0:128], rhs=uT[:, kk, :], start=(kk == 0), stop=(kk == 7)),
                             [sl, uT], [pg])
                    for kk in range(8):
                        k.op("pe", lambda e: e.matmul(pu[:], lhsT=sl[:, kk, 128:256], rhs=uT[:, kk, :], start=(kk == 0), stop=(kk == 7)),
                             [sl, uT], [pu])
                    sg = sgs[j % 2]
                    k.op("act", lambda e: e.activation(out=sg[:], in_=pg[:], func=AF.Silu), [pg], [sg])
                    k.op("dve", lambda e: e.tensor_tensor(out=hid[:, j, :], in0=sg[:], in1=pu[:], op=ALU.mult), [sg, pu], [hid])
                for s in range(4):
                    for fh in range(2):
                        po = pget()
                        for j in range(NJ):
                            k.op("pe", lambda e: e.matmul(po[:], lhsT=hid[:, j, s * 128:(s + 1) * 128], rhs=wout[:, j, fh * 512:(fh + 1) * 512],
                                                          start=(j == 0), stop=(j == NJ - 1)), [hid, wout], [po])
                        k.op("dve", lambda e: e.scalar_tensor_tensor(out=hb[:, s, fh * 512:(fh + 1) * 512], in0=po[:], scalar=0.5,
                                                                     in1=hb[:, s, fh * 512:(fh + 1) * 512], op0=ALU.mult, op1=ALU.add),
                             [po, hb], [hb])
                if final:
                    for s in range(4):
                        xn = xnr[ctr[0] % 4]
                        ctr[0] += 1
                        k.op("dve", lambda e: e.scalar_tensor_tensor(out=sgs[0][:], in0=hb[:, s, 0:512], scalar=1.0, in1=hb[:, s, 0:512],
                                                                     op0=ALU.mult, op1=ALU.mult, accum_out=ss[:, s:s + 1]), [hb], [sgs[0], ss])
                        k.op("dve", lambda e: e.scalar_tensor_tensor(out=sgs[0][:], in0=hb[:, s, 512:1024], scalar=1.0, in1=hb[:, s, 512:1024],
                                                                     op0=ALU.mult, op1=ALU.mult, accum_out=ss[:, 4 + s:5 + s]), [hb], [sgs[0], ss])
                    k.op("dve", lambda e: e.tensor_tensor(out=ss[:, 0:4], in0=ss[:, 0:4], in1=ss[:, 4:8], op=ALU.add), [ss], [ss])
                    rsqrt(rs, rs[:, 0:4], ss, ss[:, 0:4], scale=1.0 / D)
                    for s in range(4):
                        k.op("dve", lambda e: e.scalar_tensor_tensor(out=hb[:, s, :], in0=hb[:, s, :], scalar=rs[:, s:s + 1], in1=fg[:],
                                                                     op0=ALU.mult, op1=ALU.mult), [hb, rs, fg], [hb])
                k.dma("pool", dst[t0:t0 + TT, :].rearrange("(s p) d -> p s d", p=128), hb[:], [hb], [], f"hst{ti % 2}")

    def m1_phase(l, src):
        with k.phase():
            pset(range(8))
            win = k.sb([128, 8, INC + 64], BF16, "win")
            for kk in range(8):
                load_w_wide(win, win[:, kk, 0:INC], W["w_in"][l, kk * 128:(kk + 1) * 128, :])
                load_w(win, win[:, kk, INC:INC + 32], W["w_in"][l, kk * 128:(kk + 1) * 128, 4160:4192], scale=-1.0)
                load_w(win, win[:, kk, INC + 32:INC + 64], W["w_in"][l, kk * 128:(kk + 1) * 128, 4128:4160])
            wsm = k.sb([128, 8, 32], BF16, "wsm")
            for kk in range(8):
                load_w(wsm, wsm[:, kk, 0:16], W["w_in"][l, kk * 128:(kk + 1) * 128, 2048:2064])
                load_w(wsm, wsm[:, kk, 16:32], W["w_in"][l, kk * 128:(kk + 1) * 128, 3600:3616])
            gain = k.sb([128, 8], F32, "gain")
            load_gain(gain, W["mix_norm"][l], 8)
            hbs = k.ring(2, [128, 4, D], F32, "hb")
            xnr = k.ring(4, [128, D], BF16, "xn")
            uTs = k.ring(2, [128, 8, TT], BF16, "uT")
            ss = k.sb([128, 8], F32, "ss")
            rs = k.sb([128, 8], F32, "rs")
            stg = k.ring(4, [128, TT], BF16, "stg")
            zst = k.ring(2, [128, 4, 512], F32, "zst")
            smt = k.ring(2, [128, 4, 32], F32, "smt")
            ctr = [0]
            groups = [(g, g * 128) for g in range(16)] + [(16 + g, 2576 + g * 128) for g in range(8)] + \
                     [(24 + g, 3616 + g * 128) for g in range(4)] + [(28, 4128)]
            ns = 0
            ntl = NT // TT
            load_h(src, 0, hbs[0], "hld0")
            norm_T(hbs[0], gain, uTs[0], xnr, ss, rs, ctr)
            for ti in range(ntl):
                t0 = ti * TT
                hb = hbs[ti % 2]
                uT = uTs[ti % 2]
                nxt = None
                for gi, (rg, co) in enumerate(groups):
                    if ti + 1 < ntl:
                        if gi == 1:
                            load_h(src, t0 + TT, hbs[(ti + 1) % 2], f"hld{(ti + 1) % 2}")
                        if gi == 6:
                            nxt = norm_A(hbs[(ti + 1) % 2], xnr, ss, rs, ctr)
                        if gi == 18:
                            norm_B(nxt, gain, uTs[(ti + 1) % 2])
                    pb = pget()
                    for kk in range(8):
                        k.op("pe", lambda e: e.matmul(pb[:], lhsT=win[:, kk, co:co + 128], rhs=uT[:, kk, :], start=(kk == 0), stop=(kk == 7)),
                             [win, uT], [pb])
                    st = stg[ns % 4]
                    if ns % 2:
                        k.op("dve", lambda e: e.tensor_copy(out=st[:], in_=pb[:]), [pb], [st])
                    else:
                        k.op("act", lambda e: e.activation(out=st[:], in_=pb[:], func=AF.Copy), [pb], [st])
                    k.dma("pool", featT[rg * 128:(rg + 1) * 128, t0:t0 + TT], st[:], [st], [], f"stg{ns % 4}")
                    ns += 1
                z = zst[ti % 2]
                sm = smt[ti % 2]
                for s in range(4):
                    pb = pget()
                    for kk in range(8):
                        k.op("pe", lambda e: e.matmul(pb[:], lhsT=uT[:, kk, s * 128:(s + 1) * 128], rhs=win[:, kk, 2064:2576], start=(kk == 0), stop=(kk == 7)),
                             [win, uT], [pb])
                    k.op("act", lambda e: e.activation(out=z[:, s, :], in_=pb[:], func=AF.Copy), [pb], [z])
                    pb = pget()
                    for kk in range(8):
                        k.op("pe", lambda e: e.matmul(pb[:, 0:32], lhsT=uT[:, kk, s * 128:(s + 1) * 128], rhs=wsm[:, kk, :], start=(kk == 0), stop=(kk == 7)),
                             [wsm, uT], [pb])
                    k.op("dve", lambda e: e.tensor_copy(out=sm[:, s, :], in_=pb[:, 0:32]), [pb], [sm])
                k.dma("pool", zs_tm[t0:t0 + TT, :].rearrange("(s p) d -> p s d", p=128), z[:], [z], [], f"zst{ti % 2}")
                k.dma("pool", sm_tm[t0:t0 + TT, :].rearrange("(s p) d -> p s d", p=128), sm[:], [sm], [], f"smt{ti % 2}")

    def scan_phase(l):
        with k.phase():
            cw_g = k.sb([128, 12, 5], F32, "cw_g")
            for j in range(5):
                k.dma("sp", cw_g[:, :, j], W["gdn_conv"][l, j].rearrange("(g p) -> p g", p=128), [], [cw_g], "cst1", allow_slow_non_contiguous=True)
            cw_s = k.sb([128, 8, 5], F32, "cw_s")
            for j in range(5):
                k.dma("sp", cw_s[:, :, j], W["ssd_conv"][l, j].rearrange("(g p) -> p g", p=128), [], [cw_s], "cst1", allow_slow_non_contiguous=True)
            dg_g = k.sb([128, 12, 5, 128], BF16, "dg_g")
            dg_s = k.sb([128, 8, 5, 128], BF16, "dg_s")
            for g in range(12):
                for j in range(5):
                    k.op("dve", lambda e: e.tensor_scalar(out=dg_g[:, g, j, :], in0=ident_f, scalar1=cw_g[:, g, j:j + 1], scalar2=None, op0=ALU.mult),
                         [cm, cw_g], [dg_g])
            for g in range(8):
                for j in range(5):
                    k.op("dve", lambda e: e.tensor_scalar(out=dg_s[:, g, j, :], in0=ident_f, scalar1=cw_s[:, g, j:j + 1], scalar2=None, op0=ALU.mult),
                         [cm, cw_s], [dg_s])
            cb_row = k.sb([1, 1024], BF16, "cb_row")
            load_w(cb_row, cb_row[:], W["ssd_conv_b"][l:l + 1, :])
            cb_pp = k.sb([128, 8], F32, "cb_pp")
            load_gain(cb_pp, W["ssd_conv_b"][l], 8)
            gA = k.sb([128, 8], F32, "gA")
            gdtb = k.sb([128, 8], F32, "gdtb")
            sA = k.sb([128, 16], F32, "sA")
            sdtb = k.sb([128, 16], F32, "sdtb")
            sD = k.sb([128, 8], F32, "sD")
            gng = k.sb([128, 1], F32, "gng")
            sng = k.sb([128, 512], F32, "sng")
            k.dma("sp", gA[:], W["gdn_A_log"][l].partition_broadcast(128), [], [gA], "cst1")
            k.dma("sp", gdtb[:], W["gdn_dt_bias"][l].partition_broadcast(128), [], [gdtb], "cst1")
            k.dma("sp", sA[:], W["ssd_A_log"][l].partition_broadcast(128), [], [sA], "cst1")
            k.dma("sp", sdtb[:], W["ssd_dt_bias"][l].partition_broadcast(128), [], [sdtb], "cst1")
            k.dma("sp", sD[:], W["ssd_D"][l].partition_broadcast(128), [], [sD], "cst1")
            k.dma("sp", sng[:], W["ssd_norm"][l].partition_broadcast(128), [], [sng], "cst1")
            k.dma("sp", gng[:], W["gdn_norm"][l].rearrange("(p o) -> p o", o=1), [], [gng], "cst1")
            for t in (gA, sA):
                k.op("act", lambda e, t=t: e.activation(out=t[:], in_=t[:], func=AF.Exp), [t], [t])
                k.op("dve", lambda e, t=t: e.tensor_scalar(out=t[:], in0=t[:], scalar1=-1.0, scalar2=None, op0=ALU.mult), [t], [t])
            ones_row = k.sb([1, 128], BF16, "ones_row")
            k.op("dve", lambda e: e.memset(ones_row[:], 1.0), [], [ones_row])

            R = 2
            pre_g = k.ring(R, [128, 12, 132], BF16, "pre_g")
            pre_s = k.ring(R, [128, 8, 132], BF16, "pre_s")
            abt = k.ring(R, [128, 32], F32, "abt")
            qkv = k.ring(R, [128, 12, 128], F32, "qkv")
            vbf = k.ring(R, [128, 4, 128], BF16, "vbf")
            sq = k.sb([128, 8, 128], F32, "sq")
            ssn = k.sb([128, 8], F32, "ssn")
            rinv = k.sb([128, 8], F32, "rinv")
            qkn = k.ring(R, [128, 8, 128], BF16, "qkn")
            gt = k.sb([128, 16], F32, "gt")
            gg = k.ring(R, [128, 8], F32, "gg")
            beta = k.ring(R, [128, 8], F32, "beta")
            nbeta = k.ring(R, [128, 8], F32, "nbeta")
            gcs = k.ring(R, [128, 4], F32, "gcs")
            egc = k.ring(R, [128, 4], F32, "egc")
            edl = k.ring(R, [128, 4], F32, "edl")
            egl = k.ring(R, [128, 4], F32, "egl")
            GL = k.ring(1, [128, 4, 128], F32, "GL")
            Dm = k.ring(1, [128, 4, 128], F32, "Dm")
            GLs = k.ring(1, [128, 8, 128], F32, "GLs")
            Dms = k.ring(1, [128, 8, 128], F32, "Dms")
            ss2 = k.sb([128, 4], F32, "ss2")
            rs2 = k.sb([128, 4], F32, "rs2")
            gst = {"i": 0}
            sst = {"i": 0}

            def pgG():
                b = banks[gst["i"] % 4]
                gst["i"] += 1
                return b

            def pgS():
                b = banks[4 + sst["i"] % 4]
                sst["i"] += 1
                return b
            DmI = k.sb([128, 4, 128], F32, "DmI")
            kqT = k.ring(R, [128, 4, 2, 128], BF16, "kqT")
            qg = k.sb([128, 4, 128], BF16, "qg")
            qgT = k.ring(R, [128, 4, 128], BF16, "qgT")
            kg = k.ring(R, [128, 4, 128], BF16, "kg")
            kd = k.ring(R, [128, 4, 128], BF16, "kd")
            At = k.ring(R, [128, 4, 128], BF16, "At")
            t1 = k.sb([128, 4, 128], F32, "t1")
            Qs = k.ring(2, [128, 4, 128], INV_DT, "Qs")
            QTs = k.ring(2, [128, 4, 128], INV_DT, "QTs")
            IQT = k.ring(2, [128, 4, 128], INV_DT, "IQT")
            Ts = k.ring(2, [128, 4, 128], INV_DT, "Ts")
            Xbs = k.ring(R, [128, 4, 128], BF16, "Xb")
            nWT = k.ring(R, [128, 4, 128], BF16, "nWT")
            vnew = k.ring(R, [128, 4, 128], BF16, "vnew")
            S32 = [k.sb([128, 4, 128], F32, "S32") for _ in range(2)]
            Sbf = [k.sb([128, 4, 128], BF16, "Sbf") for _ in range(2)]
            osb = k.ring(R, [128, 4, 128], F32, "osb")
            ofl = k.ring(R, [128, 4, 128], F32, "ofl")
            on = k.sb([128, 4, 128], BF16, "on")
            zT = k.ring(R, [128, 4, 128], BF16, "zT")
            sz = k.sb([128, 4, 128], F32, "sz")
            goT = k.ring(R, [128, 4, 128], BF16, "goT")
            xB = k.ring(R, [128, 6, 128], F32, "xB")
            Bbf = k.ring(R, [128, 2, 128], BF16, "Bbf")
            BCT = k.ring(R, [128, 4, 128], BF16, "BCT")
            dts = k.ring(R, [128, 16], F32, "dts")
            dA = k.ring(R, [128, 16], F32, "dA")
            acs = k.ring(R, [128, 8], F32, "acs")
            eac = k.ring(R, [128, 8], F32, "eac")
            dst_ = k.ring(R, [128, 8], F32, "dst")
            ecd = k.ring(R, [128, 8], F32, "ecd")
            Mt = k.ring(R, [128, 8, 128], BF16, "Mt")
            xdt = k.ring(R, [128, 8, 64], BF16, "xdt")
            xdd = k.ring(R, [128, 8, 64], BF16, "xdd")
            dd = k.sb([128, 8], F32, "dd")
            H32 = [k.sb([128, 512], F32, "H32") for _ in range(2)]
            Hbf = [k.sb([128, 512], BF16, "Hbf") for _ in range(2)]
            ysb = k.ring(R, [128, 512], F32, "ysb")
            yfl = k.ring(R, [128, 512], F32, "yfl")
            ytmp = k.sb([128, 512], F32, "ytmp")
            zsl = k.ring(R, [128, 512], F32, "zsl")
            ynb = k.sb([128, 512], BF16, "ynb")
            soT = k.ring(R, [128, 4, 128], BF16, "soT")
            ss1 = k.sb([128, 4], F32, "ss1")
            rs1 = k.sb([128, 4], F32, "rs1")
            cnt = {"g": 0, "s": 0}
            r_of = Res(acc=True)
            r_yf = Res(acc=True)

            def load_pre(buf, row0, ngrp, seq0, L, c, key):
                t0 = c * 128
                lo = max(t0 - 2, 0)
                hi = min(t0 + 130, L)
                if lo != t0 - 2 or hi != t0 + 130:
                    k.op("pool", lambda e: e.memset(buf[:], 0.0), [], [buf])
                k.dma("sp", buf[:, :, lo - (t0 - 2): hi - (t0 - 2)],
                      featT[row0:row0 + ngrp * 128, seq0 + lo: seq0 + hi].rearrange("(g p) t -> p g t", p=128), [], [buf], key)

            def gdn_chunk(seq0, L, c, d):
                i = cnt["g"]
                cnt["g"] += 1
                r = i % R
                tg = seq0 + c * 128
                pg_, ab, qv, vb = pre_g[r], abt[r], qkv[r], vbf[r]
                load_pre(pg_, 0, 12, seq0, L, c, f"preg{r}")
                k.dma("sp", ab[:], sm_tm[tg:tg + 128, :], [], [ab], f"ab{r}")
                for part in range(3):
                    pb = pgG()
                    for h in range(4):
                        g = part * 4 + h
                        for j in range(5):
                            k.op("pe", lambda e: e.matmul(pb[:, h * 128:(h + 1) * 128], lhsT=pg_[:, g, j:j + 128], rhs=dg_g[:, g, j, :],
                                                          start=(j == 0), stop=(j == 4)), [pg_, dg_g], [pb])
                    k.op("act", lambda e: e.activation(out=qv[:, part * 4:(part + 1) * 4, :], in_=pb[:].rearrange("p (h d) -> p h d", h=4), func=AF.Silu),
                         [pb], [qv])
                k.op("pool", lambda e: e.tensor_copy(out=vb[:], in_=qv[:, 8:12, :]), [qv], [vb])
                yield
                k.op("dve", lambda e: e.tensor_tensor(out=sq[:], in0=qv[:, 0:8, :], in1=qv[:, 0:8, :], op=ALU.mult), [qv], [sq])
                k.op("dve", lambda e: e.tensor_reduce(out=ssn[:], in_=sq[:], axis=AX.X, op=ALU.add), [sq], [ssn])
                rsqrt(rinv, rinv[:], ssn, ssn[:])
                k.op("dve", lambda e: e.tensor_scalar(out=rinv[:, 0:4], in0=rinv[:, 0:4], scalar1=128 ** -0.5, scalar2=None, op0=ALU.mult), [rinv], [rinv])
                qn = qkn[r]
                k.op("dve", lambda e: e.tensor_tensor(out=qn[:], in0=qv[:, 0:8, :], in1=bc3(rinv[:], 128), op=ALU.mult), [qv, rinv], [qn])
                yield
                g_, be, nbe = gg[r], beta[r], nbeta[r]
                k.op("dve", lambda e: e.tensor_tensor(out=gt[:, 0:8], in0=ab[:, 0:8], in1=gdtb[:], op=ALU.add), [ab, gdtb], [gt])
                k.op("act", lambda e: e.activation(out=gt[:, 0:8], in_=gt[:, 0:8], func=AF.Exp), [gt], [gt])
                k.op("act", lambda e: e.activation(out=gt[:, 0:8], in_=gt[:, 0:8], func=AF.Ln, bias=cst[:, 1:2]), [gt, cst], [gt])
                k.op("dve", lambda e: e.tensor_tensor(out=g_[:], in0=gt[:, 0:8], in1=gA[:], op=ALU.mult), [gt, gA], [g_])
                k.op("act", lambda e: e.activation(out=gt[:, 8:16], in_=ab[:, 8:16], func=AF.Exp, scale=-1.0), [ab], [gt])
                k.op("dve", lambda e: e.tensor_scalar(out=gt[:, 8:16], in0=gt[:, 8:16], scalar1=1.0, scalar2=None, op0=ALU.add), [gt], [gt])
                k.op("dve", lambda e: e.reciprocal(out=be[:], in_=gt[:, 8:16]), [gt], [be])
                k.op("dve", lambda e: e.tensor_scalar(out=nbe[:], in0=be[:], scalar1=-1.0, scalar2=None, op0=ALU.mult), [be], [nbe])
                gd = g_[:, d * 4:(d + 1) * 4]
                tri = cm[:, 1 + d, :]
                yield
                pb = pgG()
                k.op("pe", lambda e: e.matmul(pb[:, 0:4], lhsT=tri, rhs=gd, start=True, stop=True), [cm, g_], [pb])
                k.op("pe", lambda e: e.matmul(pb[:, 4:8], lhsT=onesf[:], rhs=gd, start=True, stop=True), [onesf, g_], [pb])
                gc_, eg, ed, el = gcs[r], egc[r], edl[r], egl[r]
                k.op("act", lambda e: e.activation(out=gc_[:], in_=pb[:, 0:4], func=AF.Copy), [pb], [gc_])
                k.op("act", lambda e: e.activation(out=eg[:], in_=pb[:, 0:4], func=AF.Exp), [pb], [eg])
                k.op("act", lambda e: e.activation(out=el[:], in_=pb[:, 4:8], func=AF.Exp), [pb], [el])
                k.op("dve", lambda e: e.tensor_tensor(out=ed[:], in0=pb[:, 4:8], in1=gc_[:], op=ALU.subtract), [pb, gc_], [ed])
                k.op("act", lambda e: e.activation(out=ed[:], in_=ed[:], func=AF.Exp), [ed], [ed])
                yield
                gl, dm = GL[0], Dm[0]
                k.op("dve", lambda e: e.tensor_tensor(out=gl[:], in0=bcm(tri, 4), in1=bc3(gd, 128), op=ALU.mult), [cm, g_], [gl])
                pb = pgG()
                k.op("pe", lambda e: e.matmul(pb[:], lhsT=onesf[:], rhs=gl[:].rearrange("p h i -> p (h i)"), start=True, stop=False), [onesf, gl], [pb])
                for h in range(4):
                    k.op("pe", lambda e: e.matmul(pb[:, h * 128:(h + 1) * 128], lhsT=gl[:, h, :], rhs=nonesf[:], start=False, stop=True), [gl, nonesf], [pb])
                k.op("dve", lambda e: e.tensor_tensor(out=dm[:], in0=pb[:].rearrange("p (h i) -> p h i", h=4), in1=bcm(cm[:, 3 + d, :], 4), op=ALU.add),
                     [pb, cm], [dm])
                k.op("act", lambda e: e.activation(out=dm[:], in_=dm[:], func=AF.Exp), [dm], [dm])
                k.op("pool", lambda e: e.tensor_tensor(out=DmI[:], in0=dm[:], in1=bcm(ident_f, 4), op=ALU.add), [dm, cm], [DmI])
                yield
                kg_, kd_ = kg[r], kd[r]
                k.op("pool", lambda e: e.tensor_tensor(out=qg[:], in0=qn[:, 0:4, :], in1=bc3(eg[:], 128), op=ALU.mult), [qn, eg], [qg])
                k.op("pool", lambda e: e.tensor_tensor(out=kg_[:], in0=qn[:, 4:8, :], in1=bc3(eg[:], 128), op=ALU.mult), [qn, eg], [kg_])
                k.op("pool", lambda e: e.tensor_tensor(out=kd_[:], in0=qn[:, 4:8, :], in1=bc3(ed[:], 128), op=ALU.mult), [qn, ed], [kd_])
                yield
                kq, qgt = kqT[r], qgT[r]
                pb = pgG()
                pv = pb[:].bitcast(BF16)
                for h in range(4):
                    k.op("pe", lambda e: e.transpose(out=pv[:, (2 * h) * 128:(2 * h + 1) * 128], in_=qn[:, 4 + h, :], identity=identb[:]), [qn, identb], [pb])
                    k.op("pe", lambda e: e.transpose(out=pv[:, (2 * h + 1) * 128:(2 * h + 2) * 128], in_=qn[:, h, :], identity=identb[:]), [qn, identb], [pb])
                k.op("act", lambda e: e.activation(out=kq[:].rearrange("p h t i -> p (h t i)"), in_=pv[:, 0:1024], func=AF.Copy), [pb], [kq])
                pb = pgG()
                pv = pb[:].bitcast(BF16)
                for h in range(4):
                    k.op("pe", lambda e: e.transpose(out=pv[:, h * 128:(h + 1) * 128], in_=qg[:, h, :], identity=identb[:]), [qg, identb], [pb])
                k.op("dve", lambda e: e.tensor_copy(out=qgt[:].rearrange("p h i -> p (h i)"), in_=pv[:, 0:512]), [pb], [qgt])
                yield
                pk = [pgG(), pgG()]
                for h in range(4):
                    b = pk[h // 2]
                    k.op("pe", lambda e: e.matmul(b[:, (h % 2) * 256:(h % 2 + 1) * 256], lhsT=kq[:, h, 0, :], rhs=kq[:, h, :, :].rearrange("p t i -> p (t i)"),
                                                  start=True, stop=True), [kq], [b])
                at = At[r]
                q0, q0t = Qs[0], QTs[0]
                for hh in range(2):
                    b = pk[hh]
                    b4 = b[:].rearrange("p (h t i) -> p h t i", h=2, t=2)
                    k.op("dve", lambda e: e.tensor_tensor(out=at[:, 2 * hh:2 * hh + 2, :], in0=b4[:, :, 1, :], in1=DmI[:, 2 * hh:2 * hh + 2, :], op=ALU.mult),
                         [b, DmI], [at])
                    k.op("dve", lambda e: e.tensor_tensor(out=t1[:, 2 * hh:2 * hh + 2, :], in0=b4[:, :, 0, :], in1=dm[:, 2 * hh:2 * hh + 2, :], op=ALU.mult),
                         [b, dm], [t1])
                nb_d = nbe[:, d * 4:(d + 1) * 4]
                k.op("dve", lambda e: e.tensor_tensor(out=q0[:], in0=t1[:], in1=bc3(nb_d, 128), op=ALU.mult), [t1, nbe], [q0])
                yield
                ident_inv = identb[:] if INV_DT == BF16 else ident_f
                pb = pgG()
                if INV_DT == BF16:
                    pv = pb[:].bitcast(BF16)
                else:
                    pv = pb[:]
                for h in range(4):
                    k.op("pe", lambda e: e.transpose(out=pv[:, h * 128:(h + 1) * 128], in_=q0[:, h, :], identity=ident_inv), [q0, identb, cm], [pb])
                k.op("act", lambda e: e.activation(out=q0t[:].rearrange("p h i -> p (h i)"), in_=pv[:, 0:512], func=AF.Copy), [pb], [q0t])
                T = Ts[0]
                k.op("pool", lambda e: e.tensor_tensor(out=T[:], in0=q0[:], in1=bcm(ident_f, 4), op=ALU.add), [q0, cm], [T])
                Qp, QTp = q0, q0t
                for lev in range(1, 7):
                    yield
                    Qn, QTn, iq, Tn = Qs[lev % 2], QTs[lev % 2], IQT[lev % 2], Ts[lev % 2]
                    pbt = pgG()
                    for h in range(4):
                        k.op("pe", lambda e: e.matmul(pbt[:, h * 128:(h + 1) * 128], lhsT=Qp[:, h, :], rhs=QTp[:, h, :], start=True, stop=True), [Qp, QTp], [pbt])
                    if lev < 6:
                        pbq = pgG()
                        for h in range(4):
                            k.op("pe", lambda e: e.matmul(pbq[:, h * 128:(h + 1) * 128], lhsT=QTp[:, h, :], rhs=Qp[:, h, :], start=True, stop=True), [Qp, QTp], [pbq])
                        k.op("act", lambda e: e.activation(out=Qn[:].rearrange("p h i -> p (h i)"), in_=pbq[:], func=AF.Copy), [pbq], [Qn])
                        k.op("act", lambda e: e.activation(out=QTn[:].rearrange("p h i -> p (h i)"), in_=pbt[:], func=AF.Copy), [pbt], [QTn])
                    k.op("dve", lambda e: e.tensor_tensor(out=iq[:], in0=pbt[:].rearrange("p (h i) -> p h i", h=4), in1=bcm(ident_f, 4), op=ALU.add),
                         [pbt, cm], [iq])
                    yield
                    pbT = pgG()
                    for h in range(4):
                        k.op("pe", lambda e: e.matmul(pbT[:, h * 128:(h + 1) * 128], lhsT=iq[:, h, :], rhs=T[:, h, :], start=True, stop=True), [iq, T], [pbT])
                    k.op("dve", lambda e: e.tensor_copy(out=Tn[:].rearrange("p h i -> p (h i)"), in_=pbT[:]), [pbT], [Tn])
                    T, Qp, QTp = Tn, Qn, QTn
                yield
                X = Xbs[r]
                k.op("pool", lambda e: e.tensor_copy(out=X[:], in_=T[:]), [T], [X])
                nw = nWT[r]
                pb = pgG()
                for h in range(4):
                    k.op("pe", lambda e: e.matmul(pb[:, h * 128:(h + 1) * 128], lhsT=kg_[:, h, :], rhs=X[:, h, :], start=True, stop=True), [kg_, X], [pb])
                k.op("act", lambda e: e.activation(out=nw[:].rearrange("p h i -> p (h i)"), in_=pb[:], func=AF.Copy, scale=-1.0), [pb], [nw])
                yield
                S3, Sb = S32[d], Sbf[d]
                first = (c == 0) if d == 0 else (c == L // 128 - 1)
                if first:
                    k.op("pool", lambda e: e.memset(S3[:], 0.0), [], [S3])
                    k.op("pool", lambda e: e.memset(Sb[:], 0.0), [], [Sb])
                pb = pgG()
                for h in range(4):
                    k.op("pe", lambda e: e.matmul(pb[:, h * 128:(h + 1) * 128], lhsT=X[:, h, :], rhs=vb[:, h, :], start=True, stop=False), [X, vb], [pb])
                    k.op("pe", lambda e: e.matmul(pb[:, h * 128:(h + 1) * 128], lhsT=nw[:, h, :], rhs=Sb[:, h, :], start=False, stop=True), [nw, Sb], [pb])
                vn = vnew[r]
                k.op("dve", lambda e: e.tensor_tensor(out=vn[:], in0=pb[:].rearrange("p (h i) -> p h i", h=4), in1=bc3(be[:, d * 4:(d + 1) * 4], 128), op=ALU.mult),
                     [pb, be], [vn])
                yield
                po = pgG()
                for h in range(4):
                    k.op("pe", lambda e: e.matmul(po[:, h * 128:(h + 1) * 128], lhsT=qgt[:, h, :], rhs=Sb[:, h, :], start=True, stop=False), [qgt, Sb], [po])
                    k.op("pe", lambda e: e.matmul(po[:, h * 128:(h + 1) * 128], lhsT=at[:, h, :], rhs=vn[:, h, :], start=False, stop=True), [at, vn], [po])
                pS = pgG()
                for h in range(4):
                    k.op("pe", lambda e: e.matmul(pS[:, h * 128:(h + 1) * 128], lhsT=kd_[:, h, :], rhs=vn[:, h, :], start=True, stop=True), [kd_, vn], [pS])
                k.op("dve", lambda e: e.tensor_tensor(out=S3[:], in0=S3[:], in1=bc3(el[:], 128), op=ALU.mult), [S3, el], [S3])
                k.op("dve", lambda e: e.tensor_tensor(out=S3[:], in0=S3[:], in1=pS[:].rearrange("p (h i) -> p h i", h=4), op=ALU.add), [S3, pS], [S3])
                k.op("act", lambda e: e.activation(out=Sb[:], in_=S3[:], func=AF.Copy), [S3], [Sb])
                yield
                o = osb[r]
                if d == 0:
                    k.op("act", lambda e: e.activation(out=o[:].rearrange("p h i -> p (h i)"), in_=po[:], func=AF.Copy), [po], [o])
                    k.dma("pool", of_tm[tg:tg + 128, :], o[:].rearrange("p h i -> p (h i)"), [o], [r_of], f"ofst{r}")
                    return
                ofb = ofl[r]
                k.dma("sp", ofb[:].rearrange("p h i -> p (h i)"), of_tm[tg:tg + 128, :], [r_of], [ofb], f"ofld{r}")
                k.op("dve", lambda e: e.tensor_tensor(out=o[:], in0=ofb[:], in1=po[:].rearrange("p (h i) -> p h i", h=4), op=ALU.add), [ofb, po], [o])
                yield
                k.op("dve", lambda e: e.tensor_tensor(out=sq[:, 0:4, :], in0=o[:], in1=o[:], op=ALU.mult), [o], [sq])
                k.op("dve", lambda e: e.tensor_reduce(out=ss1[:], in_=sq[:, 0:4, :], axis=AX.X, op=ALU.add), [sq], [ss1])
                rsqrt(rs1, rs1[:], ss1, ss1[:], scale=1.0 / 128)
                k.op("dve", lambda e: e.tensor_tensor(out=on[:], in0=o[:], in1=bc3(rs1[:], 128), op=ALU.mult), [o, rs1], [on])
                z_ = zT[r]
                k.dma("sp", z_[:], featT[1536:2048, tg:tg + 128].rearrange("(h p) t -> p h t", p=128), [], [z_], f"zT{r}")
                k.op("act", lambda e: e.activation(out=sz[:], in_=z_[:], func=AF.Silu), [z_], [sz])
                pb = pgG()
                pv = pb[:].bitcast(BF16)
                for h in range(4):
                    k.op("pe", lambda e: e.transpose(out=pv[:, h * 128:(h + 1) * 128], in_=on[:, h, :], identity=identb[:]), [on, identb], [pb])
                go = goT[r]
                k.op("dve", lambda e: e.scalar_tensor_tensor(out=go[:].rearrange("p h i -> p (h i)"), in0=pv[:, 0:512], scalar=gng[:, 0:1],
                                                             in1=sz[:].rearrange("p h i -> p (h i)"), op0=ALU.mult, op1=ALU.mult), [pb, gng, sz], [go])
                k.dma("pool", brT[0][:, tg:tg + 128].rearrange("(h p) t -> p h t", p=128), go[:], [go], [], f"gost{r}")

            def ssd_chunk(seq0, L, c, d):
                i = cnt["s"]
                cnt["s"] += 1
                r = i % R
                tg = seq0 + c * 128
                ps_, ab = pre_s[r], abt[r]
                load_pre(ps_, 2048, 8, seq0, L, c, f"pres{r}")
                xb, bb, bct = xB[r], Bbf[r], BCT[r]
                for part, (g0, ng) in enumerate(((0, 4), (4, 2))):
                    pb = pgS()
                    for gi in range(ng):
                        g = g0 + gi
                        for j in range(5):
                            k.op("pe", lambda e: e.matmul(pb[:, gi * 128:(gi + 1) * 128], lhsT=ps_[:, g, j:j + 128], rhs=dg_s[:, g, j, :],
                                                          start=(j == 0), stop=False), [ps_, dg_s], [pb])
                        k.op("pe", lambda e: e.matmul(pb[:, gi * 128:(gi + 1) * 128], lhsT=ones_row[0:1, :], rhs=cb_row[0:1, g * 128:(g + 1) * 128],
                                                      start=False, stop=True), [ones_row, cb_row], [pb])
                    k.op("act", lambda e: e.activation(out=xb[:, g0:g0 + ng, :], in_=pb[:, 0:ng * 128].rearrange("p (h d) -> p h d", h=ng), func=AF.Silu),
                         [pb], [xb])
                k.op("pool", lambda e: e.tensor_copy(out=bb[:], in_=xb[:, 4:6, :]), [xb], [bb])
                yield
                pb = pgS()
                for gi in range(4):
                    g = 4 + gi
                    for j in range(5):
                        k.op("pe", lambda e: e.matmul(pb[:, gi * 128:(gi + 1) * 128], lhsT=dg_s[:, g, j, :], rhs=ps_[:, g, j:j + 128],
                                                      start=(j == 0), stop=(j == 4)), [ps_, dg_s], [pb])
                for gi in range(4):
                    k.op("act", lambda e: e.activation(out=bct[:, gi, :], in_=pb[:, gi * 128:(gi + 1) * 128], func=AF.Silu, bias=cb_pp[:, 4 + gi:5 + gi]),
                         [pb, cb_pp], [bct])
                yield
                dt_, da = dts[r], dA[r]
                k.op("dve", lambda e: e.tensor_tensor(out=dt_[:], in0=ab[:, 16:32], in1=sdtb[:], op=ALU.add), [ab, sdtb], [dt_])
                k.op("act", lambda e: e.activation(out=dt_[:], in_=dt_[:], func=AF.Exp), [dt_], [dt_])
                k.op("act", lambda e: e.activation(out=dt_[:], in_=dt_[:], func=AF.Ln, bias=cst[:, 1:2]), [dt_, cst], [dt_])
                k.op("dve", lambda e: e.tensor_tensor(out=da[:], in0=dt_[:], in1=sA[:], op=ALU.mult), [dt_, sA], [da])
                dad = da[:, d * 8:(d + 1) * 8]
                dtd = dt_[:, d * 8:(d + 1) * 8]
                tri = cm[:, 1 + d, :]
                pb = pgS()
                k.op("pe", lambda e: e.matmul(pb[:, 0:8], lhsT=tri, rhs=dad, start=True, stop=True), [cm, da], [pb])
                k.op("pe", lambda e: e.matmul(pb[:, 8:16], lhsT=onesf[:], rhs=dad, start=True, stop=True), [onesf, da], [pb])
                ac, ea, ds, ec = acs[r], eac[r], dst_[r], ecd[r]
                k.op("act", lambda e: e.activation(out=ac[:], in_=pb[:, 0:8], func=AF.Copy), [pb], [ac])
                k.op("act", lambda e: e.activation(out=ea[:], in_=pb[:, 0:8], func=AF.Exp), [pb], [ea])
                k.op("act", lambda e: e.activation(out=ec[:], in_=pb[:, 8:16], func=AF.Exp), [pb], [ec])
                k.op("dve", lambda e: e.tensor_tensor(out=ds[:], in0=pb[:, 8:16], in1=ac[:], op=ALU.subtract), [pb, ac], [ds])
                k.op("act", lambda e: e.activation(out=ds[:], in_=ds[:], func=AF.Exp), [ds], [ds])
                yield
                gl, dm = GLs[0], Dms[0]
                k.op("dve", lambda e: e.tensor_tensor(out=gl[:], in0=bcm(tri, 8), in1=bc3(dad, 128), op=ALU.mult), [cm, da], [gl])
                pbs = [pgS(), pgS()]
                for hh in range(2):
                    b = pbs[hh]
                    k.op("pe", lambda e: e.matmul(b[:], lhsT=onesf[:], rhs=gl[:, hh * 4:(hh + 1) * 4, :].rearrange("p h i -> p (h i)"), start=True, stop=False),
                         [onesf, gl], [b])
                    for h in range(4):
                        k.op("pe", lambda e: e.matmul(b[:, h * 128:(h + 1) * 128], lhsT=gl[:, hh * 4 + h, :], rhs=nonesf[:], start=False, stop=True),
                             [gl, nonesf], [b])
                    k.op("dve", lambda e: e.tensor_tensor(out=dm[:, hh * 4:(hh + 1) * 4, :], in0=b[:].rearrange("p (h i) -> p h i", h=4),
                                                          in1=bcm(cm[:, 5 + d, :], 4), op=ALU.add), [b, cm], [dm])
                k.op("act", lambda e: e.activation(out=dm[:], in_=dm[:], func=AF.Exp), [dm], [dm])
                yield
                pb = pgS()
                for g in range(2):
                    k.op("pe", lambda e: e.matmul(pb[:, g * 128:(g + 1) * 128], lhsT=bct[:, g, :], rhs=bct[:, 2 + g, :], start=True, stop=True), [bct], [pb])
                mt = Mt[r]
                for g in range(2):
                    k.op("dve", lambda e: e.tensor_tensor(out=mt[:, g * 4:(g + 1) * 4, :], in0=dm[:, g * 4:(g + 1) * 4, :],
                                                          in1=bcm(pb[:, g * 128:(g + 1) * 128], 4), op=ALU.mult), [dm, pb], [mt])
                yield
                xd, xe = xdt[r], xdd[r]
                x3 = xb[:, 0:4, :].rearrange("p g (h2 q) -> p (g h2) q", h2=2)
                k.op("pool", lambda e: e.tensor_tensor(out=xd[:], in0=x3, in1=bc3(dtd, 64), op=ALU.mult), [xb, dt_], [xd])
                k.op("pool", lambda e: e.tensor_tensor(out=dd[:], in0=dtd, in1=ds[:], op=ALU.mult), [dt_, ds], [dd])
                k.op("pool", lambda e: e.tensor_tensor(out=xe[:], in0=x3, in1=bc3(dd[:], 64), op=ALU.mult), [xb, dd], [xe])
                H3, Hb = H32[d], Hbf[d]
                first = (c == 0) if d == 0 else (c == L // 128 - 1)
                if first:
                    k.op("pool", lambda e: e.memset(H3[:], 0.0), [], [H3])
                    k.op("pool", lambda e: e.memset(Hb[:], 0.0), [], [Hb])
                yield
                py = pgS()
                for h in range(8):
                    k.op("pe", lambda e: e.matmul(py[:, h * 64:(h + 1) * 64], lhsT=mt[:, h, :], rhs=xd[:, h, :], start=True, stop=True), [mt, xd], [py])
                pyo = pgS()
                for g in range(2):
                    k.op("pe", lambda e: e.matmul(pyo[:, g * 256:(g + 1) * 256], lhsT=bct[:, 2 + g, :], rhs=Hb[:, g * 256:(g + 1) * 256], start=True, stop=True),
                         [bct, Hb], [pyo])
                pst = pgS()
                for g in range(2):
                    k.op("pe", lambda e: e.matmul(pst[:, g * 256:(g + 1) * 256], lhsT=bb[:, g, :], rhs=xe[:, g * 4:(g + 1) * 4, :].rearrange("p h q -> p (h q)"),
                                                  start=True, stop=True), [bb, xe], [pst])
                yield
                y = ysb[r]
                k.op("dve", lambda e: e.tensor_tensor(out=ytmp[:].rearrange("p (h q) -> p h q", h=8), in0=pyo[:].rearrange("p (h q) -> p h q", h=8),
                                                      in1=bc3(ea[:], 64), op=ALU.mult), [pyo, ea], [ytmp])
                k.op("dve", lambda e: e.tensor_tensor(out=y[:], in0=ytmp[:], in1=py[:], op=ALU.add), [ytmp, py], [y])
                k.op("dve", lambda e: e.tensor_tensor(out=H3[:].rearrange("p (h q) -> p h q", h=8), in0=H3[:].rearrange("p (h q) -> p h q", h=8),
                                                      in1=bc3(ec[:], 64), op=ALU.mult), [H3, ec], [H3])
                k.op("dve", lambda e: e.tensor_tensor(out=H3[:], in0=H3[:], in1=pst[:], op=ALU.add), [H3, pst], [H3])
                k.op("act", lambda e: e.activation(out=Hb[:], in_=H3[:], func=AF.Copy), [H3], [Hb])
                yield
                if d == 0:
                    k.dma("pool", yf_tm[tg:tg + 128, :], y[:], [y], [r_yf], f"yfst{r}")
                    return
                yf = yfl[r]
                zl = zsl[r]
                k.dma("sp", yf[:], yf_tm[tg:tg + 128, :], [r_yf], [yf], f"yfld{r}")
                k.dma("sp", zl[:], zs_tm[tg:tg + 128, :], [], [zl], f"zsl{r}")
                k.op("dve", lambda e: e.tensor_tensor(out=y[:], in0=y[:], in1=yf[:], op=ALU.add), [y, yf], [y])
                k.op("pool", lambda e: e.tensor_tensor(out=ytmp[:].rearrange("p (h q) -> p h q", h=8), in0=x3, in1=bc3(sD[:], 64), op=ALU.mult), [xb, sD], [ytmp])
                k.op("dve", lambda e: e.tensor_tensor(out=y[:], in0=y[:], in1=ytmp[:], op=ALU.add), [y, ytmp], [y])
                k.op("act", lambda e: e.activation(out=zl[:], in_=zl[:], func=AF.Silu), [zl], [zl])
                k.op("dve", lambda e: e.tensor_tensor(out=y[:], in0=y[:], in1=zl[:], op=ALU.mult), [y, zl], [y])
                k.op("dve", lambda e: e.scalar_tensor_tensor(out=ytmp[:], in0=y[:], scalar=1.0, in1=y[:], op0=ALU.mult, op1=ALU.mult, accum_out=ss2[:, 0:1]),
                     [y], [ytmp, ss2])
                rsqrt(rs2, rs2[:, 0:1], ss2, ss2[:, 0:1], scale=1.0 / 512)
                k.op("dve", lambda e: e.scalar_tensor_tensor(out=ynb[:], in0=y[:], scalar=rs2[:, 0:1], in1=sng[:], op0=ALU.mult, op1=ALU.mult),
                     [y, rs2, sng], [ynb])
                pb = pgS()
                pv = pb[:].bitcast(BF16)
                for cc in range(4):
                    k.op("pe", lambda e: e.transpose(out=pv[:, cc * 128:(cc + 1) * 128], in_=ynb[:, cc * 128:(cc + 1) * 128], identity=identb[:]), [ynb, identb], [pb])
                so = soT[r]
                k.op("act", lambda e: e.activation(out=so[:].rearrange("p h i -> p (h i)"), in_=pv[:, 0:512], func=AF.Copy), [pb], [so])
                k.dma("pool", brT[1][:, tg:tg + 128].rearrange("(h p) t -> p h t", p=128), so[:], [so], [], f"sost{r}")

            seq0 = 0
            for L in seqs:
                nch = L // 128
                for d in range(2):
                    order = range(nch) if d == 0 else range(nch - 1, -1, -1)
                    for c in order:
                        tasks = [gdn_chunk(seq0, L, c, d), ssd_chunk(seq0, L, c, d)]
                        while tasks:
                            for t in list(tasks):
                                try:
                                    next(t)
                                except StopIteration:
                                    tasks.remove(t)
                seq0 += L

    def mla_phase(l):
        with k.phase():
            pset(range(8))
            wuq = k.sb([128, 2, 768 + 256], BF16, "wuq")
            wukv = k.sb([128, 2, 1024], BF16, "wukv")
            for c in range(2):
                load_w(wuq, wuq[:, c, 0:768], W["mla_w_uq"][l, c * 128:(c + 1) * 128, :])
                load_w(wukv, wukv[:, c, :], W["mla_w_ukv"][l, c * 128:(c + 1) * 128, :])
                for h in range(4):
                    load_w(wuq, wuq[:, c, 768 + h * 64:768 + h * 64 + 32], W["mla_w_uq"][l, c * 128:(c + 1) * 128, h * 192 + 160:h * 192 + 192], scale=-1.0)
                    load_w(wuq, wuq[:, c, 768 + h * 64 + 32:768 + h * 64 + 64], W["mla_w_uq"][l, c * 128:(c + 1) * 128, h * 192 + 128:h * 192 + 160])
            wv = k.sb([128, 2, 512], BF16, "wv")
            k.op("pool", lambda e: e.tensor_copy(out=wv[:].rearrange("p c (h x) -> p c h x", h=4),
                                                 in_=wukv[:].rearrange("p c (h x) -> p c h x", h=4)[:, :, :, 128:256]), [wukv], [wv])
            qg_ = k.sb([128, 2], F32, "qgain")
            kg_ = k.sb([128, 2], F32, "kvgain")
            load_gain(qg_, W["mla_q_norm"][l], 2)
            load_gain(kg_, W["mla_kv_norm"][l], 2)
            lat = k.ring(2, [128, 4, TT], BF16, "lat")
            krr = k.ring(2, [64, 2, TT], BF16, "krr")
            cs = k.ring(2, [64, 2, TT], F32, "cs")
            sqb = k.sb([128, 2, TT], BF16, "sqb")
            rb = k.sb([128, TT], F32, "rb")
            nrm = k.ring(2, [128, 4, TT], BF16, "nrm")
            stg = k.ring(4, [128, TT], BF16, "stg")
            r1 = k.sb([64, TT], F32, "r1")
            r2 = k.sb([64, TT], F32, "r2")
            rst = k.ring(4, [64, TT], BF16, "rst")
            vst = k.ring(2, [128, 4, 512], BF16, "vst")
            ns = 0
            nr = 0
            seq0 = 0
            ti = 0
            for L in seqs:
                for p0 in range(0, L, TT):
                    t0 = seq0 + p0
                    la, kr_, cs_ = lat[ti % 2], krr[ti % 2], cs[ti % 2]
                    k.dma("sp", la[:], featT[3072:3584, t0:t0 + TT].rearrange("(c p) t -> p c t", p=128), [], [la], f"lat{ti % 2}")
                    k.dma("sp", kr_[:], featT[3584:3712, t0:t0 + TT].rearrange("(c p) t -> p c t", p=64), [], [kr_], f"krr{ti % 2}")
                    k.dma("sp", cs_[:], ropecs[:, :, p0:p0 + TT].rearrange("c p t -> p c t"), [], [cs_], f"cs{ti % 2}")
                    nm = nrm[ti % 2]
                    for which in range(2):
                        gain = qg_ if which == 0 else kg_
                        k.op("pool", lambda e: e.tensor_tensor(out=sqb[:], in0=la[:, 2 * which:2 * which + 2, :], in1=la[:, 2 * which:2 * which + 2, :], op=ALU.mult),
                             [la], [sqb])
                        pb = pget()
                        for c in range(2):
                            k.op("pe", lambda e: e.matmul(pb[:], lhsT=onesb[:], rhs=sqb[:, c, :], start=(c == 0), stop=(c == 1)), [onesb, sqb], [pb])
                        rsqrt(rb, rb[:], pb, pb[:], scale=1.0 / 256)
                        for c in range(2):
                            k.op("dve", lambda e: e.scalar_tensor_tensor(out=nm[:, 2 * which + c, :], in0=la[:, 2 * which + c, :], scalar=gain[:, c:c + 1], in1=rb[:],
                                                                         op0=ALU.mult, op1=ALU.mult), [la, gain, rb], [nm])
                    for h in range(4):
                        pb = pget()
                        for c in range(2):
                            k.op("pe", lambda e: e.matmul(pb[:], lhsT=wuq[:, c, h * 192:h * 192 + 128], rhs=nm[:, c, :], start=(c == 0), stop=(c == 1)), [wuq, nm], [pb])
                        st = stg[ns % 4]
                        k.op("act", lambda e: e.activation(out=st[:], in_=pb[:], func=AF.Copy), [pb], [st])
                        k.dma("pool", qT_d[h, :, t0:t0 + TT], st[:], [st], [], f"mstg{ns % 4}")
                        ns += 1
                        pb = pget()
                        for c in range(2):
                            k.op("pe", lambda e: e.matmul(pb[:], lhsT=wukv[:, c, h * 256:h * 256 + 128], rhs=nm[:, 2 + c, :], start=(c == 0), stop=(c == 1)), [wukv, nm], [pb])
                        st = stg[ns % 4]
                        k.op("act", lambda e: e.activation(out=st[:], in_=pb[:], func=AF.Copy), [pb], [st])
                        k.dma("pool", kT_d[h, :, t0:t0 + TT], st[:], [st], [], f"mstg{ns % 4}")
                        ns += 1
                        pb = pget()
                        pb2 = pget()
                        for c in range(2):
                            k.op("pe", lambda e: e.matmul(pb[0:64, :], lhsT=wuq[:, c, h * 192 + 128:h * 192 + 192], rhs=nm[:, c, :], start=(c == 0), stop=(c == 1)), [wuq, nm], [pb])
                        for c in range(2):
                            k.op("pe", lambda e: e.matmul(pb2[0:64, :], lhsT=wuq[:, c, 768 + h * 64:768 + (h + 1) * 64], rhs=nm[:, c, :], start=(c == 0), stop=(c == 1)), [wuq, nm], [pb2])
                        k.op("dve", lambda e: e.tensor_tensor(out=r1[:], in0=pb[0:64, :], in1=cs_[:, 0, :], op=ALU.mult), [pb, cs_], [r1])
                        k.op("dve", lambda e: e.tensor_tensor(out=r2[:], in0=pb2[0:64, :], in1=cs_[:, 1, :], op=ALU.mult), [pb2, cs_], [r2])
                        rs_ = rst[nr % 4]
                        k.op("pool", lambda e: e.tensor_tensor(out=rs_[:], in0=r1[:], in1=r2[:], op=ALU.add), [r1, r2], [rs_])
                        k.dma("pool", qrT_d[h, :, t0:t0 + TT], rs_[:], [rs_], [], f"rstg{nr % 4}")
                        nr += 1
                    k.op("dve", lambda e: e.tensor_tensor(out=r1[:], in0=kr_[:, 0, :], in1=cs_[:, 0, :], op=ALU.mult), [kr_, cs_], [r1])
                    k.op("dve", lambda e: e.tensor_tensor(out=r2[:], in0=kr_[:, 1, :], in1=cs_[:, 1, :], op=ALU.mult), [kr_, cs_], [r2])
                    rs_ = rst[nr % 4]
                    k.op("pool", lambda e: e.tensor_tensor(out=rs_[:], in0=r1[:], in1=r2[:], op=ALU.add), [r1, r2], [rs_])
                    k.dma("pool", krT_d[:, t0:t0 + TT], rs_[:], [rs_], [], f"rstg{nr % 4}")
                    nr += 1
                    vs = vst[ti % 2]
                    for s in range(4):
                        pb = pget()
                        for c in range(2):
                            k.op("pe", lambda e: e.matmul(pb[:], lhsT=nm[:, 2 + c, s * 128:(s + 1) * 128], rhs=wv[:, c, :], start=(c == 0), stop=(c == 1)), [wv, nm], [pb])
                        k.op("act", lambda e: e.activation(out=vs[:, s, :], in_=pb[:], func=AF.Copy), [pb], [vs])
                    k.dma("pool", v_d[t0:t0 + TT, :].rearrange("(s p) d -> p s d", p=128), vs[:], [vs], [], f"vst{ti % 2}")
                    ti += 1
                seq0 += L
        with k.phase():
            pset([0, 1, 2])
            kT = k.ring(2, [128, LMAX], BF16, "kT")
            vv = k.ring(2, [128, LMAX // 128, 128], BF16, "vv")
            krT = k.ring(2, [64, LMAX], BF16, "krT")
            qt = k.ring(2, [128, TT], BF16, "qt")
            qrt = k.ring(2, [64, TT], BF16, "qrt")
            pt = k.ring(3, [128, TT], BF16, "pt")
            rl = k.sb([128, TT], F32, "rl")
            ot = k.ring(2, [128, TT], BF16, "ot")
            seq0 = 0
            nq = 0
            npb = 0
            for si, L in enumerate(seqs):
                nkb = L // 128
                kr_ = krT[si % 2]
                k.dma("sp", kr_[:, 0:L], krT_d[:, seq0:seq0 + L], [], [kr_], f"krT{si % 2}")
                for h in range(4):
                    hh = (si * 4 + h) % 2
                    kt_, v_ = kT[hh], vv[hh]
                    k.dma("sp", kt_[:, 0:L], kT_d[h, :, seq0:seq0 + L], [], [kt_], f"kT{hh}")
                    for b0 in range(0, nkb, 8):
                        b1 = min(b0 + 8, nkb)
                        k.dma("sp", v_[:, b0:b1, :], v_d[seq0 + b0 * 128:seq0 + b1 * 128, h * 128:(h + 1) * 128].rearrange("(n p) d -> p n d", p=128), [], [v_], f"vv{hh}")
                    for p0 in range(0, L, TT):
                        t0 = seq0 + p0
                        q_, qr_ = qt[nq % 2], qrt[nq % 2]
                        k.dma("sp", q_[:], qT_d[h, :, t0:t0 + TT], [], [q_], f"qt{nq % 2}")
                        k.dma("sp", qr_[:], qrT_d[h, :, t0:t0 + TT], [], [qr_], f"qrt{nq % 2}")
                        po = banks[3 + 2 * (nq % 2)]
                        pl = banks[4 + 2 * (nq % 2)]
                        def s_step(kb):
                            ps_ = pget()
                            k.op("pe", lambda e: e.matmul(ps_[:], lhsT=kt_[:, kb * 128:(kb + 1) * 128], rhs=q_[:], start=True, stop=False), [kt_, q_], [ps_])
                            k.op("pe", lambda e: e.matmul(ps_[:], lhsT=kr_[:, kb * 128:(kb + 1) * 128], rhs=qr_[:], start=False, stop=True), [kr_, qr_], [ps_])
                            p_ = pt[s_step.n % 3]
                            s_step.n += 1
                            k.op("act", lambda e: e.activation(out=p_[:], in_=ps_[:], func=AF.Exp, scale=MLA_SCALE), [ps_], [p_])
                            return p_

                        def pv_step(kb, p_):
                            k.op("pe", lambda e: e.matmul(po[:], lhsT=v_[:, kb, :], rhs=p_[:], start=(kb == 0), stop=(kb == nkb - 1)), [v_, p_], [po])
                            k.op("pe", lambda e: e.matmul(pl[:], lhsT=onesb[:], rhs=p_[:], start=(kb == 0), stop=(kb == nkb - 1)), [onesb, p_], [pl])
                        s_step.n = npb
                        pend = [s_step(0)]
                        for kb in range(nkb):
                            if kb + 1 < nkb:
                                pend.append(s_step(kb + 1))
                            pv_step(kb, pend.pop(0))
                        npb = s_step.n
                        k.op("dve", lambda e: e.reciprocal(out=rl[:], in_=pl[:]), [pl], [rl])
                        o_ = ot[nq % 2]
                        k.op("dve", lambda e: e.tensor_tensor(out=o_[:], in0=po[:], in1=rl[:], op=ALU.mult), [po, rl], [o_])
                        k.dma("pool", brT[2][h * 128:(h + 1) * 128, t0:t0 + TT], o_[:], [o_], [], f"ot{nq % 2}")
                        nq += 1
                seq0 += L

    def merge_phase(l, src, dst):
        with k.phase():
            pset(range(8))
            wg = k.sb([128, 8, 3 * D], BF16, "wg")
            for kk in range(8):
                load_w_wide(wg, wg[:, kk, :], W["w_gate"][l, kk * 128:(kk + 1) * 128, :])
            wbr = k.sb([128, 3, 4, D], BF16, "wbr")
            for bi, wn in enumerate(("w_branch_a", "w_branch_b", "w_branch_c")):
                for kk in range(4):
                    load_w(wbr, wbr[:, bi, kk, :], W[wn][l, kk * 128:(kk + 1) * 128, :])
            wo = k.sb([128, 8, D], BF16, "wo")
            for kk in range(8):
                load_w(wo, wo[:, kk, :], W["w_out"][l, kk * 128:(kk + 1) * 128, :])
            gain = k.sb([128, 8], F32, "gain")
            load_gain(gain, W["mix_norm"][l], 8)
            bg = k.sb([128, 24], F32, "bg")
            k.dma("sp", bg[:], W["b_gate"][l].rearrange("(kk p) -> p kk", p=128), [], [bg], "gain", allow_slow_non_contiguous=True)
            hbs = k.ring(2, [128, 4, D], F32, "hb")
            xnr = k.ring(4, [128, D], BF16, "xn")
            uTs = k.ring(2, [128, 8, TT], BF16, "uT")
            ss = k.sb([128, 8], F32, "ss")
            rs = k.sb([128, 8], F32, "rs")
            brs = k.ring(2, [128, 3, 4, TT], BF16, "brs")
            gts = k.ring(3, [128, TT], F32, "gts")
            macc = k.sb([128, TT], F32, "macc")
            mt2 = k.sb([128, TT], F32, "mt2")
            mT = k.sb([128, 8, TT], BF16, "mT")
            ctr = [0]
            ng = 0
            ntl = NT // TT

            def load_tile(tix):
                load_h(src, tix * TT, hbs[tix % 2], f"hld{tix % 2}")
                for bi in range(3):
                    k.dma("sp", brs[tix % 2][:, bi, :, :], brT[bi][:, tix * TT:(tix + 1) * TT].rearrange("(c p) t -> p c t", p=128), [], [brs[tix % 2]], f"brs{tix % 2}")
            load_tile(0)
            norm_T(hbs[0], gain, uTs[0], xnr, ss, rs, ctr)
            for ti in range(ntl):
                t0 = ti * TT
                hb = hbs[ti % 2]
                br = brs[ti % 2]
                uT = uTs[ti % 2]
                nxt = None
                for c in range(8):
                    if ti + 1 < ntl:
                        if c == 1:
                            load_tile(ti + 1)
                        if c == 3:
                            nxt = norm_A(hbs[(ti + 1) % 2], xnr, ss, rs, ctr)
                        if c == 6:
                            norm_B(nxt, gain, uTs[(ti + 1) % 2])
                    for bi in range(3):
                        pgt = pget()
                        for kk in range(8):
                            k.op("pe", lambda e: e.matmul(pgt[:], lhsT=wg[:, kk, bi * D + c * 128: bi * D + (c + 1) * 128], rhs=uT[:, kk, :], start=(kk == 0), stop=(kk == 7)),
                                 [wg, uT], [pgt])
                        py = pget()
                        for kk in range(4):
                            k.op("pe", lambda e: e.matmul(py[:], lhsT=wbr[:, bi, kk, c * 128:(c + 1) * 128], rhs=br[:, bi, kk, :], start=(kk == 0), stop=(kk == 3)),
                                 [wbr, br], [py])
                        gt_ = gts[ng % 3]
                        ng += 1
                        k.op("act", lambda e: e.activation(out=gt_[:], in_=pgt[:], func=AF.Sigmoid, bias=bg[:, bi * 8 + c: bi * 8 + c + 1]), [pgt, bg], [gt_])
                        if bi == 0:
                            k.op("dve", lambda e: e.tensor_tensor(out=macc[:], in0=gt_[:], in1=py[:], op=ALU.mult), [gt_, py], [macc])
                        elif bi == 1:
                            k.op("dve", lambda e: e.tensor_tensor(out=mt2[:], in0=gt_[:], in1=py[:], op=ALU.mult), [gt_, py], [mt2])
                            k.op("pool", lambda e: e.tensor_tensor(out=macc[:], in0=macc[:], in1=mt2[:], op=ALU.add), [macc, mt2], [macc])
                        else:
                            k.op("dve", lambda e: e.tensor_tensor(out=mt2[:], in0=gt_[:], in1=py[:], op=ALU.mult), [gt_, py], [mt2])
                            k.op("pool", lambda e: e.tensor_tensor(out=mT[:, c, :], in0=macc[:], in1=mt2[:], op=ALU.add), [macc, mt2], [mT])
                for s in range(4):
                    for fh in range(2):
                        po = pget()
                        for c in range(8):
                            k.op("pe", lambda e: e.matmul(po[:], lhsT=mT[:, c, s * 128:(s + 1) * 128], rhs=wo[:, c, fh * 512:(fh + 1) * 512], start=(c == 0), stop=(c == 7)),
                                 [mT, wo], [po])
                        k.op("dve", lambda e: e.tensor_tensor(out=hb[:, s, fh * 512:(fh + 1) * 512], in0=po[:], in1=hb[:, s, fh * 512:(fh + 1) * 512], op=ALU.add),
                             [po, hb], [hb])
                k.dma("pool", dst[t0:t0 + TT, :].rearrange("(s p) d -> p s d", p=128), hb[:], [hb], [], f"hst{ti % 2}")

    import os
    stopat = int(os.environ.get("STOPAT", "999"))
    phases = []
    cur = xin
    for l in range(DEPTH):
        last = (l == DEPTH - 1)
        phases.append(lambda l=l, cur=cur: ffn_phase(l, 0, cur, hA, False))
        phases.append(lambda l=l: m1_phase(l, hA))
        phases.append(lambda l=l: scan_phase(l))
        phases.append(lambda l=l: mla_phase(l))
        phases.append(lambda l=l: merge_phase(l, hA, hB))
        phases.append(lambda l=l, last=last: ffn_phase(l, 1, hB, yout if last else hA, last))
        cur = hA
    for i, ph in enumerate(phases):
        if i >= stopat:
            break
        ph()
    if stopat < 999:
        scr = {"hA": hA, "hB": hB, "featT": featT, "zs_tm": zs_tm, "sm_tm": sm_tm, "of_tm": of_tm, "yf_tm": yf_tm,
               "brT0": brT[0], "brT1": brT[1], "brT2": brT[2], "krT_d": krT_d, "v_d": v_d, "qT_d": qT_d.rearrange("h p t -> (h p) t"), "qrT_d": qrT_d.rearrange("h p t -> (h p) t"), "kT_d": kT_d.rearrange("h p t -> (h p) t")}
        for n, ap in scr.items():
            pr = nc.dram_tensor("p_" + n, list(ap.shape), ap.dtype, kind="ExternalOutput").ap()
            for r0 in range(0, ap.shape[0], 128):
                r1 = min(r0 + 128, ap.shape[0])
                k.dma("sp", pr[r0:r1], ap[r0:r1], [], [], "probe")
    if stopat < 999:
        for n, b, shp, dt_ in (("onesb", onesb, [128, 128], BF16), ("identb", identb, [128, 128], BF16), ("onesf", onesf, [128, 128], F32),
                               ("cst", cst, [128, 4], F32), ("cm", cm, [128, 7 * 128], F32)):
            pr = nc.dram_tensor("p_" + n, shp, dt_, kind="ExternalOutput").ap()
            src = b[:] if n != "cm" else b[:].rearrange("p a b -> p (a b)")
            k.dma("sp", pr, src, [b], [], "probe")
    k.barrier()
    return nc, k


def _consts(LMAX):
    idx = np.arange(128)
    j = idx[:, None]
    i = idx[None, :]
    cm = np.zeros((128, 7, 128), np.float32)
    cm[:, 0] = (j == i)
    cm[:, 1] = (j <= i)
    cm[:, 2] = (j >= i)
    NEG = -1e30
    cm[:, 3] = np.where(i > j, 0.0, NEG)
    cm[:, 4] = np.where(i < j, 0.0, NEG)
    cm[:, 5] = np.where(i >= j, 0.0, NEG)
    cm[:, 6] = np.where(i <= j, 0.0, NEG)
    inv_freq = np.power(np.float32(10000.0), -np.arange(0, 64, 2, dtype=np.float32) / np.float32(64)).astype(np.float32)
    ang = np.arange(LMAX, dtype=np.float32)[:, None] * inv_freq[None, :]
    ang = np.concatenate([ang, ang], axis=-1).astype(np.float32)
    rc = np.stack([np.cos(ang).T, np.sin(ang).T]).astype(np.float32)
    return cm, np.ascontiguousarray(rc)


_CACHE = {}


def run(seq_lists, x_per_core, weights):
    seqs = tuple(seq_lists)
    if seqs not in _CACHE:
        _CACHE[seqs] = kernel_build(list(seqs))
    nc, _ = _CACHE[seqs]
    cm, rc = _consts(max(seqs))
    wm = {}
    for n, a in weights.items():
        a = np.asarray(a, np.float32)
        if n in ("gdn_A_log", "gdn_dt_bias"):
            a = a.reshape(DEPTH, 8)
        elif n in ("ssd_A_log", "ssd_dt_bias"):
            a = a.reshape(DEPTH, 16)
        elif n == "final_norm":
            a = a.reshape(1, D)
        wm[n] = np.ascontiguousarray(a)
    in_maps = []
    for x in x_per_core:
        m = dict(wm)
        m["xin"] = np.ascontiguousarray(x, dtype=np.float32)
        m["cmask"] = cm
        m["ropecs"] = rc
        in_maps.append(m)
    res = run_bass_kernel_spmd(nc, in_maps, core_ids=list(range(len(x_per_core))))
    return [r["yout"] for r in res.results]


def kernel(x_prompt, x_sample, **weights):
    x_prompt = np.asarray(x_prompt, np.float32)
    x_sample = np.asarray(x_sample, np.float32)
    B, S, _ = x_prompt.shape
    DB, DS, _ = x_sample.shape
    n = N_CORES
    pp = B // n
    sp = DB // n
    seqs = [S] * pp + [DS] * sp
    xs = []
    for c in range(n):
        parts = [x_prompt[c * pp + i] for i in range(pp)] + [x_sample[c * sp + i] for i in range(sp)]
        xs.append(np.concatenate(parts, axis=0))
    outs = run(seqs, xs, weights)
    y_prompt = np.empty_like(x_prompt)
    y_sample = np.empty_like(x_sample)
    for c in range(n):
        o = outs[c]
        off = 0
        for i in range(pp):
            y_prompt[c * pp + i] = o[off:off + S]
            off += S
        for i in range(sp):
            y_sample[c * sp + i] = o[off:off + DS]
            off += DS
    return (y_prompt, y_sample)
```
